# Optimizing a Trainium2 kernel written in Bass

```python
import jax, jax.numpy as jnp
from jax import lax
import numpy as np

D_MODEL = 1024
BATCH = 2
SEQ = 8192
DEPTH = 1

N_META = 16
CONF_DIM = D_MODEL
CONF_KERNEL = 31
SSM_EXPAND = 2
D_INNER = SSM_EXPAND * D_MODEL
SSM_HEADDIM = 64
SSM_HEADS = D_INNER // SSM_HEADDIM
SSM_GROUPS = 4
HEADS_PER_GROUP = SSM_HEADS // SSM_GROUPS
D_STATE = 128
SSM_CONV = 7
CHUNK = 128
META_PAD = CHUNK - N_META
XBC_DIM = D_INNER + 2 * SSM_GROUPS * D_STATE
D_FF = -(-8 * D_MODEL // (3 * 256)) * 256
IN_SPLITS = (CONF_DIM, CONF_DIM, D_INNER, D_INNER, SSM_GROUPS * D_STATE, SSM_GROUPS * D_STATE,
             SSM_HEADS, SSM_HEADS, D_MODEL, D_MODEL)
IN_DIM = sum(IN_SPLITS)
EPS = 1e-6

kernel_name = "gated_conformer_bissd_hybrid_block"


def rmsnorm(x, g):
    xf = x.astype(jnp.float32)
    y = xf * lax.rsqrt(jnp.mean(xf * xf, axis=-1, keepdims=True) + EPS)
    return (y * g.astype(jnp.float32)).astype(x.dtype)


def layernorm(x, g, b):
    xf = x.astype(jnp.float32)
    mu = jnp.mean(xf, axis=-1, keepdims=True)
    var = jnp.mean(jnp.square(xf - mu), axis=-1, keepdims=True)
    y = (xf - mu) * lax.rsqrt(var + EPS)
    return (y * g.astype(jnp.float32) + b.astype(jnp.float32)).astype(x.dtype)


def gated_rmsnorm(y, z, w):
    shp = y.shape
    v = y.astype(jnp.float32) * jax.nn.silu(z.astype(jnp.float32))
    v = v.reshape(shp[:-1] + (SSM_GROUPS, shp[-1] // SSM_GROUPS))
    v = v * lax.rsqrt(jnp.mean(v * v, axis=-1, keepdims=True) + EPS)
    return (v.reshape(shp) * w.astype(jnp.float32)).astype(y.dtype)


def dwconv(x, w, b):
    k, c = w.shape
    half = (k - 1) // 2
    y = lax.conv_general_dilated(x, w[:, None, :].astype(x.dtype), window_strides=(1,),
                                 padding=[(half, half)], dimension_numbers=('NWC', 'WIO', 'NWC'),
                                 feature_group_count=c)
    return y + b


def segsum(v):
    t = v.shape[-1]
    vv = jnp.broadcast_to(v[..., :, None], v.shape + (t,))
    vv = jnp.where(jnp.tril(jnp.ones((t, t), dtype=bool), -1), vv, 0.0)
    s = jnp.cumsum(vv, axis=-2)
    return jnp.where(jnp.tril(jnp.ones((t, t), dtype=bool), 0), s, -jnp.inf)


def ssd_chunked(xh, dt, a, bm, cm):
    b, t, g, e, p = xh.shape
    n = bm.shape[-1]
    nc = t // CHUNK
    xdt = (xh * dt[..., None]).reshape(b, nc, CHUNK, g, e, p)
    adt = jnp.moveaxis((dt * a).reshape(b, nc, CHUNK, g, e), 2, -1)
    bc = bm.reshape(b, nc, CHUNK, g, n)
    cc = cm.reshape(b, nc, CHUNK, g, n)
    a_cum = jnp.cumsum(adt, axis=-1)
    lmat = jnp.exp(segsum(adt))
    cb = jnp.einsum('bclgn,bcsgn->bcgls', cc, bc)
    y_diag = jnp.einsum('bcgels,bcsgep->bclgep', cb[:, :, :, None] * lmat, xdt)
    decay_states = jnp.exp(a_cum[..., -1:] - a_cum)
    states = jnp.einsum('bclgn,bcgel,bclgep->bcgepn', bc, decay_states, xdt)
    tot = jnp.pad(a_cum[..., -1], ((0, 0), (1, 0), (0, 0), (0, 0)))
    decay_chunk = jnp.exp(segsum(jnp.moveaxis(tot, 1, -1)))
    states = jnp.concatenate([jnp.zeros_like(states[:, :1]), states], axis=1)
    states_in = jnp.einsum('bgezc,bcgepn->bzgepn', decay_chunk, states)[:, :-1]
    y_off = jnp.einsum('bclgn,bcgepn,bcgel->bclgep', cc, states_in, jnp.exp(a_cum))
    return (y_diag + y_off).reshape(b, t, g, e, p)


def pad_front(v):
    return jnp.pad(v, ((0, 0), (META_PAD, 0)) + ((0, 0),) * (v.ndim - 2))


def mixer_block(h, w_in, conv_dw_w, conv_dw_b, conv_ln_g, conv_ln_b, conv_out_w,
                ssm_conv_w, ssm_conv_b, dt_bias_f, dt_bias_b, a_log_f, a_log_b,
                ssm_d, ssm_norm_w, ssm_out_w, w_o):
    bsz, length, _ = h.shape
    offs = [int(o) for o in np.cumsum(IN_SPLITS)[:-1]]
    u = h @ w_in
    g_val, g_gate, z, xs, bs, cs, dtf, dtb, gate_conv, gate_ssm = jnp.split(u, offs, axis=-1)

    a = g_val * jax.nn.sigmoid(g_gate)
    a = dwconv(a, conv_dw_w, conv_dw_b)
    a = jax.nn.silu(layernorm(a, conv_ln_g, conv_ln_b))
    y_conv = a @ conv_out_w

    xbc = jax.nn.silu(dwconv(jnp.concatenate([xs, bs, cs], axis=-1), ssm_conv_w, ssm_conv_b))
    xs, bs, cs = jnp.split(xbc, [D_INNER, D_INNER + SSM_GROUPS * D_STATE], axis=-1)
    xh = xs.astype(jnp.float32).reshape(bsz, length, SSM_GROUPS, HEADS_PER_GROUP, SSM_HEADDIM)
    bm = bs.astype(jnp.float32).reshape(bsz, length, SSM_GROUPS, D_STATE)
    cm = cs.astype(jnp.float32).reshape(bsz, length, SSM_GROUPS, D_STATE)
    shp_dt = (bsz, length, SSM_GROUPS, HEADS_PER_GROUP)
    dt_f = jax.nn.softplus((dtf + dt_bias_f).astype(jnp.float32)).reshape(shp_dt)
    dt_b = jax.nn.softplus((dtb + dt_bias_b).astype(jnp.float32)).reshape(shp_dt)
    a_f = -jnp.exp(a_log_f.astype(jnp.float32)).reshape(SSM_GROUPS, HEADS_PER_GROUP)
    a_b = -jnp.exp(a_log_b.astype(jnp.float32)).reshape(SSM_GROUPS, HEADS_PER_GROUP)
    xp, bp, cp = pad_front(xh), pad_front(bm), pad_front(cm)
    dfp, dbp = pad_front(dt_f), pad_front(dt_b)
    rev = lambda v: jnp.flip(v, axis=1)
    y_fwd = ssd_chunked(xp, dfp, a_f, bp, cp)
    y_bwd = rev(ssd_chunked(rev(xp), rev(dbp), a_b, rev(bp), rev(cp)))
    d_skip = ssm_d.astype(jnp.float32).reshape(SSM_GROUPS, HEADS_PER_GROUP)[..., None]
    y = (y_fwd + y_bwd)[:, META_PAD:] + d_skip * xh
    y = y.reshape(bsz, length, D_INNER).astype(h.dtype)
    y_ssm = gated_rmsnorm(y, z, ssm_norm_w) @ ssm_out_w

    merged = jax.nn.sigmoid(gate_conv) * y_conv + jax.nn.sigmoid(gate_ssm) * y_ssm
    return merged @ w_o


def swiglu(h, w_gate, w_up, w_down):
    return (jax.nn.silu(h @ w_gate) * (h @ w_up)) @ w_down


def setup_inputs(seed: int = 0) -> dict:
    key = jax.random.key(seed)
    ks = jax.random.split(key, 24)
    f32 = jnp.float32
    nrm = lambda k, shp, s: jax.random.normal(k, shp, f32) * s
    dt0 = jnp.exp(jax.random.uniform(ks[12], (DEPTH, SSM_HEADS), f32) * (np.log(0.1) - np.log(0.001)) + np.log(0.001))
    dt1 = jnp.exp(jax.random.uniform(ks[13], (DEPTH, SSM_HEADS), f32) * (np.log(0.1) - np.log(0.001)) + np.log(0.001))
    inv_softplus = lambda d: d + jnp.log(-jnp.expm1(-d))
    return {
        "x": nrm(ks[0], (BATCH, SEQ, D_MODEL), 1.0),
        "meta_tokens": nrm(ks[1], (N_META, D_MODEL), 1.0),
        "norm_mix": 1.0 + nrm(ks[2], (DEPTH, D_MODEL), 0.02),
        "w_in": nrm(ks[3], (DEPTH, D_MODEL, IN_DIM), D_MODEL ** -0.5),
        "conv_dw_w": nrm(ks[4], (DEPTH, CONF_KERNEL, CONF_DIM), CONF_KERNEL ** -0.5),
        "conv_dw_b": nrm(ks[5], (DEPTH, CONF_DIM), 0.02),
        "conv_ln_g": 1.0 + nrm(ks[6], (DEPTH, CONF_DIM), 0.02),
        "conv_ln_b": nrm(ks[7], (DEPTH, CONF_DIM), 0.02),
        "conv_out_w": nrm(ks[8], (DEPTH, CONF_DIM, D_MODEL), CONF_DIM ** -0.5),
        "ssm_conv_w": nrm(ks[9], (DEPTH, SSM_CONV, XBC_DIM), SSM_CONV ** -0.5),
        "ssm_conv_b": nrm(ks[10], (DEPTH, XBC_DIM), 0.02),
        "dt_bias_f": inv_softplus(dt0),
        "dt_bias_b": inv_softplus(dt1),
        "a_log_f": jnp.log(jax.random.uniform(ks[14], (DEPTH, SSM_HEADS), f32, 1.0, 16.0)),
        "a_log_b": jnp.log(jax.random.uniform(ks[15], (DEPTH, SSM_HEADS), f32, 1.0, 16.0)),
        "ssm_d": 1.0 + nrm(ks[16], (DEPTH, SSM_HEADS), 0.02),
        "ssm_norm_w": 1.0 + nrm(ks[17], (DEPTH, D_INNER), 0.02),
        "ssm_out_w": nrm(ks[18], (DEPTH, D_INNER, D_MODEL), D_INNER ** -0.5),
        "w_o": nrm(ks[19], (DEPTH, D_MODEL, D_MODEL), D_MODEL ** -0.5),
        "norm_ffn": 1.0 + nrm(ks[20], (DEPTH, D_MODEL), 0.02),
        "w_gate": nrm(ks[21], (DEPTH, D_MODEL, D_FF), D_MODEL ** -0.5),
        "w_up": nrm(ks[22], (DEPTH, D_MODEL, D_FF), D_MODEL ** -0.5),
        "w_down": nrm(ks[23], (DEPTH, D_FF, D_MODEL), D_FF ** -0.5),
        "norm_final": 1.0 + nrm(ks[11], (D_MODEL,), 0.02),
    }


def reference(x, meta_tokens, norm_mix, w_in, conv_dw_w, conv_dw_b, conv_ln_g, conv_ln_b,
              conv_out_w, ssm_conv_w, ssm_conv_b, dt_bias_f, dt_bias_b, a_log_f, a_log_b,
              ssm_d, ssm_norm_w, ssm_out_w, w_o, norm_ffn, w_gate, w_up, w_down, norm_final):
    bsz = x.shape[0]
    meta = jnp.broadcast_to(meta_tokens[None].astype(x.dtype), (bsz, N_META, x.shape[-1]))
    hs = jnp.concatenate([meta, x], axis=1)
    for i in range(DEPTH):
        hn = rmsnorm(hs, norm_mix[i])
        hs = hs + mixer_block(hn, w_in[i], conv_dw_w[i], conv_dw_b[i], conv_ln_g[i], conv_ln_b[i],
                              conv_out_w[i], ssm_conv_w[i], ssm_conv_b[i], dt_bias_f[i], dt_bias_b[i],
                              a_log_f[i], a_log_b[i], ssm_d[i], ssm_norm_w[i], ssm_out_w[i], w_o[i])
        hs = hs + swiglu(rmsnorm(hs, norm_ffn[i]), w_gate[i], w_up[i], w_down[i])
    return rmsnorm(hs, norm_final)[:, N_META:]
```

```python
import numpy as np
from contextlib import ExitStack
import concourse.bass as bass
import concourse.mybir as mybir
from concourse.bass_utils import run_bass_kernel_spmd

F32 = mybir.dt.float32
BF16 = mybir.dt.bfloat16
AF = mybir.ActivationFunctionType
ALU = mybir.AluOpType

D = 1024
KT = 8
L = 8192
NB = 2
SEG = 2048
NCH = 16
SLAB = 2304
NOT = 49
DFF = 2816
FT = 22
EPS = 1e-6
O_VAL, O_GATE, O_Z, O_X, O_B, O_C, O_DT, O_GC, O_GS = 0, 1024, 2048, 4096, 6144, 6656, 7168, 7232, 8256

ENGS = ["pe", "act", "dve", "pool", "sp"]


class Buf:
    __slots__ = ("name", "w", "r", "excl")

    def __init__(self, name, excl=False):
        self.name = name
        self.excl = excl
        self.w = None
        self.r = []


class Sched:
    def __init__(self, nc, es):
        self.nc = nc
        self.es = es
        self.sems = {}
        for e in ENGS:
            self.sems[e] = es.enter_context(nc.semaphore(f"sem_{e}"))
        self.cnt = {e: 0 for e in ENGS}
        self.seen = {e: {} for e in ENGS}
        self.dcnt = {}

    def dsem(self, name):
        key = f"d_{name}"
        if key not in self.sems:
            self.sems[key] = self.es.enter_context(self.nc.semaphore(f"dsem_{name}"))
            self.dcnt[key] = 0
        return key

    def _deps(self, eng, reads, writes):
        deps = {}

        def add(ev):
            if ev is None:
                return
            k, v = ev
            if deps.get(k, 0) < v:
                deps[k] = v
        for b in reads:
            add(b.w)
            if b.excl:
                for ev in b.r:
                    add(ev)
        for b in writes:
            add(b.w)
            for ev in b.r:
                add(ev)
        waits = []
        seen = self.seen[eng]
        for k, v in deps.items():
            if seen.get(k, 0) < v:
                seen[k] = v
                waits.append((k, v))
        return waits

    def _mark(self, ev, reads, writes):
        for b in reads:
            if b.excl:
                b.w = ev
                b.r = []
            else:
                b.r = [x for x in b.r if x[0] != ev[0]] + [ev]
        for b in writes:
            b.w = ev
            b.r = []

    def _emit(self, name, waits, fn, dkey, n):
        nc = self.nc
        e = {"pe": nc.tensor, "act": nc.scalar, "dve": nc.vector, "pool": nc.gpsimd, "sp": nc.sync}[name]
        sems = self.sems
        for k, v in waits:
            e.wait_ge(sems[k], v)
        if fn is None:
            return
        r = fn(e)
        if dkey is None:
            r.then_inc(sems[name], 1)
        else:
            if not isinstance(r, (list, tuple)):
                r = [r]
            assert len(r) == n, (len(r), n)
            for ins in r:
                ins.then_inc(sems[dkey], 16)

    def op(self, eng, fn, reads=(), writes=()):
        waits = self._deps(eng, reads, writes)
        self.cnt[eng] += 1
        ev = (eng, self.cnt[eng])
        self._emit(eng, waits, fn, None, 0)
        self._mark(ev, reads, writes)
        return ev

    def dma(self, q, name, fn, reads=(), writes=(), n=1):
        dkey = self.dsem(name)
        waits = self._deps(q, reads, writes)
        self.dcnt[dkey] += 16 * n
        ev = (dkey, self.dcnt[dkey])
        self._emit(q, waits, fn, dkey, n)
        self._mark(ev, reads, writes)
        return ev

    def barrier(self):
        for e in ENGS:
            waits = []
            seen = self.seen[e]
            for k in ENGS:
                if self.cnt[k] > seen.get(k, 0):
                    seen[k] = self.cnt[k]
                    waits.append((k, self.cnt[k]))
            for k, v in self.dcnt.items():
                if v > seen.get(k, 0):
                    seen[k] = v
                    waits.append((k, v))
            self._emit(e, waits, None, None, 0)


class T:
    def __init__(self, t, name, excl=False):
        self.t = t
        self.b = Buf(name, excl)

    def __getitem__(self, k):
        return self.t[k]


def build_nc(upto=99, dbg=()):
    nc = bass.Bass("TRN2", target_bir_lowering=False)

    def din(name, shape):
        return nc.dram_tensor(name, list(shape), F32, kind="ExternalInput").ap()

    xs = din("xs", [SLAB, D])
    xo = din("xo", [(NOT + 1) * 128, D])
    w_in = din("w_in", [D, 9280])
    conv_out_w = din("conv_out_w", [D, D])
    ssm_out_w = din("ssm_out_w", [2048, D])
    w_o = din("w_o", [D, D])
    w_gate = din("w_gate", [D, DFF])
    w_up = din("w_up", [D, DFF])
    w_down = din("w_down", [DFF, D])
    cdw_d = din("cdw", [128, 8, 31])
    scw_d = din("scw", [128, 24, 7])
    vec8_d = din("vec8", [128, 5, 8])
    scb_d = din("scb", [128, 24])
    rows_d = din("rows", [1, 128])
    dsk_d = din("dsk", [1, 32])
    normw_d = din("normw", [1, 2048])
    nfin_d = din("nfin", [1, D])
    hmask_d = din("hmask", [128, 1])
    mrow_d = din("mrow", [1, 2 * NOT])
    cident_d = din("cident", [128, 128])
    cuinc_d = din("cuinc", [128, 128])
    cnustr_d = din("cnustr", [128, 128])
    cnegf_d = din("cnegf", [128, 512])
    cnegb_d = din("cnegb", [128, 512])
    out_d = nc.dram_tensor("out", [SEG, D], F32, kind="ExternalOutput").ap()

    hns = nc.dram_tensor("hns", [NOT + 1, 128, D], BF16).ap()
    yscr = nc.dram_tensor("yscr", [2, 4, NCH, 128, 512], F32).ap()

    dbg_out = {}

    with ExitStack() as es:
        S = Sched(nc, es)

        uniq = {"n": 0}

        def sb(name, shape, dt, st=es):
            uniq["n"] += 1
            return T(st.enter_context(nc.sbuf_tensor("s%d_%s" % (uniq["n"], name), list(shape), dt)), name)

        banks = [T(es.enter_context(nc.psum_tensor(f"pb{i}", [128, 512], F32)), f"pb{i}", excl=True) for i in range(8)]
        bank_rr = {"i": 0, "n": 8}

        def bank():
            b = banks[bank_rr["i"] % bank_rr["n"]]
            bank_rr["i"] += 1
            return b

        def bfv(b):
            return b.t[:].bitcast(BF16)

        def dump(name, src, shape, dt, reads):
            if name not in dbg:
                return
            dd = nc.dram_tensor("dbg_" + name, list(shape), dt, kind="ExternalOutput").ap()
            dbg_out[name] = dd
            S.dma("sp", "dbg_" + name, lambda e: [e.dma_start(out=dd, in_=src)], reads=reads, writes=[Buf("dbgo")])

        ident = sb("ident", [128, 128], BF16)
        ones = sb("ones", [128, 128], BF16)
        onesm = sb("onesm", [128, 128], BF16)
        epsc = sb("epsc", [128, 1], F32)
        onec = sb("onec", [128, 1], F32)
        negh = sb("negh", [128, 1], F32)
        vec8 = sb("vec8", [128, 5, 8], F32)
        hnT = sb("hnT", [128, KT, SLAB], BF16)
        pA = ExitStack()
        uinc = sb("uinc", [128, 128], BF16, pA)
        nustr = sb("nustr", [128, 128], BF16, pA)
        negf = sb("negf", [128, 512], BF16, pA)
        negb = sb("negb", [128, 512], BF16, pA)
        scb = sb("scb", [128, 24], F32, pA)
        scw = sb("scw", [128, 24, 7], F32, pA)
        rows = sb("rows", [128, 128], F32, pA)
        abc = sb("abc", [128, 64], F32, pA)
        dsk = sb("dsk", [128, 32], F32, pA)
        hmask = sb("hmask", [128, 1], F32, pA)
        mrow = sb("mrow", [128, 2 * NOT], F32, pA)
        wdt = sb("wdt", [128, KT, 64], BF16, pA)
        sin = sb("sin", [128, 2, 2048], F32, pA)
        dtraw_o = sb("dtraw_o", [128, NCH, 64], F32, pA)
        ps1 = ExitStack()
        dtraw_s = sb("dtraw_s", [128, NOT, 64], F32, ps1)
        p0 = ExitStack()
        xt2 = [sb(f"xt{i}", [128, D], F32, p0) for i in range(8)]
        xn2 = [sb(f"xn{i}", [128, D], BF16, p0) for i in range(8)]
        junk = sb("junk", [128, D], BF16, p0)
        st2 = [sb(f"st{i}", [128, 4], F32, p0) for i in range(8)]
        hnt2 = [sb(f"hnt{i}", [128, KT, 128], BF16, p0) for i in range(4)]

        for (t_, d_) in [(ident, cident_d), (uinc, cuinc_d), (nustr, cnustr_d), (negf, cnegf_d), (negb, cnegb_d)]:
            S.dma("pool", "c_" + t_.b.name, (lambda t_, d_: lambda e: [e.dma_start(out=t_[:], in_=d_)])(t_, d_), writes=[t_.b])
        for (t_, d_) in [(vec8, vec8_d), (scb, scb_d), (scw, scw_d), (hmask, hmask_d)]:
            S.dma("sp", "c_" + t_.b.name, (lambda t_, d_: lambda e: [e.dma_start(out=t_[:], in_=d_)])(t_, d_), writes=[t_.b])
        for (t_, d_, n_) in [(rows, rows_d, 128), (dsk, dsk_d, 32), (mrow, mrow_d, 2 * NOT)]:
            S.dma("sp", "c_" + t_.b.name, (lambda t_, d_: lambda e: [e.dma_start(out=t_[:], in_=d_.partition_broadcast(128))])(t_, d_), writes=[t_.b])
        S.op("dve", lambda e: e.memset(ones[:], 1.0), writes=[ones.b])
        S.op("dve", lambda e: e.memset(onesm[:], 1.0 / 1024.0), writes=[onesm.b])
        S.op("dve", lambda e: e.memset(epsc[:], EPS), writes=[epsc.b])
        S.op("dve", lambda e: e.memset(onec[:], 1.0), writes=[onec.b])
        S.op("act", lambda e: e.activation(out=abc[:], in_=rows[:, 64:128], func=AF.Exp), reads=[rows.b], writes=[abc.b])
        S.op("dve", lambda e: e.tensor_scalar(out=abc[:], in0=abc[:], scalar1=-1.0, scalar2=None, op0=ALU.mult), reads=[abc.b], writes=[abc.b])

        def wload(dst, dst_ap, W, c0, ncols, kt=KT, r0=0):
            src = W[r0:r0 + kt * 128, c0:c0 + ncols].rearrange("(k p) c -> p k c", p=128)
            S.dma("pool", "w_" + dst.b.name, lambda e: [e.dma_start(out=dst_ap, in_=src)], writes=[dst.b])

        wload(wdt, wdt[:], w_in, O_DT, 64)

        pc_jobs = []

        def pc_issue(n):
            for _ in range(n):
                if pc_jobs:
                    pc_jobs.pop(0)()

        def precast(name, W, c0, ncols, nrows):
            dst = nc.dram_tensor("bf_" + name, [nrows, ncols], BF16).ap()
            b = Buf("bf_" + name)
            for r0 in range(0, nrows, 128):
                for cc in range(0, ncols, 2048):
                    w = min(2048, ncols - cc)
                    pc_jobs.append((lambda r0, cc, w: lambda: S.dma("pool", "pc_" + name, lambda e: [e.dma_start(out=dst[r0:r0 + 128, cc:cc + w], in_=W[r0:r0 + 128, c0 + cc:c0 + cc + w])],
                                                                   writes=[b]))(r0, cc, w))
            return dst, b
        bf_co = precast("co", conv_out_w, 0, D, D)
        bf_gg = precast("gg", w_in, O_GC, 2048, D)
        bf_so = precast("so", ssm_out_w, 0, D, 2048)
        bf_wo = precast("wo", w_o, 0, D, D)
        bf_wg = precast("wg", w_gate, 0, DFF, D)
        bf_wu = precast("wu", w_up, 0, DFF, D)
        bf_wd = precast("wd", w_down, 0, D, DFF)
        bf_z = precast("z", w_in, O_Z, 2048, D)

        ntc = {"i": 0}

        def rstd_of(st, ssq_ap, dim):
            S.op("act", lambda e: e.activation(out=st[:, 1:2], in_=ssq_ap, func=AF.Sqrt, bias=epsc[:, 0:1], scale=1.0 / dim),
                 reads=[st.b, epsc.b], writes=[st.b])
            S.op("dve", lambda e: e.reciprocal(out=st[:, 2:3], in_=st[:, 1:2]), reads=[st.b], writes=[st.b])

        S.op("dve", lambda e: e.memset(negh[:], -0.5), writes=[negh.b])

        def rstd_pool(st, ssq_ap, dim):
            S.op("dve", lambda e: e.tensor_scalar(out=st[:, 1:2], in0=ssq_ap, scalar1=1.0 / dim, scalar2=EPS, op0=ALU.mult, op1=ALU.add), reads=[st.b], writes=[st.b])
            S.op("pool", lambda e: e.tensor_tensor(out=st[:, 2:3], in0=st[:, 1:2], in1=negh[:, 0:1], op=ALU.pow), reads=[st.b, negh.b], writes=[st.b])

        def norm_tile(src_rows, dst, dst_ap, gidx):
            i = ntc["i"] % 8
            ntc["i"] += 1
            xt, xn, st = xt2[i], xn2[i], st2[i]
            S.dma("sp", "xt%d" % i, lambda e: [e.dma_start(out=xt[:], in_=src_rows)], writes=[xt.b])
            S.op("act", lambda e: e.activation(out=junk[:], in_=xt[:], func=AF.Square, accum_out=st[:, 0:1]),
                 reads=[xt.b], writes=[junk.b, st.b])
            rstd_of(st, st[:, 0:1], D)
            S.op("dve", lambda e: e.tensor_scalar(out=xn[:], in0=xt[:], scalar1=st[:, 2:3], scalar2=None, op0=ALU.mult), reads=[xt.b, st.b], writes=[xn.b])
            bk = bank()

            def tr(e):
                v = bfv(bk)
                for k in range(KT):
                    r = e.transpose(out=v[:, k * 128:(k + 1) * 128], in_=xn[:, k * 128:(k + 1) * 128], identity=ident[:])
                return r
            S.op("pe", tr, reads=[xn.b, ident.b], writes=[bk.b])
            S.op("dve", lambda e: e.tensor_tensor(out=dst_ap, in0=bfv(bk).rearrange("p (k t) -> p k t", k=KT),
                                                  in1=vec8[:, gidx, :].unsqueeze(2).to_broadcast([128, KT, 128]), op=ALU.mult),
                 reads=[bk.b, vec8.b], writes=[dst.b])

        for i in range(SLAB // 128):
            norm_tile(xs[i * 128:(i + 1) * 128, :], hnT, hnT[:, :, i * 128:(i + 1) * 128], 3)

        def dt_mm(lhs_fn, lhs_reads, dst, dst_ap):
            bk = bank()

            def f(e):
                for k in range(KT):
                    r = e.matmul(bk[:, 0:64], lhsT=lhs_fn(k), rhs=wdt[:, k, :], start=(k == 0), stop=(k == KT - 1))
                return r
            S.op("pe", f, reads=lhs_reads + [wdt.b], writes=[bk.b])
            S.op("dve", lambda e: e.tensor_tensor(out=dst_ap, in0=bk[:, 0:64], in1=rows[:, 0:64], op=ALU.add),
                 reads=[bk.b, rows.b], writes=[dst.b])

        for c in range(NCH):
            dt_mm((lambda c: lambda k: hnT[:, k, (c + 1) * 128:(c + 2) * 128])(c), [hnT.b], dtraw_o, dtraw_o[:, c, :])
        dump("hnT", hnT[:], [128, KT, SLAB], BF16, [hnT.b])
        dump("dtraw_o", dtraw_o[:], [128, NCH, 64], F32, [dtraw_o.b])

        for i in range(NOT + 1):
            h = hnt2[i % 4]
            norm_tile(xo[i * 128:(i + 1) * 128, :], h, h[:], 3)
            if i < NOT:
                dt_mm((lambda h: lambda k: h[:, k, :])(h), [h.b], dtraw_s, dtraw_s[:, i, :])
            S.dma("pool", "hns%d" % (i % 4), (lambda h, i: lambda e: [e.dma_start(out=hns[i], in_=h[:].rearrange("p k t -> p (k t)"))])(h, i),
                  reads=[h.b], writes=[Buf("hns_dram%d" % i)])
        hns_b = Buf("hns_all")
        S.barrier()
        p0.close()
        dump("dtraw_s", dtraw_s[:], [128, NOT, 64], F32, [dtraw_s.b])
        if upto <= 0:
            return finish(nc, S, dbg_out)

        ps1a = ExitStack()
        tB = sb("tB", [128, NOT, 64], F32, ps1a)
        tC = sb("tC", [128, NOT, 64], F32, ps1a)
        tW = sb("tW", [128, NOT, 64], F32, ps1a)
        adts = sb("adts", [128, NOT, 64], BF16, ps1a)
        tA = dtraw_s

        def softplus_inplace(t, n):
            S.op("act", lambda e: e.activation(out=t[:], in_=t[:], func=AF.Exp), reads=[t.b], writes=[t.b])
            S.op("act", lambda e: e.activation(out=t[:], in_=t[:], func=AF.Ln, bias=onec[:, 0:1], scale=1.0), reads=[t.b, onec.b], writes=[t.b])

        def cums(adt, n, tinc, ttot):
            flat = adt[:].rearrange("p c j -> p (c j)")
            fi = tinc[:].rearrange("p c j -> p (c j)")
            ft = ttot[:].rearrange("p c j -> p (c j)")
            tot = n * 64
            for c0 in range(0, tot, 512):
                w = min(512, tot - c0)
                b1 = bank()
                S.op("pe", (lambda b1, c0, w: lambda e: e.matmul(b1[:, 0:w], lhsT=uinc[:], rhs=flat[:, c0:c0 + w], start=True, stop=True))(b1, c0, w),
                     reads=[adt.b, uinc.b], writes=[b1.b])
                S.op("act", (lambda b1, c0, w: lambda e: e.activation(out=fi[:, c0:c0 + w], in_=b1[:, 0:w], func=AF.Copy))(b1, c0, w),
                     reads=[b1.b], writes=[tinc.b])
                b2 = bank()
                S.op("pe", (lambda b2, c0, w: lambda e: e.matmul(b2[:, 0:w], lhsT=ones[:], rhs=flat[:, c0:c0 + w], start=True, stop=True))(b2, c0, w),
                     reads=[adt.b, ones.b], writes=[b2.b])
                S.op("dve", (lambda b2, c0, w: lambda e: e.tensor_copy(out=ft[:, c0:c0 + w], in_=b2[:, 0:w]))(b2, c0, w),
                     reads=[b2.b], writes=[ttot.b])

        softplus_inplace(tA, NOT)
        S.op("dve", lambda e: e.tensor_scalar(out=tA[:, 0, :], in0=tA[:, 0, :], scalar1=hmask[:, 0:1], scalar2=None, op0=ALU.mult),
             reads=[tA.b, hmask.b], writes=[tA.b])
        for d in range(2):
            S.op("dve", (lambda d: lambda e: e.tensor_tensor(out=tA[:, :, d * 32:(d + 1) * 32], in0=tA[:, :, d * 32:(d + 1) * 32],
                                                               in1=mrow[:, d * NOT:(d + 1) * NOT].unsqueeze(2).to_broadcast([128, NOT, 32]), op=ALU.mult))(d),
                 reads=[tA.b, mrow.b], writes=[tA.b])
        S.op("dve", lambda e: e.tensor_tensor(out=adts[:], in0=tA[:], in1=abc[:].unsqueeze(1).to_broadcast([128, NOT, 64]), op=ALU.mult),
             reads=[tA.b, abc.b], writes=[adts.b])
        cums(adts, NOT, tB, tC)
        S.op("dve", lambda e: e.tensor_tensor(out=tB[:, :, 0:32], in0=tC[:, :, 0:32], in1=tB[:, :, 0:32], op=ALU.subtract), reads=[tB.b, tC.b], writes=[tB.b])
        S.op("dve", lambda e: e.tensor_tensor(out=tB[:, :, 32:64], in0=tB[:, :, 32:64], in1=adts[:, :, 32:64], op=ALU.subtract), reads=[tB.b, adts.b], writes=[tB.b])
        S.op("dve", lambda e: e.memset(tW[:, NOT - 1, 0:32], 0.0), writes=[tW.b])
        S.op("dve", lambda e: e.memset(tW[:, 0, 32:64], 0.0), writes=[tW.b])
        for c in range(NOT - 2, -1, -1):
            S.op("dve", (lambda c: lambda e: e.tensor_tensor(out=tW[:, c, 0:32], in0=tW[:, c + 1, 0:32], in1=tC[:, c + 1, 0:32], op=ALU.add))(c),
                 reads=[tW.b, tC.b], writes=[tW.b])
        for c in range(1, NOT):
            S.op("pool", (lambda c: lambda e: e.tensor_tensor(out=tW[:, c, 32:64], in0=tW[:, c - 1, 32:64], in1=tC[:, c - 1, 32:64], op=ALU.add))(c),
                 reads=[tW.b, tC.b], writes=[tW.b])
        S.op("dve", lambda e: e.tensor_tensor(out=tB[:], in0=tB[:], in1=tW[:], op=ALU.add), reads=[tB.b, tW.b], writes=[tB.b])
        S.op("act", lambda e: e.activation(out=tB[:], in_=tB[:], func=AF.Exp), reads=[tB.b], writes=[tB.b])
        S.op("dve", lambda e: e.tensor_tensor(out=tA[:], in0=tA[:], in1=tB[:], op=ALU.mult), reads=[tA.b, tB.b], writes=[tA.b])
        mult_s = tA
        dump("mult_s", mult_s[:], [128, NOT, 64], F32, [mult_s.b])
        S.barrier()
        ps1a.close()

        bank_rr["n"] = 6
        accS = [banks[6], banks[7]]
        wsg_2 = [sb(f"wsg{i}", [128, KT, 640], BF16, ps1) for i in range(2)]
        dgs_2 = [sb(f"dgs{i}", [128, 5, 7, 128], BF16, ps1) for i in range(2)]
        xh = sb("xh", [128, 5, 128], BF16, ps1)
        hnb2 = [sb(f"hnb{i}", [128, 4, D], BF16, ps1) for i in range(2)]
        hnh = sb("hnh", [128, D], BF16, ps1)
        xpre_2 = [sb(f"xpre{i}", [128, 5, 520], BF16, ps1) for i in range(2)]
        xcb_2 = [sb(f"xcb{i}", [128, 5, 512], BF16, ps1) for i in range(2)]
        xpre_mb = [[Buf(f"xpm{i}_{m}") for m in range(5)] for i in range(2)]
        xpre_hb = [[Buf(f"xph{i}_{m}") for m in range(5)] for i in range(2)]
        xcb_mb = [[Buf(f"xcm{i}_{m}") for m in range(5)] for i in range(2)]
        xbt_2 = [[sb(f"xbt{j}_{i}", [128, 640], BF16, ps1) for i in range(4)] for j in range(2)]
        xdd_2 = [[[sb(f"xdd{j}_{i}_{d}", [128, 512], BF16, ps1) for d in range(2)] for i in range(4)] for j in range(2)]
        S.dma("sp", "hnh", lambda e: [e.dma_start(out=hnh[:], in_=hns[NOT])], reads=[hns_b], writes=[hnh.b])

        def build_diag(dg, dg_ap, tile, ntap, wt, eng):
            for tap in range(ntap):
                S.op(eng, (lambda tap: lambda e: e.tensor_scalar(out=dg_ap[:, tap, :], in0=ident[:], scalar1=wt[:, tile, tap:tap + 1], scalar2=None, op0=ALU.mult))(tap),
                     reads=[ident.b, wt.b], writes=[dg.b])

        nblk = 13
        def s2_prefetch(g):
            wsg_, dgs_ = wsg_2[g % 2], dgs_2[g % 2]
            wload(wsg_, wsg_[:, :, 0:512], w_in, O_X + g * 512, 512)
            wload(wsg_, wsg_[:, :, 512:640], w_in, O_B + g * 128, 128)
            tl_ = [g * 4 + m for m in range(4)] + [16 + g]
            for m in range(5):
                build_diag(dgs_, dgs_[:, m], tl_[m], 7, scw, "dve")

        s2_prefetch(0)
        for g in range(4):
            wsg, dgs = wsg_2[g % 2], dgs_2[g % 2]
            tiles = [g * 4 + m for m in range(4)] + [16 + g]
            for m in range(5):
                bk = bank()

                def f(e, bk=bk, m=m):
                    for k in range(KT):
                        r = e.matmul(bk[:, 0:128], lhsT=wsg[:, k, m * 128:(m + 1) * 128], rhs=hnh[:, k * 128:(k + 1) * 128], start=(k == 0), stop=(k == KT - 1))
                    return r
                S.op("pe", f, reads=[wsg.b, hnh.b], writes=[bk.b])
                S.op("act", (lambda bk, m: lambda e: e.activation(out=xh[:, m, :], in_=bk[:, 0:128], func=AF.Copy))(bk, m), reads=[bk.b], writes=[xh.b])
            def blkinfo(blk):
                NT = 1 if blk == 0 else 4
                t0 = 0 if blk == 0 else 1 + 4 * (blk - 1)
                return NT, t0, NT * 128

            def stageA(blk):
                NT, t0, N = blkinfo(blk)
                pc_issue(2)
                hb = hnb2[blk % 2]
                xpre, xcb = xpre_2[blk % 2], xcb_2[blk % 2]
                xpm, xph, xcm = xpre_mb[blk % 2], xpre_hb[blk % 2], xcb_mb[blk % 2]
                S.dma("sp", "hnb%d" % (blk % 2), lambda e: [e.dma_start(out=hb[:, 0:NT, :], in_=hns[t0:t0 + NT].rearrange("t p f -> p t f"))],
                      reads=[hns_b], writes=[hb.b])
                hbv = hb[:].rearrange("p t (k c) -> p t k c", k=KT)
                for m in range(5):
                    bk = bank()

                    def f(e, bk=bk, m=m):
                        for k in range(KT):
                            r = e.matmul(bk[:, 0:N], lhsT=wsg[:, k, m * 128:(m + 1) * 128], rhs=hbv[:, 0:NT, k, :], start=(k == 0), stop=(k == KT - 1))
                        return r
                    S.op("pe", f, reads=[wsg.b, hb.b], writes=[bk.b])
                    S.op("act", (lambda bk, m: lambda e: e.activation(out=xpre[:, m, 3:3 + N], in_=bk[:, 0:N], func=AF.Copy))(bk, m),
                         reads=[bk.b], writes=[xpm[m]])
                    S.op("pool", (lambda m: lambda e: e.tensor_copy(out=xpre[:, m, 0:3], in_=xh[:, m, blk * 6:blk * 6 + 3]))(m), reads=[xh.b], writes=[xph[m]])
                    S.op("pool", (lambda m: lambda e: e.tensor_copy(out=xpre[:, m, 3 + N:6 + N], in_=xh[:, m, blk * 6 + 3:blk * 6 + 6]))(m), reads=[xh.b], writes=[xph[m]])
                for m in range(5):
                    bk = bank()

                    def f(e, bk=bk, m=m):
                        for tap in range(7):
                            r = e.matmul(bk[:, 0:N], lhsT=dgs[:, m, tap, :], rhs=xpre[:, m, tap:tap + N], start=(tap == 0), stop=(tap == 6))
                        return r
                    S.op("pe", f, reads=[dgs.b, xpm[m], xph[m]], writes=[bk.b])
                    S.op("act", (lambda bk, m, tl: lambda e: e.activation(out=xcb[:, m, 0:N], in_=bk[:, 0:N], func=AF.Silu, bias=scb[:, tl:tl + 1], scale=1.0))(bk, m, tiles[m]),
                         reads=[bk.b, scb.b], writes=[xcm[m]])

            def stageB(blk):
                NT, t0, N = blkinfo(blk)
                xcb, xbt, xdd = xcb_2[blk % 2], xbt_2[blk % 2], xdd_2[blk % 2]
                xcm = xcb_mb[blk % 2]
                for t in range(NT):
                    c = t0 + t
                    bk = bank()

                    def f(e, bk=bk, t=t):
                        v = bfv(bk)
                        for m in range(5):
                            r = e.transpose(out=v[:, m * 128:(m + 1) * 128], in_=xcb[:, m, t * 128:(t + 1) * 128], identity=ident[:])
                        return r
                    S.op("pe", f, reads=xcm + [ident.b], writes=[bk.b])
                    S.op("dve", (lambda bk, t: lambda e: e.tensor_copy(out=xbt[t][:], in_=bfv(bk)[:, 0:640]))(bk, t), reads=[bk.b], writes=[xbt[t].b])
                    for d in range(2):
                        S.op("pool" if d == 0 else "dve",
                             (lambda t, d, c: lambda e: e.tensor_tensor(out=xdd[t][d][:].rearrange("p (h q) -> p h q", h=8),
                                                                        in0=xbt[t][:, 0:512].rearrange("p (h q) -> p h q", h=8),
                                                                        in1=mult_s[:, c, d * 32 + g * 8:d * 32 + g * 8 + 8].unsqueeze(2).to_broadcast([128, 8, 64]), op=ALU.mult))(t, d, c),
                             reads=[xbt[t].b, mult_s.b], writes=[xdd[t][d].b])
                for d in range(2):
                    def f(e, d=d):
                        for t in range(NT):
                            r = e.matmul(accS[d][:, :], lhsT=xbt[t][:, 512:640], rhs=xdd[t][d][:], start=(blk == 0 and t == 0), stop=(blk == nblk - 1 and t == NT - 1))
                        return r
                    S.op("pe", f, reads=[xbt[t].b for t in range(NT)] + [xdd[t][d].b for t in range(NT)], writes=[accS[d].b])

            stageA(0)
            for blk in range(nblk):
                if blk + 1 < nblk:
                    stageA(blk + 1)
                if blk == 2 and g + 1 < 4:
                    s2_prefetch(g + 1)
                stageB(blk)
            for d in range(2):
                S.op("act" if d == 0 else "dve",
                     (lambda d: (lambda e: e.activation(out=sin[:, d, g * 512:(g + 1) * 512], in_=accS[d][:, :], func=AF.Copy)) if d == 0 else
                      (lambda e: e.tensor_copy(out=sin[:, d, g * 512:(g + 1) * 512], in_=accS[d][:, :])))(d),
                     reads=[accS[d].b], writes=[sin.b])
        bank_rr["n"] = 8
        pc_issue(1000)
        S.barrier()
        ps1.close()
        dump("sin", sin[:], [128, 2, 2048], F32, [sin.b])
        if upto <= 1:
            return finish(nc, S, dbg_out)


        pm = ExitStack()
        t_dt = dtraw_o
        adto = sb("adto", [128, NCH, 64], BF16, pm)
        t_ai = sb("t_ai", [128, NCH, 64], F32, pm)
        t_tot = sb("t_tot", [128, NCH, 64], F32, pm)
        nadto = sb("nadto", [128, NCH, 64], BF16, pm)
        t_E = sb("t_E", [128, NCH, 32], F32, pm)
        t_dtd = sb("t_dtd", [128, NCH, 64], F32, pm)
        t_sc = sb("t_sc", [128, NCH, 64], F32, pm)
        softplus_inplace(t_dt, NCH)
        S.op("dve", lambda e: e.tensor_tensor(out=adto[:], in0=t_dt[:], in1=abc[:].unsqueeze(1).to_broadcast([128, NCH, 64]), op=ALU.mult),
             reads=[t_dt.b, abc.b], writes=[adto.b])
        cums(adto, NCH, t_ai, t_tot)
        F_, B_ = slice(0, 32), slice(32, 64)
        S.op("dve", lambda e: e.tensor_scalar(out=nadto[:], in0=adto[:], scalar1=-1.0, scalar2=None, op0=ALU.mult), reads=[adto.b], writes=[nadto.b])
        S.op("dve", lambda e: e.tensor_tensor(out=t_E[:], in0=t_ai[:, :, B_], in1=adto[:, :, B_], op=ALU.subtract), reads=[t_ai.b, adto.b], writes=[t_E.b])
        S.op("dve", lambda e: e.tensor_tensor(out=t_dtd[:, :, F_], in0=t_tot[:, :, F_], in1=t_ai[:, :, F_], op=ALU.subtract), reads=[t_ai.b, t_tot.b], writes=[t_dtd.b])
        S.op("dve", lambda e: e.tensor_copy(out=t_dtd[:, :, B_], in_=t_E[:]), reads=[t_E.b], writes=[t_dtd.b])
        S.op("act", lambda e: e.activation(out=t_dtd[:], in_=t_dtd[:], func=AF.Exp), reads=[t_dtd.b], writes=[t_dtd.b])
        S.op("dve", lambda e: e.tensor_tensor(out=t_dtd[:], in0=t_dtd[:], in1=t_dt[:], op=ALU.mult), reads=[t_dtd.b, t_dt.b], writes=[t_dtd.b])
        S.op("dve", lambda e: e.tensor_copy(out=t_sc[:, :, F_], in_=t_ai[:, :, F_]), reads=[t_ai.b], writes=[t_sc.b])
        S.op("dve", lambda e: e.tensor_tensor(out=t_sc[:, :, B_], in0=t_tot[:, :, B_], in1=t_E[:], op=ALU.subtract), reads=[t_tot.b, t_E.b], writes=[t_sc.b])
        S.op("act", lambda e: e.activation(out=t_sc[:], in_=t_sc[:], func=AF.Exp), reads=[t_sc.b], writes=[t_sc.b])
        S.op("act", lambda e: e.activation(out=t_tot[:], in_=t_tot[:], func=AF.Exp), reads=[t_tot.b], writes=[t_tot.b])
        t_et = t_tot

        wmg = sb("wmg", [128, KT, 768], BF16, pm)
        dgm2 = [sb(f"dgm{i}", [128, 7, 128], BF16, pm) for i in range(2)]
        xprem = sb("xprem", [128, 2056], BF16, pm)
        xc = sb("xc", [128, 6, SEG], BF16, pm)
        dpar = lambda name, shape, dt: [[sb(f"{name}{d}_{i}", shape, dt, pm) for i in range(2)] for d in range(2)]
        xbt2 = dpar("mxbt", [128, 640], BF16)
        L2 = dpar("Lb", [128, 8, 128], BF16)
        M2 = dpar("Mb", [128, 8, 128], BF16)
        xdt2 = dpar("xdt", [128, 512], BF16)
        xdd2 = dpar("mxdd", [128, 512], BF16)
        yfb2 = dpar("yfb", [128, 512], F32)
        Sbf2 = dpar("Sbf", [128, 512], BF16)
        xD2 = [sb(f"xD{i}", [128, 512], BF16, pm) for i in range(2)]
        t1_2 = [sb(f"t1_{d}", [128, 512], F32, pm) for d in range(2)]
        Sst = [sb(f"Sst{d}", [128, 512], F32, pm) for d in range(2)]
        ydram = Buf("yscr")

        def h8(ap):
            return ap.rearrange("p (h q) -> p h q", h=8)

        xc_2 = [xc, sb("xcB", [128, 6, SEG], BF16, pm)]

        def prologue_jobs(g):
            xc_t = xc_2[g % 2]
            tiles = [g * 4 + m for m in range(4)] + [16 + g, 20 + g]
            jobs = []

            def jw():
                wload(wmg, wmg[:, :, 0:512], w_in, O_X + g * 512, 512)
                wload(wmg, wmg[:, :, 512:640], w_in, O_B + g * 128, 128)
                wload(wmg, wmg[:, :, 640:768], w_in, O_C + g * 128, 128)
            jobs.append(jw)
            for m in range(6):
                dg = dgm2[m % 2]
                jobs.append((lambda dg, m: lambda: build_diag(dg, dg[:], tiles[m], 7, scw, "dve"))(dg, m))
                for j in range(5):
                    def ji(m=m, j=j):
                        s0 = 124 + 512 * j
                        w = min(512, 2180 - s0)
                        bk = bank()

                        def f(e):
                            for k in range(KT):
                                r = e.matmul(bk[:, 0:w], lhsT=wmg[:, k, m * 128:(m + 1) * 128], rhs=hnT[:, k, s0:s0 + w], start=(k == 0), stop=(k == KT - 1))
                            return r
                        S.op("pe", f, reads=[wmg.b, hnT.b], writes=[bk.b])
                        S.op("act", lambda e: e.activation(out=xprem[:, 512 * j:512 * j + w], in_=bk[:, 0:w], func=AF.Copy), reads=[bk.b], writes=[xprem.b])
                    jobs.append(ji)
                for q in range(4):
                    def jc(m=m, q=q, dg=dg):
                        bk = bank()

                        def f(e):
                            for tap in range(7):
                                r = e.matmul(bk[:, :], lhsT=dg[:, tap, :], rhs=xprem[:, 1 + 512 * q + tap:1 + 512 * q + tap + 512], start=(tap == 0), stop=(tap == 6))
                            return r
                        S.op("pe", f, reads=[dg.b, xprem.b], writes=[bk.b])
                        tl = tiles[m]
                        S.op("act", lambda e: e.activation(out=xc_t[:, m, q * 512:(q + 1) * 512], in_=bk[:, :], func=AF.Silu, bias=scb[:, tl:tl + 1], scale=1.0),
                             reads=[bk.b, scb.b], writes=[xc_t.b])
                    jobs.append(jc)
            return jobs

        next_jobs = prologue_jobs(0)
        for g in range(4):
            for jb in next_jobs:
                jb()
            next_jobs = prologue_jobs(g + 1) if g + 1 < 4 else []
            xc = xc_2[g % 2]
            if g == 0:
                dump("xc0", xc[:], [128, 6, SEG], BF16, [xc.b])

            sbi = [0, 0]
            for d in range(2):
                S.op("dve", (lambda d: lambda e: e.tensor_copy(out=Sst[d][:], in_=sin[:, d, g * 512:(g + 1) * 512]))(d), reads=[sin.b], writes=[Sst[d].b])
                S.op("act", (lambda d: lambda e: e.activation(out=Sbf2[d][0][:], in_=Sst[d][:], func=AF.Copy))(d), reads=[Sst[d].b], writes=[Sbf2[d][0].b])

            def stage1(d, it):
                c = it if d == 0 else NCH - 1 - it
                par = it % 2
                j0 = d * 32 + g * 8
                umat = uinc if d == 0 else nustr
                neg = negf if d == 0 else negb
                cs = slice(c * 128, (c + 1) * 128)
                xbt, Lb, Mb, xdt, xdd_ = xbt2[d][par], L2[d][par], M2[d][par], xdt2[d][par], xdd2[d][par]
                xD = xD2[par]
                bk = bank()

                def f(e):
                    v = bfv(bk)
                    for m in range(5):
                        e.transpose(out=v[:, m * 128:(m + 1) * 128], in_=xc[:, m, cs], identity=ident[:])
                    return e.matmul(bk[:, 320:448], lhsT=xc[:, 4, cs], rhs=xc[:, 5, cs], start=True, stop=True)
                S.op("pe", f, reads=[xc.b, ident.b], writes=[bk.b])
                S.op("act", lambda e: e.activation(out=xbt[:], in_=bfv(bk)[:, 0:640], func=AF.Copy), reads=[bk.b], writes=[xbt.b])
                for half in range(2):
                    bkl = bank()

                    def f(e, bkl=bkl, half=half):
                        jj = j0 + 4 * half
                        e.matmul(bkl[:, :], lhsT=umat[:], rhs=nadto[:, c, jj:jj + 4].unsqueeze(2).to_broadcast([128, 4, 128]), start=True, stop=False)
                        e.matmul(bkl[:, :], lhsT=ident[:], rhs=neg[:], start=False, stop=False)
                        for hh in range(4):
                            r = e.matmul(bkl[:, hh * 128:(hh + 1) * 128], lhsT=adto[:, c, jj + hh:jj + hh + 1].to_broadcast([128, 128]), rhs=umat[:], start=False, stop=(hh == 3))
                        return r
                    S.op("pe", f, reads=[umat.b, ident.b, neg.b, adto.b, nadto.b], writes=[bkl.b])
                    S.op("act", (lambda bkl, half: lambda e: e.activation(out=Lb[:, half * 4:(half + 1) * 4, :], in_=bkl[:, :].rearrange("p (h l) -> p h l", h=4), func=AF.Exp))(bkl, half),
                         reads=[bkl.b], writes=[Lb.b])
                S.op("dve", lambda e: e.tensor_tensor(out=Mb[:], in0=Lb[:], in1=bk[:, 320:448].unsqueeze(1).to_broadcast([128, 8, 128]), op=ALU.mult),
                     reads=[bk.b, Lb.b], writes=[Mb.b])
                S.op("pool", lambda e: e.tensor_tensor(out=h8(xdt[:]), in0=h8(xbt[:, 0:512]), in1=t_dt[:, c, j0:j0 + 8].unsqueeze(2).to_broadcast([128, 8, 64]), op=ALU.mult),
                     reads=[xbt.b, t_dt.b], writes=[xdt.b])
                S.op("pool", lambda e: e.tensor_tensor(out=h8(xdd_[:]), in0=h8(xbt[:, 0:512]), in1=t_dtd[:, c, j0:j0 + 8].unsqueeze(2).to_broadcast([128, 8, 64]), op=ALU.mult),
                     reads=[xbt.b, t_dtd.b], writes=[xdd_.b])
                if d == 0:
                    S.op("pool", lambda e: e.tensor_tensor(out=h8(xD[:]), in0=h8(xbt[:, 0:512]), in1=dsk[:, g * 8:g * 8 + 8].unsqueeze(2).to_broadcast([128, 8, 64]), op=ALU.mult),
                         reads=[xbt.b, dsk.b], writes=[xD.b])

            def stage2(d, it):
                c = it if d == 0 else NCH - 1 - it
                par = it % 2
                j0 = d * 32 + g * 8
                cs = slice(c * 128, (c + 1) * 128)
                xbt, Mb, xdt, xdd_ = xbt2[d][par], M2[d][par], xdt2[d][par], xdd2[d][par]
                xD = xD2[par]
                Sd, t1 = Sst[d], t1_2[d]
                bks = bank()
                S.op("pe", lambda e: e.matmul(bks[:, :], lhsT=xbt[:, 512:640], rhs=xdd_[:], start=True, stop=True), reads=[xbt.b, xdd_.b], writes=[bks.b])
                bko = bank()
                Sb_cur = Sbf2[d][sbi[d]]
                S.op("pe", lambda e: e.matmul(bko[:, :], lhsT=xc[:, 5, cs], rhs=Sb_cur[:], start=True, stop=True), reads=[xc.b, Sb_cur.b], writes=[bko.b])
                bky = bank()
                if d == 0:
                    def f(e):
                        e.matmul(bky[:, :], lhsT=ident[:], rhs=xD[:], start=True, stop=False)
                        for h in range(8):
                            r = e.matmul(bky[:, h * 64:(h + 1) * 64], lhsT=Mb[:, h, :], rhs=xdt[:, h * 64:(h + 1) * 64], start=False, stop=(h == 7))
                        return r
                    S.op("pe", f, reads=[ident.b, Mb.b, xdt.b, xD.b], writes=[bky.b])
                else:
                    def f(e):
                        for h in range(8):
                            r = e.matmul(bky[:, h * 64:(h + 1) * 64], lhsT=Mb[:, h, :], rhs=xdt[:, h * 64:(h + 1) * 64], start=True, stop=True)
                        return r
                    S.op("pe", f, reads=[Mb.b, xdt.b], writes=[bky.b])
                if it + 1 < NCH:
                    S.op("dve", lambda e: e.tensor_tensor(out=h8(Sd[:]), in0=h8(Sd[:]), in1=t_et[:, c, j0:j0 + 8].unsqueeze(2).to_broadcast([128, 8, 64]), op=ALU.mult),
                         reads=[Sd.b, t_et.b], writes=[Sd.b])
                    S.op("dve", lambda e: e.tensor_tensor(out=Sd[:], in0=Sd[:], in1=bks[:, :], op=ALU.add), reads=[Sd.b, bks.b], writes=[Sd.b])
                    sbi[d] = 1 - sbi[d]
                    nxt = Sbf2[d][sbi[d]]
                    S.op("act", lambda e: e.activation(out=nxt[:], in_=Sd[:], func=AF.Copy), reads=[Sd.b], writes=[nxt.b])
                S.op("dve", lambda e: e.tensor_tensor(out=h8(t1[:]), in0=h8(bko[:, :]), in1=t_sc[:, c, j0:j0 + 8].unsqueeze(2).to_broadcast([128, 8, 64]), op=ALU.mult),
                     reads=[bko.b, t_sc.b], writes=[t1.b])
                yfb = yfb2[d][par]
                S.op("dve", lambda e: e.tensor_tensor(out=yfb[:], in0=t1[:], in1=bky[:, :], op=ALU.add), reads=[t1.b, bky.b], writes=[yfb.b])
                if g == 0 and c in (15, 0):
                    dump("y%d_%d" % (d, c), yfb[:], [128, 512], F32, [yfb.b])
                S.dma("sp", "yout%d_%d" % (d, par), lambda e: [e.dma_start(out=yscr[d, g, c], in_=yfb[:])], reads=[yfb.b], writes=[ydram])

            stage1(0, 0)
            stage1(1, 0)
            for it in range(NCH):
                if it + 1 < NCH:
                    stage1(0, it + 1)
                    stage1(1, it + 1)
                stage2(0, it)
                for _ in range(2):
                    if next_jobs:
                        next_jobs.pop(0)()
                stage2(1, it)
                for _ in range(2):
                    if next_jobs:
                        next_jobs.pop(0)()
        S.barrier()
        pm.close()
        pA.close()
        if upto <= 2:
            return finish(nc, S, dbg_out)


        cvscr = nc.dram_tensor("cvscr", [8, 128, SEG], BF16).ap()
        cvdram = Buf("cvscr")
        pc = ExitStack()
        cdw = sb("cdw", [128, 8, 31], F32, pc)
        dgc2 = [sb(f"dgc{i}", [128, 31, 128], BF16, pc) for i in range(2)]
        wc2 = [sb(f"wc{i}", [128, KT, 256], BF16, pc) for i in range(2)]
        abuf2 = [sb(f"abuf{i}", [128, 2080], BF16, pc) for i in range(2)]
        sgc = sb("sgc", [128, 512], F32, pc)
        cvt2 = [sb(f"cvo{i}", [128, 512], BF16, pc) for i in range(2)]
        S.dma("sp", "cdw", lambda e: [e.dma_start(out=cdw[:], in_=cdw_d)], writes=[cdw.b])
        for ct in range(8):
            wc, dgc, abuf = wc2[ct % 2], dgc2[ct % 2], abuf2[ct % 2]
            wload(wc, wc[:, :, 0:128], w_in, O_VAL + ct * 128, 128)
            wload(wc, wc[:, :, 128:256], w_in, O_GATE + ct * 128, 128)
            for tap in range(31):
                S.op("dve",
                     (lambda tap: lambda e: e.tensor_scalar(out=dgc[:, tap, :], in0=ident[:], scalar1=cdw[:, ct, tap:tap + 1], scalar2=None, op0=ALU.mult))(tap),
                     reads=[ident.b, cdw.b], writes=[dgc.b])
            for j in range(5):
                s0 = 112 + 512 * j
                w = min(512, 2192 - s0)
                bv, bg = bank(), bank()
                for (bk_, off) in ((bv, 0), (bg, 128)):
                    def f(e, bk_=bk_, off=off, s0=s0, w=w):
                        for k in range(KT):
                            r = e.matmul(bk_[:, 0:w], lhsT=wc[:, k, off:off + 128], rhs=hnT[:, k, s0:s0 + w], start=(k == 0), stop=(k == KT - 1))
                        return r
                    S.op("pe", f, reads=[wc.b, hnT.b], writes=[bk_.b])
                S.op("act", (lambda bg, w: lambda e: e.activation(out=sgc[:, 0:w], in_=bg[:, 0:w], func=AF.Sigmoid))(bg, w), reads=[bg.b], writes=[sgc.b])
                S.op("dve", (lambda bv, j, w: lambda e: e.tensor_tensor(out=abuf[:, 512 * j:512 * j + w], in0=bv[:, 0:w], in1=sgc[:, 0:w], op=ALU.mult))(bv, j, w),
                     reads=[bv.b, sgc.b], writes=[abuf.b])
            for q in range(4):
                bk = bank()

                def f(e, bk=bk, q=q):
                    for tap in range(31):
                        r = e.matmul(bk[:, :], lhsT=dgc[:, tap, :], rhs=abuf[:, 1 + 512 * q + tap:1 + 512 * q + tap + 512], start=(tap == 0), stop=(tap == 30))
                    return r
                S.op("pe", f, reads=[dgc.b, abuf.b], writes=[bk.b])
                cvt = cvt2[q % 2]
                S.op("act", (lambda bk, cvt: lambda e: e.activation(out=cvt[:], in_=bk[:, :], func=AF.Identity, bias=vec8[:, 0, ct:ct + 1], scale=1.0))(bk, cvt),
                     reads=[bk.b, vec8.b], writes=[cvt.b])
                S.dma("sp", "cvo%d" % (q % 2), (lambda cvt, q: lambda e: [e.dma_start(out=cvscr[ct, :, q * 512:(q + 1) * 512], in_=cvt[:])])(cvt, q), reads=[cvt.b], writes=[cvdram])
        S.barrier()
        pc.close()
        if upto <= 3:
            return finish(nc, S, dbg_out)

        pb = ExitStack()
        hs2 = sb("hs2", [128, 4, D], F32, pb)
        wb2 = [sb(f"wb{i}", [128, 4096], BF16, pb) for i in range(3)]
        wbc = {"i": 0}

        def wchunk(W, c0, ncols, kt=KT, r0=0):
            wb = wb2[wbc["i"] % 3]
            wbc["i"] += 1
            view = wb[:, 0:kt * ncols].rearrange("p (k c) -> p k c", k=kt)
            Wd_, Wb_ = W
            src = Wd_[r0:r0 + kt * 128, c0:c0 + ncols].rearrange("(k p) c -> p k c", p=128)
            S.dma("sp", "w_" + wb.b.name, lambda e: [e.dma_start(out=view, in_=src)], reads=[Wb_], writes=[wb.b])
            return wb, view

        for T_ in range(4):
            tok0 = 512 * T_
            sc0 = 128 + tok0
            tcol = slice(sc0, sc0 + 512)
            pmix = ExitStack()
            cvt = sb("cvt", [128, 8, 512], BF16, pmix)
            aact = sb("aact", [128, 8, 512], BF16, pmix)
            mc = sb("mc", [128, 8, 512], F32, pmix)
            vT = sb("vT", [128, 16, 512], BF16, pmix)
            merged = sb("merged", [128, 8, 512], BF16, pmix)
            lnm = sb("lnm", [128, 512], F32, pmix)
            lnr = sb("lnr", [128, 512], F32, pmix)
            lt = sb("lt", [128, 512], F32, pmix)
            lt2 = sb("lt2", [128, 512], F32, pmix)
            sgb = sb("sgb", [128, 512], F32, pmix)
            sq2 = [sb(f"sq{i}", [128, 512], BF16, pmix) for i in range(2)]
            S.dma("sp", "cvt", lambda e: [e.dma_start(out=cvt[:], in_=cvscr[:, :, tok0:tok0 + 512].rearrange("c p t -> p c t"))], reads=[cvdram], writes=[cvt.b])
            NBUF = 4
            normw = sb("normw", [128, 2048], F32, pmix)
            S.dma("sp", "normw", lambda e: [e.dma_start(out=normw[:], in_=normw_d.partition_broadcast(128))], writes=[normw.b])
            yl2 = [sb(f"yl{i}", [128, 2, 512], F32, pmix) for i in range(NBUF)]
            szb2 = [sb(f"szb{i}", [128, 512], F32, pmix) for i in range(NBUF)]
            vb2 = [sb(f"vb{i}", [128, 512], F32, pmix) for i in range(NBUF)]
            vnb2 = [sb(f"vnb{i}", [128, 512], BF16, pmix) for i in range(NBUF)]
            vjunk = sb("vjunk", [128, 512], BF16, pmix)
            vst2 = [sb(f"vst{i}", [128, 4], F32, pmix) for i in range(NBUF)]
            wz_cache = {}
            wzb2 = [sb(f"wzb{i}", [128, 4096], BF16, pmix) for i in range(2)]

            def gateZ(u):
                g, j = u // 4, u % 4
                if g not in wz_cache:
                    wzb = wzb2[g % 2]
                    wzview = wzb[:].rearrange("p (k c) -> p k c", k=KT)
                    S.dma("sp", "w_" + wzb.b.name, (lambda wzview, g: lambda e: [e.dma_start(out=wzview, in_=bf_z[0][:, g * 512:(g + 1) * 512].rearrange("(k p) c -> p k c", p=128))])(wzview, g),
                          reads=[bf_z[1]], writes=[wzb.b])
                    wz_cache[g] = (wzb, wzview)
                wz_, wzv = wz_cache[g]
                c = 4 * T_ + j
                i2 = u % NBUF
                yl, szb, vb, vnb, vst = yl2[i2], szb2[i2], vb2[i2], vnb2[i2], vst2[i2]
                S.dma("sp", "yl%d" % i2, lambda e: [e.dma_start(out=yl[:], in_=yscr[:, g, c].rearrange("d p f -> p d f"))], reads=[ydram], writes=[yl.b])
                bkz = bank()

                def f(e):
                    for k in range(KT):
                        r = e.matmul(bkz[:, :], lhsT=hnT[:, k, sc0 + j * 128:sc0 + (j + 1) * 128], rhs=wzv[:, k, :], start=(k == 0), stop=(k == KT - 1))
                    return r
                S.op("pe", f, reads=[hnT.b, wz_.b], writes=[bkz.b])
                S.op("act", lambda e: e.activation(out=szb[:], in_=bkz[:, :], func=AF.Sigmoid), reads=[bkz.b], writes=[szb.b])
                S.op("dve", lambda e: e.tensor_tensor(out=vb[:], in0=yl[:, 0, :], in1=yl[:, 1, :], op=ALU.add), reads=[yl.b], writes=[vb.b])
                S.op("dve", lambda e: e.tensor_tensor(out=vb[:], in0=vb[:], in1=bkz[:, :], op=ALU.mult), reads=[vb.b, bkz.b], writes=[vb.b])
                S.op("dve", lambda e: e.tensor_tensor(out=vb[:], in0=vb[:], in1=szb[:], op=ALU.mult), reads=[vb.b, szb.b], writes=[vb.b])
                S.op("act", lambda e: e.activation(out=vjunk[:], in_=vb[:], func=AF.Square, accum_out=vst[:, 0:1]), reads=[vb.b], writes=[vjunk.b, vst.b])
                rstd_pool(vst, vst[:, 0:1], 512)
                S.op("dve", lambda e: e.scalar_tensor_tensor(out=vnb[:], in0=vb[:], scalar=vst[:, 2:3], in1=normw[:, g * 512:(g + 1) * 512], op0=ALU.mult, op1=ALU.mult),
                     reads=[vb.b, vst.b, normw.b], writes=[vnb.b])

            def gateT(u):
                g, j = u // 4, u % 4
                vnb = vnb2[u % NBUF]
                bkt = bank()

                def f(e):
                    v = bfv(bkt)
                    for m in range(4):
                        r = e.transpose(out=v[:, m * 128:(m + 1) * 128], in_=vnb[:, m * 128:(m + 1) * 128], identity=ident[:])
                    return r
                S.op("pe", f, reads=[vnb.b, ident.b], writes=[bkt.b])
                S.op("act", lambda e: e.activation(out=vT[:, g * 4:(g + 1) * 4, j * 128:(j + 1) * 128], in_=bfv(bkt)[:, 0:512].rearrange("p (m t) -> p m t", m=4), func=AF.Copy),
                     reads=[bkt.b], writes=[vT.b])

            gate_sched = []
            for u in range(16 + 3):
                if u < 16:
                    gate_sched.append(("Z", u))
                if u >= 3:
                    gate_sched.append(("T", u - 3))

            def gate_emit(n):
                for _ in range(n):
                    if gate_sched:
                        kind, u = gate_sched.pop(0)
                        (gateZ if kind == "Z" else gateT)(u)
            bm, bq = bank(), bank()

            def f(e):
                for ct in range(8):
                    r = e.matmul(bm[:, :], lhsT=onesm[:], rhs=cvt[:, ct, :], start=(ct == 0), stop=(ct == 7))
                return r
            S.op("pe", f, reads=[onesm.b, cvt.b], writes=[bm.b])
            for ct in range(8):
                sq = sq2[ct % 2]
                S.op("act", (lambda sq, ct: lambda e: e.activation(out=sq[:], in_=cvt[:, ct, :], func=AF.Square))(sq, ct), reads=[cvt.b], writes=[sq.b])
                S.op("pe", (lambda sq, ct: lambda e: e.matmul(bq[:, :], lhsT=onesm[:], rhs=sq[:], start=(ct == 0), stop=(ct == 7)))(sq, ct), reads=[onesm.b, sq.b], writes=[bq.b])
            S.op("act", lambda e: e.activation(out=lnm[:], in_=bm[:, :], func=AF.Copy), reads=[bm.b], writes=[lnm.b])
            S.op("dve", lambda e: e.tensor_tensor(out=lt[:], in0=lnm[:], in1=lnm[:], op=ALU.mult), reads=[lnm.b], writes=[lt.b])
            S.op("dve", lambda e: e.tensor_tensor(out=lt[:], in0=bq[:, :], in1=lt[:], op=ALU.subtract), reads=[bq.b, lt.b], writes=[lt.b])
            S.op("act", lambda e: e.activation(out=lt[:], in_=lt[:], func=AF.Sqrt, bias=epsc[:, 0:1], scale=1.0), reads=[lt.b, epsc.b], writes=[lt.b])
            S.op("dve", lambda e: e.reciprocal(out=lnr[:], in_=lt[:]), reads=[lt.b], writes=[lnr.b])
            gate_emit(6)
            for ct in range(8):
                S.op("dve", (lambda ct: lambda e: e.tensor_tensor(out=lt2[:], in0=cvt[:, ct, :], in1=lnm[:], op=ALU.subtract))(ct), reads=[cvt.b, lnm.b], writes=[lt2.b])
                S.op("dve", lambda e: e.tensor_tensor(out=lt2[:], in0=lt2[:], in1=lnr[:], op=ALU.mult), reads=[lt2.b, lnr.b], writes=[lt2.b])
                S.op("dve", (lambda ct: lambda e: e.tensor_scalar(out=lt2[:], in0=lt2[:], scalar1=vec8[:, 1, ct:ct + 1], scalar2=vec8[:, 2, ct:ct + 1], op0=ALU.mult, op1=ALU.add))(ct),
                     reads=[lt2.b, vec8.b], writes=[lt2.b])
                S.op("act", lambda e: e.activation(out=lt[:], in_=lt2[:], func=AF.Sigmoid), reads=[lt2.b], writes=[lt.b])
                S.op("dve", (lambda ct: lambda e: e.tensor_tensor(out=aact[:, ct, :], in0=lt2[:], in1=lt[:], op=ALU.mult))(ct), reads=[lt2.b, lt.b], writes=[aact.b])
            if T_ == 0:
                dump("aact0", aact[:], [128, 8, 512], BF16, [aact.b])
            for half in range(2):
                wa, wav = wchunk(bf_co, half * 512, 512)
                wg_, wgv = wchunk(bf_gg, half * 512, 512)
                for mm in range(4):
                    m = half * 4 + mm
                    by, bg = bank(), bank()

                    def f(e, by=by, mm=mm, wav=wav):
                        for k in range(KT):
                            r = e.matmul(by[:, :], lhsT=wav[:, k, mm * 128:(mm + 1) * 128], rhs=aact[:, k, :], start=(k == 0), stop=(k == KT - 1))
                        return r
                    S.op("pe", f, reads=[wa.b, aact.b], writes=[by.b])

                    def f(e, bg=bg, mm=mm, wgv=wgv):
                        for k in range(KT):
                            r = e.matmul(bg[:, :], lhsT=wgv[:, k, mm * 128:(mm + 1) * 128], rhs=hnT[:, k, tcol], start=(k == 0), stop=(k == KT - 1))
                        return r
                    S.op("pe", f, reads=[wg_.b, hnT.b], writes=[bg.b])
                    S.op("act", (lambda bg: lambda e: e.activation(out=sgb[:], in_=bg[:, :], func=AF.Sigmoid))(bg), reads=[bg.b], writes=[sgb.b])
                    S.op("dve", (lambda by, m: lambda e: e.tensor_tensor(out=mc[:, m, :], in0=by[:, :], in1=sgb[:], op=ALU.mult))(by, m), reads=[by.b, sgb.b], writes=[mc.b])
                    gate_emit(4)
            gate_emit(100)
            for half in range(2):
                wg_, wgv = wchunk(bf_gg, 1024 + half * 512, 512)
                for pr in range(2):
                    wa, wav = wchunk(bf_so, half * 512 + pr * 256, 256, kt=16)
                    for mm in range(2):
                        m = half * 4 + pr * 2 + mm
                        gm = pr * 2 + mm
                        by, bg = bank(), bank()

                        def f(e, by=by, mm=mm, wav=wav):
                            for k in range(16):
                                r = e.matmul(by[:, :], lhsT=wav[:, k, mm * 128:(mm + 1) * 128], rhs=vT[:, k, :], start=(k == 0), stop=(k == 15))
                            return r
                        S.op("pe", f, reads=[wa.b, vT.b], writes=[by.b])

                        def f(e, bg=bg, gm=gm, wgv=wgv):
                            for k in range(KT):
                                r = e.matmul(bg[:, :], lhsT=wgv[:, k, gm * 128:(gm + 1) * 128], rhs=hnT[:, k, tcol], start=(k == 0), stop=(k == KT - 1))
                            return r
                        S.op("pe", f, reads=[wg_.b, hnT.b], writes=[bg.b])
                        S.op("act", (lambda bg: lambda e: e.activation(out=sgb[:], in_=bg[:, :], func=AF.Sigmoid))(bg), reads=[bg.b], writes=[sgb.b])
                        S.op("dve", (lambda by: lambda e: e.tensor_tensor(out=lt[:], in0=by[:, :], in1=sgb[:], op=ALU.mult))(by), reads=[by.b, sgb.b], writes=[lt.b])
                        S.op("dve", (lambda m: lambda e: e.tensor_tensor(out=merged[:, m, :], in0=lt[:], in1=mc[:, m, :], op=ALU.add))(m), reads=[lt.b, mc.b], writes=[merged.b])
            if T_ == 0:
                dump("merged0", merged[:], [128, 8, 512], BF16, [merged.b])
            for j in range(4):
                S.dma("sp", "xr%d" % (j % 2), (lambda j: lambda e: [e.dma_start(out=hs2[:, j, :], in_=xs[sc0 + j * 128:sc0 + (j + 1) * 128, :])])(j), writes=[hs2.b])
            for nh in range(2):
                wa, wav = wchunk(bf_wo, nh * 512, 512)
                for j in range(4):
                    bk = bank()

                    def f(e, bk=bk, j=j, wav=wav):
                        for k in range(KT):
                            r = e.matmul(bk[:, :], lhsT=merged[:, k, j * 128:(j + 1) * 128], rhs=wav[:, k, :], start=(k == 0), stop=(k == KT - 1))
                        return r
                    S.op("pe", f, reads=[wa.b, merged.b], writes=[bk.b])
                    S.op("dve", (lambda bk, j, nh: lambda e: e.tensor_tensor(out=hs2[:, j, nh * 512:(nh + 1) * 512], in0=bk[:, :], in1=hs2[:, j, nh * 512:(nh + 1) * 512], op=ALU.add))(bk, j, nh),
                         reads=[bk.b, hs2.b], writes=[hs2.b])
            if T_ == 0:
                dump("hs2_0", hs2[:], [128, 4, D], F32, [hs2.b])
            S.barrier()
            pmix.close()
            pf = ExitStack()
            hn2T = sb("hn2T", [128, KT, 512], BF16, pf)
            actT = sb("actT", [128, FT, 512], BF16, pf)
            wd = sb("wd", [128, FT, D], BF16, pf)
            sgf = sb("sgf", [128, 512], F32, pf)
            ob2 = [sb(f"ob{i}", [128, D], F32, pf) for i in range(2)]
            nfin = sb("nfin", [128, D], F32, pf)
            xn2 = [sb(f"fxn{i}", [128, D], BF16, pf) for i in range(2)]
            junk = sb("fjunk", [128, D], BF16, pf)
            st2 = [sb(f"fst{i}", [128, 4], F32, pf) for i in range(2)]
            xt2 = [sb(f"hs3_{i}", [128, D], F32, pf) for i in range(2)]
            S.dma("sp", "nfin", lambda e: [e.dma_start(out=nfin[:], in_=nfin_d.partition_broadcast(128))], writes=[nfin.b])
            for (f0, nf) in ((0, 8), (8, 8), (16, 6)):
                S.dma("sp", "wd", (lambda f0, nf: lambda e: [e.dma_start(out=wd[:, f0:f0 + nf, :], in_=bf_wd[0][f0 * 128:(f0 + nf) * 128, :].rearrange("(k p) c -> p k c", p=128))])(f0, nf),
                      reads=[bf_wd[1]], writes=[wd.b])
            for j in range(4):
                i = j % 2
                xn, st = xn2[i], st2[i]
                S.op("act", (lambda st, j: lambda e: e.activation(out=junk[:], in_=hs2[:, j, :], func=AF.Square, accum_out=st[:, 0:1]))(st, j), reads=[hs2.b], writes=[junk.b, st.b])
                rstd_pool(st, st[:, 0:1], D)
                S.op("act", (lambda xn, st, j: lambda e: e.activation(out=xn[:], in_=hs2[:, j, :], func=AF.Copy, scale=st[:, 2:3]))(xn, st, j), reads=[hs2.b, st.b], writes=[xn.b])
                bk = bank()

                def f(e, bk=bk, xn=xn):
                    v = bfv(bk)
                    for k in range(KT):
                        r = e.transpose(out=v[:, k * 128:(k + 1) * 128], in_=xn[:, k * 128:(k + 1) * 128], identity=ident[:])
                    return r
                S.op("pe", f, reads=[xn.b, ident.b], writes=[bk.b])
                S.op("dve", (lambda bk, j: lambda e: e.tensor_tensor(out=hn2T[:, :, j * 128:(j + 1) * 128], in0=bfv(bk).rearrange("p (k t) -> p k t", k=KT),
                                                                       in1=vec8[:, 4, :].unsqueeze(2).to_broadcast([128, KT, 128]), op=ALU.mult))(bk, j),
                     reads=[bk.b, vec8.b], writes=[hn2T.b])
            for (f0, nf) in ((0, 4), (4, 4), (8, 4), (12, 4), (16, 4), (20, 2)):
                wg_, wgv = wchunk(bf_wg, f0 * 128, nf * 128)
                wu_, wuv = wchunk(bf_wu, f0 * 128, nf * 128)
                for fl in range(nf):
                    fi = f0 + fl
                    bg, bu = bank(), bank()
                    for (bk_, wv_, wt_) in ((bg, wgv, wg_), (bu, wuv, wu_)):
                        def f(e, bk_=bk_, wv_=wv_, fl=fl):
                            for k in range(KT):
                                r = e.matmul(bk_[:, :], lhsT=wv_[:, k, fl * 128:(fl + 1) * 128], rhs=hn2T[:, k, :], start=(k == 0), stop=(k == KT - 1))
                            return r
                        S.op("pe", f, reads=[wt_.b, hn2T.b], writes=[bk_.b])
                    S.op("act", (lambda bg: lambda e: e.activation(out=sgf[:], in_=bg[:, :], func=AF.Sigmoid))(bg), reads=[bg.b], writes=[sgf.b])
                    S.op("dve", (lambda bg: lambda e: e.tensor_tensor(out=sgf[:], in0=bg[:, :], in1=sgf[:], op=ALU.mult))(bg), reads=[bg.b, sgf.b], writes=[sgf.b])
                    S.op("dve", (lambda bu, fi: lambda e: e.tensor_tensor(out=actT[:, fi, :], in0=bu[:, :], in1=sgf[:], op=ALU.mult))(bu, fi), reads=[bu.b, sgf.b], writes=[actT.b])
            for j in range(4):
                i = j % 2
                hs3, ob, st = xt2[i], ob2[i], st2[i]
                for nh in range(2):
                    bk = bank()

                    def f(e, bk=bk, j=j, nh=nh):
                        for fi in range(FT):
                            r = e.matmul(bk[:, :], lhsT=actT[:, fi, j * 128:(j + 1) * 128], rhs=wd[:, fi, nh * 512:(nh + 1) * 512], start=(fi == 0), stop=(fi == FT - 1))
                        return r
                    S.op("pe", f, reads=[actT.b, wd.b], writes=[bk.b])
                    S.op("dve", (lambda bk, hs3, j, nh: lambda e: e.tensor_tensor(out=hs3[:, nh * 512:(nh + 1) * 512], in0=bk[:, :], in1=hs2[:, j, nh * 512:(nh + 1) * 512], op=ALU.add))(bk, hs3, j, nh),
                         reads=[bk.b, hs2.b], writes=[hs3.b])
                S.op("act", (lambda hs3, st: lambda e: e.activation(out=junk[:], in_=hs3[:], func=AF.Square, accum_out=st[:, 0:1]))(hs3, st), reads=[hs3.b], writes=[junk.b, st.b])
                rstd_pool(st, st[:, 0:1], D)
                S.op("dve", (lambda hs3, ob, st: lambda e: e.scalar_tensor_tensor(out=ob[:], in0=hs3[:], scalar=st[:, 2:3], in1=nfin[:], op0=ALU.mult, op1=ALU.mult))(hs3, ob, st),
                     reads=[hs3.b, st.b, nfin.b], writes=[ob.b])
                S.dma("act", "out%d" % i, (lambda ob, j: lambda e: [e.dma_start(out=out_d[tok0 + j * 128:tok0 + (j + 1) * 128, :], in_=ob[:])])(ob, j), reads=[ob.b], writes=[Buf("outd")])
            S.barrier()
            pf.close()
        pb.close()
        return finish(nc, S, dbg_out)
    return nc


def finish(nc, S, dbg_out):
    S.barrier()
    with nc.Block() as block:
        @block.tensor
        def _(e):
            e.nop()

        @block.scalar
        def _(e):
            e.nop()

        @block.vector
        def _(e):
            e.nop()

        @block.gpsimd
        def _(e):
            e.nop()

        @block.sync
        def _(e):
            e.nop()
    nc._sched_counts = (dict(S.cnt), dict(S.dcnt))
    return nc


def _consts():
    k = np.arange(128)[:, None]
    l = np.arange(128)[None, :]
    ident = np.eye(128, dtype=np.float32)
    uinc = (k <= l).astype(np.float32)
    nustr = -(k < l).astype(np.float32)
    negf = np.where(l < k, -30000.0, 0.0).astype(np.float32)
    negb = np.where(l > k, -30000.0, 0.0).astype(np.float32)
    return dict(cident=ident, cuinc=uinc, cnustr=nustr, cnegf=np.tile(negf, (1, 4)), cnegb=np.tile(negb, (1, 4)))


def prep_inputs(inputs):
    f = lambda a: np.ascontiguousarray(np.asarray(a, dtype=np.float32))
    x = f(inputs["x"])
    meta = f(inputs["meta_tokens"])
    pt = lambda v: np.ascontiguousarray(f(v).reshape(-1, 128).T)
    shared = dict(
        w_in=f(inputs["w_in"][0]), conv_out_w=f(inputs["conv_out_w"][0]), ssm_out_w=f(inputs["ssm_out_w"][0]),
        w_o=f(inputs["w_o"][0]), w_gate=f(inputs["w_gate"][0]), w_up=f(inputs["w_up"][0]), w_down=f(inputs["w_down"][0]),
        cdw=np.ascontiguousarray(f(inputs["conv_dw_w"][0]).T.reshape(8, 128, 31).transpose(1, 0, 2)),
        scw=np.ascontiguousarray(f(inputs["ssm_conv_w"][0]).T.reshape(24, 128, 7).transpose(1, 0, 2)),
        vec8=np.ascontiguousarray(np.stack([pt(inputs["conv_dw_b"][0]), pt(inputs["conv_ln_g"][0]), pt(inputs["conv_ln_b"][0]),
                                            pt(inputs["norm_mix"][0]), pt(inputs["norm_ffn"][0])], axis=1)),
        scb=pt(inputs["ssm_conv_b"][0]),
        rows=np.concatenate([f(inputs["dt_bias_f"][0]), f(inputs["dt_bias_b"][0]), f(inputs["a_log_f"][0]), f(inputs["a_log_b"][0])])[None, :].copy(),
        dsk=f(inputs["ssm_d"][0])[None, :].copy(),
        normw=f(inputs["ssm_norm_w"][0])[None, :].copy(),
        nfin=f(inputs["norm_final"])[None, :].copy(),
        hmask=np.concatenate([np.zeros(112, np.float32), np.ones(16, np.float32)])[:, None].copy(),
    )
    shared.update(_consts())
    in_maps = []
    for core in range(8):
        b, k = core // 4, core % 4
        xs = np.zeros((SLAB, D), np.float32)
        xs[128:2176] = x[b, k * SEG:(k + 1) * SEG]
        if k == 0:
            xs[112:128] = meta
        else:
            xs[0:128] = x[b, k * SEG - 128:k * SEG]
        if k < 3:
            xs[2176:2304] = x[b, (k + 1) * SEG:(k + 1) * SEG + 128]
        xo = np.zeros(((NOT + 1) * 128, D), np.float32)
        xo[112:128] = meta
        hal = xo[NOT * 128:]
        hal[3:6] = x[b, 0:3]
        mrow = np.zeros((1, 2 * NOT), np.float32)
        mrow[0, 0] = 1.0
        others = [j for j in range(4) if j != k]
        for bi, j in enumerate(others):
            for q in range(4):
                blk = 1 + bi * 4 + q
                t0 = 1 + 4 * (blk - 1)
                st = j * SEG + q * 512
                xo[t0 * 128:(t0 + 4) * 128] = x[b, st:st + 512]
                hal[blk * 6:blk * 6 + 3] = meta[13:16] if st == 0 else x[b, st - 3:st]
                if st + 512 < L:
                    hal[blk * 6 + 3:blk * 6 + 6] = x[b, st + 512:st + 515]
                mrow[0, t0:t0 + 4] = 1.0 if j < k else 0.0
                mrow[0, NOT + t0:NOT + t0 + 4] = 1.0 if j > k else 0.0
        m = dict(shared)
        m.update(xs=xs, xo=xo, mrow=mrow)
        in_maps.append(m)
    return in_maps


def kernel(**inputs):
    in_maps = prep_inputs(inputs)
    nc = build_nc()
    res = run_bass_kernel_spmd(nc, in_maps, core_ids=list(range(8)))
    out = np.empty((NB, L, D), np.float32)
    for core in range(8):
        b, k = core // 4, core % 4
        out[b, k * SEG:(k + 1) * SEG] = res.results[core]["out"]
    return out
```

```python
import numpy as np
from contextlib import ExitStack
import concourse.bass as bass
import concourse.mybir as mybir
from concourse.bass_utils import run_bass_kernel_spmd

F32 = mybir.dt.float32
BF16 = mybir.dt.bfloat16
AF = mybir.ActivationFunctionType
ALU = mybir.AluOpType

D = 1024
KT = 8
L = 8192
NB = 2
SEG = 2048
NCH = 16
SLAB = 2304
NOT = 49
DFF = 2816
FT = 22
EPS = 1e-6
O_VAL, O_GATE, O_Z, O_X, O_B, O_C, O_DT, O_GC, O_GS = 0, 1024, 2048, 4096, 6144, 6656, 7168, 7232, 8256

ENGS = ["pe", "act", "dve", "pool", "sp"]


class Buf:
    __slots__ = ("name", "w", "r", "excl")

    def __init__(self, name, excl=False):
        self.name = name
        self.excl = excl
        self.w = None
        self.r = []


class Sched:
    def __init__(self, nc, es):
        self.nc = nc
        self.es = es
        self.sems = {}
        for e in ENGS:
            self.sems[e] = es.enter_context(nc.semaphore(f"sem_{e}"))
        self.cnt = {e: 0 for e in ENGS}
        self.seen = {e: {} for e in ENGS}
        self.dcnt = {}

    def dsem(self, name):
        key = f"d_{name}"
        if key not in self.sems:
            self.sems[key] = self.es.enter_context(self.nc.semaphore(f"dsem_{name}"))
            self.dcnt[key] = 0
        return key

    def _deps(self, eng, reads, writes):
        deps = {}

        def add(ev):
            if ev is None:
                return
            k, v = ev
            if deps.get(k, 0) < v:
                deps[k] = v
        for b in reads:
            add(b.w)
            if b.excl:
                for ev in b.r:
                    add(ev)
        for b in writes:
            add(b.w)
            for ev in b.r:
                add(ev)
        waits = []
        seen = self.seen[eng]
        for k, v in deps.items():
            if seen.get(k, 0) < v:
                seen[k] = v
                waits.append((k, v))
        return waits

    def _mark(self, ev, reads, writes):
        for b in reads:
            if b.excl:
                b.w = ev
                b.r = []
            else:
                b.r = [x for x in b.r if x[0] != ev[0]] + [ev]
        for b in writes:
            b.w = ev
            b.r = []

    def _emit(self, name, waits, fn, dkey, n):
        nc = self.nc
        e = {"pe": nc.tensor, "act": nc.scalar, "dve": nc.vector, "pool": nc.gpsimd, "sp": nc.sync}[name]
        sems = self.sems
        for k, v in waits:
            e.wait_ge(sems[k], v)
        if fn is None:
            return
        r = fn(e)
        if dkey is None:
            r.then_inc(sems[name], 1)
        else:
            if not isinstance(r, (list, tuple)):
                r = [r]
            assert len(r) == n, (len(r), n)
            for ins in r:
                ins.then_inc(sems[dkey], 16)

    def op(self, eng, fn, reads=(), writes=()):
        waits = self._deps(eng, reads, writes)
        self.cnt[eng] += 1
        ev = (eng, self.cnt[eng])
        self._emit(eng, waits, fn, None, 0)
        self._mark(ev, reads, writes)
        return ev

    def dma(self, q, name, fn, reads=(), writes=(), n=1):
        dkey = self.dsem(name)
        waits = self._deps(q, reads, writes)
        self.dcnt[dkey] += 16 * n
        ev = (dkey, self.dcnt[dkey])
        self._emit(q, waits, fn, dkey, n)
        self._mark(ev, reads, writes)
        return ev

    def barrier(self):
        for e in ENGS:
            waits = []
            seen = self.seen[e]
            for k in ENGS:
                if self.cnt[k] > seen.get(k, 0):
                    seen[k] = self.cnt[k]
                    waits.append((k, self.cnt[k]))
            for k, v in self.dcnt.items():
                if v > seen.get(k, 0):
                    seen[k] = v
                    waits.append((k, v))
            self._emit(e, waits, None, None, 0)


class T:
    def __init__(self, t, name, excl=False):
        self.t = t
        self.b = Buf(name, excl)

    def __getitem__(self, k):
        return self.t[k]


def build_nc(upto=99, dbg=()):
    nc = bass.Bass("TRN2", target_bir_lowering=False)

    def din(name, shape):
        return nc.dram_tensor(name, list(shape), F32, kind="ExternalInput").ap()

    xs = din("xs", [SLAB, D])
    xo = din("xo", [(NOT + 1) * 128, D])
    w_in = din("w_in", [D, 9280])
    conv_out_w = din("conv_out_w", [D, D])
    ssm_out_w = din("ssm_out_w", [2048, D])
    w_o = din("w_o", [D, D])
    w_gate = din("w_gate", [D, DFF])
    w_up = din("w_up", [D, DFF])
    w_down = din("w_down", [DFF, D])
    cdw_d = din("cdw", [128, 8, 31])
    scw_d = din("scw", [128, 24, 7])
    vec8_d = din("vec8", [128, 5, 8])
    scb_d = din("scb", [128, 24])
    rows_d = din("rows", [1, 128])
    dsk_d = din("dsk", [1, 32])
    normw_d = din("normw", [1, 2048])
    nfin_d = din("nfin", [1, D])
    hmask_d = din("hmask", [128, 1])
    mrow_d = din("mrow", [1, 2 * NOT])
    cident_d = din("cident", [128, 128])
    cuinc_d = din("cuinc", [128, 128])
    cnustr_d = din("cnustr", [128, 128])
    cnegf_d = din("cnegf", [128, 512])
    cnegb_d = din("cnegb", [128, 512])
    out_d = nc.dram_tensor("out", [SEG, D], F32, kind="ExternalOutput").ap()

    hns = nc.dram_tensor("hns", [NOT + 1, 128, D], BF16).ap()
    yscr = nc.dram_tensor("yscr", [2, 4, NCH, 128, 512], F32).ap()

    dbg_out = {}

    with ExitStack() as es:
        S = Sched(nc, es)

        uniq = {"n": 0}

        def sb(name, shape, dt, st=es):
            uniq["n"] += 1
            return T(st.enter_context(nc.sbuf_tensor("s%d_%s" % (uniq["n"], name), list(shape), dt)), name)

        banks = [T(es.enter_context(nc.psum_tensor(f"pb{i}", [128, 512], F32)), f"pb{i}", excl=True) for i in range(8)]
        bank_rr = {"i": 0, "n": 8}

        def bank():
            b = banks[bank_rr["i"] % bank_rr["n"]]
            bank_rr["i"] += 1
            return b

        def bfv(b):
            return b.t[:].bitcast(BF16)

        def dump(name, src, shape, dt, reads):
            if name not in dbg:
                return
            dd = nc.dram_tensor("dbg_" + name, list(shape), dt, kind="ExternalOutput").ap()
            dbg_out[name] = dd
            S.dma("sp", "dbg_" + name, lambda e: [e.dma_start(out=dd, in_=src)], reads=reads, writes=[Buf("dbgo")])

        ident = sb("ident", [128, 128], BF16)
        ones = sb("ones", [128, 128], BF16)
        onesm = sb("onesm", [128, 128], BF16)
        epsc = sb("epsc", [128, 1], F32)
        onec = sb("onec", [128, 1], F32)
        vec8 = sb("vec8", [128, 5, 8], F32)
        hnT = sb("hnT", [128, KT, SLAB], BF16)
        pA = ExitStack()
        uinc = sb("uinc", [128, 128], BF16, pA)
        nustr = sb("nustr", [128, 128], BF16, pA)
        negf = sb("negf", [128, 512], BF16, pA)
        negb = sb("negb", [128, 512], BF16, pA)
        scb = sb("scb", [128, 24], F32, pA)
        scw = sb("scw", [128, 24, 7], F32, pA)
        rows = sb("rows", [128, 128], F32, pA)
        abc = sb("abc", [128, 64], F32, pA)
        dsk = sb("dsk", [128, 32], F32, pA)
        hmask = sb("hmask", [128, 1], F32, pA)
        mrow = sb("mrow", [128, 2 * NOT], F32, pA)
        wdt = sb("wdt", [128, KT, 64], BF16, pA)
        sin = sb("sin", [128, 2, 2048], F32, pA)
        dtraw_o = sb("dtraw_o", [128, NCH, 64], F32, pA)
        ps1 = ExitStack()
        dtraw_s = sb("dtraw_s", [128, NOT, 64], F32, ps1)
        p0 = ExitStack()
        xt2 = [sb(f"xt{i}", [128, D], F32, p0) for i in range(8)]
        xn2 = [sb(f"xn{i}", [128, D], BF16, p0) for i in range(8)]
        junk = sb("junk", [128, D], BF16, p0)
        st2 = [sb(f"st{i}", [128, 4], F32, p0) for i in range(8)]
        hnt2 = [sb(f"hnt{i}", [128, KT, 128], BF16, p0) for i in range(4)]

        for (t_, d_) in [(ident, cident_d), (uinc, cuinc_d), (nustr, cnustr_d), (negf, cnegf_d), (negb, cnegb_d)]:
            S.dma("pool", "c_" + t_.b.name, (lambda t_, d_: lambda e: [e.dma_start(out=t_[:], in_=d_)])(t_, d_), writes=[t_.b])
        for (t_, d_) in [(vec8, vec8_d), (scb, scb_d), (scw, scw_d), (hmask, hmask_d)]:
            S.dma("sp", "c_" + t_.b.name, (lambda t_, d_: lambda e: [e.dma_start(out=t_[:], in_=d_)])(t_, d_), writes=[t_.b])
        for (t_, d_, n_) in [(rows, rows_d, 128), (dsk, dsk_d, 32), (mrow, mrow_d, 2 * NOT)]:
            S.dma("sp", "c_" + t_.b.name, (lambda t_, d_: lambda e: [e.dma_start(out=t_[:], in_=d_.partition_broadcast(128))])(t_, d_), writes=[t_.b])
        S.op("dve", lambda e: e.memset(ones[:], 1.0), writes=[ones.b])
        S.op("dve", lambda e: e.memset(onesm[:], 1.0 / 1024.0), writes=[onesm.b])
        S.op("dve", lambda e: e.memset(epsc[:], EPS), writes=[epsc.b])
        S.op("dve", lambda e: e.memset(onec[:], 1.0), writes=[onec.b])
        S.op("act", lambda e: e.activation(out=abc[:], in_=rows[:, 64:128], func=AF.Exp), reads=[rows.b], writes=[abc.b])
        S.op("dve", lambda e: e.tensor_scalar(out=abc[:], in0=abc[:], scalar1=-1.0, scalar2=None, op0=ALU.mult), reads=[abc.b], writes=[abc.b])

        def wload(dst, dst_ap, W, c0, ncols, kt=KT, r0=0):
            src = W[r0:r0 + kt * 128, c0:c0 + ncols].rearrange("(k p) c -> p k c", p=128)
            S.dma("pool", "w_" + dst.b.name, lambda e: [e.dma_start(out=dst_ap, in_=src)], writes=[dst.b])

        wload(wdt, wdt[:], w_in, O_DT, 64)

        pc_jobs = []

        def pc_issue(n):
            for _ in range(n):
                if pc_jobs:
                    pc_jobs.pop(0)()

        def precast(name, W, c0, ncols, nrows):
            dst = nc.dram_tensor("bf_" + name, [nrows, ncols], BF16).ap()
            b = Buf("bf_" + name)
            for r0 in range(0, nrows, 128):
                for cc in range(0, ncols, 2048):
                    w = min(2048, ncols - cc)
                    pc_jobs.append((lambda r0, cc, w: lambda: S.dma("pool", "pc_" + name, lambda e: [e.dma_start(out=dst[r0:r0 + 128, cc:cc + w], in_=W[r0:r0 + 128, c0 + cc:c0 + cc + w])],
                                                                   writes=[b]))(r0, cc, w))
            return dst, b
        bf_co = precast("co", conv_out_w, 0, D, D)
        bf_gg = precast("gg", w_in, O_GC, 2048, D)
        bf_so = precast("so", ssm_out_w, 0, D, 2048)
        bf_wo = precast("wo", w_o, 0, D, D)
        bf_wg = precast("wg", w_gate, 0, DFF, D)
        bf_wu = precast("wu", w_up, 0, DFF, D)
        bf_wd = precast("wd", w_down, 0, D, DFF)
        bf_z = precast("z", w_in, O_Z, 2048, D)

        ntc = {"i": 0}

        def rstd_of(st, ssq_ap, dim):
            S.op("act", lambda e: e.activation(out=st[:, 1:2], in_=ssq_ap, func=AF.Sqrt, bias=epsc[:, 0:1], scale=1.0 / dim),
                 reads=[st.b, epsc.b], writes=[st.b])
            S.op("dve", lambda e: e.reciprocal(out=st[:, 2:3], in_=st[:, 1:2]), reads=[st.b], writes=[st.b])

        def norm_tile(src_rows, dst, dst_ap, gidx):
            i = ntc["i"] % 8
            ntc["i"] += 1
            xt, xn, st = xt2[i], xn2[i], st2[i]
            S.dma("sp", "xt%d" % i, lambda e: [e.dma_start(out=xt[:], in_=src_rows)], writes=[xt.b])
            S.op("act", lambda e: e.activation(out=junk[:], in_=xt[:], func=AF.Square, accum_out=st[:, 0:1]),
                 reads=[xt.b], writes=[junk.b, st.b])
            rstd_of(st, st[:, 0:1], D)
            S.op("dve", lambda e: e.tensor_scalar(out=xn[:], in0=xt[:], scalar1=st[:, 2:3], scalar2=None, op0=ALU.mult), reads=[xt.b, st.b], writes=[xn.b])
            bk = bank()

            def tr(e):
                v = bfv(bk)
                for k in range(KT):
                    r = e.transpose(out=v[:, k * 128:(k + 1) * 128], in_=xn[:, k * 128:(k + 1) * 128], identity=ident[:])
                return r
            S.op("pe", tr, reads=[xn.b, ident.b], writes=[bk.b])
            S.op("dve", lambda e: e.tensor_tensor(out=dst_ap, in0=bfv(bk).rearrange("p (k t) -> p k t", k=KT),
                                                  in1=vec8[:, gidx, :].unsqueeze(2).to_broadcast([128, KT, 128]), op=ALU.mult),
                 reads=[bk.b, vec8.b], writes=[dst.b])

        for i in range(SLAB // 128):
            norm_tile(xs[i * 128:(i + 1) * 128, :], hnT, hnT[:, :, i * 128:(i + 1) * 128], 3)

        def dt_mm(lhs_fn, lhs_reads, dst, dst_ap):
            bk = bank()

            def f(e):
                for k in range(KT):
                    r = e.matmul(bk[:, 0:64], lhsT=lhs_fn(k), rhs=wdt[:, k, :], start=(k == 0), stop=(k == KT - 1))
                return r
            S.op("pe", f, reads=lhs_reads + [wdt.b], writes=[bk.b])
            S.op("dve", lambda e: e.tensor_tensor(out=dst_ap, in0=bk[:, 0:64], in1=rows[:, 0:64], op=ALU.add),
                 reads=[bk.b, rows.b], writes=[dst.b])

        for c in range(NCH):
            dt_mm((lambda c: lambda k: hnT[:, k, (c + 1) * 128:(c + 2) * 128])(c), [hnT.b], dtraw_o, dtraw_o[:, c, :])
        dump("hnT", hnT[:], [128, KT, SLAB], BF16, [hnT.b])
        dump("dtraw_o", dtraw_o[:], [128, NCH, 64], F32, [dtraw_o.b])

        for i in range(NOT + 1):
            h = hnt2[i % 4]
            norm_tile(xo[i * 128:(i + 1) * 128, :], h, h[:], 3)
            if i < NOT:
                dt_mm((lambda h: lambda k: h[:, k, :])(h), [h.b], dtraw_s, dtraw_s[:, i, :])
            S.dma("pool", "hns%d" % (i % 4), (lambda h, i: lambda e: [e.dma_start(out=hns[i], in_=h[:].rearrange("p k t -> p (k t)"))])(h, i),
                  reads=[h.b], writes=[Buf("hns_dram%d" % i)])
        hns_b = Buf("hns_all")
        S.barrier()
        p0.close()
        dump("dtraw_s", dtraw_s[:], [128, NOT, 64], F32, [dtraw_s.b])
        if upto <= 0:
            return finish(nc, S, dbg_out)

        ps1a = ExitStack()
        tB = sb("tB", [128, NOT, 64], F32, ps1a)
        tC = sb("tC", [128, NOT, 64], F32, ps1a)
        tW = sb("tW", [128, NOT, 64], F32, ps1a)
        adts = sb("adts", [128, NOT, 64], BF16, ps1a)
        tA = dtraw_s

        def softplus_inplace(t, n):
            S.op("act", lambda e: e.activation(out=t[:], in_=t[:], func=AF.Exp), reads=[t.b], writes=[t.b])
            S.op("act", lambda e: e.activation(out=t[:], in_=t[:], func=AF.Ln, bias=onec[:, 0:1], scale=1.0), reads=[t.b, onec.b], writes=[t.b])

        def cums(adt, n, tinc, ttot):
            flat = adt[:].rearrange("p c j -> p (c j)")
            fi = tinc[:].rearrange("p c j -> p (c j)")
            ft = ttot[:].rearrange("p c j -> p (c j)")
            tot = n * 64
            for c0 in range(0, tot, 512):
                w = min(512, tot - c0)
                b1 = bank()
                S.op("pe", (lambda b1, c0, w: lambda e: e.matmul(b1[:, 0:w], lhsT=uinc[:], rhs=flat[:, c0:c0 + w], start=True, stop=True))(b1, c0, w),
                     reads=[adt.b, uinc.b], writes=[b1.b])
                S.op("act", (lambda b1, c0, w: lambda e: e.activation(out=fi[:, c0:c0 + w], in_=b1[:, 0:w], func=AF.Copy))(b1, c0, w),
                     reads=[b1.b], writes=[tinc.b])
                b2 = bank()
                S.op("pe", (lambda b2, c0, w: lambda e: e.matmul(b2[:, 0:w], lhsT=ones[:], rhs=flat[:, c0:c0 + w], start=True, stop=True))(b2, c0, w),
                     reads=[adt.b, ones.b], writes=[b2.b])
                S.op("dve", (lambda b2, c0, w: lambda e: e.tensor_copy(out=ft[:, c0:c0 + w], in_=b2[:, 0:w]))(b2, c0, w),
                     reads=[b2.b], writes=[ttot.b])

        softplus_inplace(tA, NOT)
        S.op("dve", lambda e: e.tensor_scalar(out=tA[:, 0, :], in0=tA[:, 0, :], scalar1=hmask[:, 0:1], scalar2=None, op0=ALU.mult),
             reads=[tA.b, hmask.b], writes=[tA.b])
        for d in range(2):
            S.op("dve", (lambda d: lambda e: e.tensor_tensor(out=tA[:, :, d * 32:(d + 1) * 32], in0=tA[:, :, d * 32:(d + 1) * 32],
                                                               in1=mrow[:, d * NOT:(d + 1) * NOT].unsqueeze(2).to_broadcast([128, NOT, 32]), op=ALU.mult))(d),
                 reads=[tA.b, mrow.b], writes=[tA.b])
        S.op("dve", lambda e: e.tensor_tensor(out=adts[:], in0=tA[:], in1=abc[:].unsqueeze(1).to_broadcast([128, NOT, 64]), op=ALU.mult),
             reads=[tA.b, abc.b], writes=[adts.b])
        cums(adts, NOT, tB, tC)
        S.op("dve", lambda e: e.tensor_tensor(out=tB[:, :, 0:32], in0=tC[:, :, 0:32], in1=tB[:, :, 0:32], op=ALU.subtract), reads=[tB.b, tC.b], writes=[tB.b])
        S.op("dve", lambda e: e.tensor_tensor(out=tB[:, :, 32:64], in0=tB[:, :, 32:64], in1=adts[:, :, 32:64], op=ALU.subtract), reads=[tB.b, adts.b], writes=[tB.b])
        S.op("dve", lambda e: e.memset(tW[:, NOT - 1, 0:32], 0.0), writes=[tW.b])
        S.op("dve", lambda e: e.memset(tW[:, 0, 32:64], 0.0), writes=[tW.b])
        for c in range(NOT - 2, -1, -1):
            S.op("dve", (lambda c: lambda e: e.tensor_tensor(out=tW[:, c, 0:32], in0=tW[:, c + 1, 0:32], in1=tC[:, c + 1, 0:32], op=ALU.add))(c),
                 reads=[tW.b, tC.b], writes=[tW.b])
        for c in range(1, NOT):
            S.op("pool", (lambda c: lambda e: e.tensor_tensor(out=tW[:, c, 32:64], in0=tW[:, c - 1, 32:64], in1=tC[:, c - 1, 32:64], op=ALU.add))(c),
                 reads=[tW.b, tC.b], writes=[tW.b])
        S.op("dve", lambda e: e.tensor_tensor(out=tB[:], in0=tB[:], in1=tW[:], op=ALU.add), reads=[tB.b, tW.b], writes=[tB.b])
        S.op("act", lambda e: e.activation(out=tB[:], in_=tB[:], func=AF.Exp), reads=[tB.b], writes=[tB.b])
        S.op("dve", lambda e: e.tensor_tensor(out=tA[:], in0=tA[:], in1=tB[:], op=ALU.mult), reads=[tA.b, tB.b], writes=[tA.b])
        mult_s = tA
        dump("mult_s", mult_s[:], [128, NOT, 64], F32, [mult_s.b])
        S.barrier()
        ps1a.close()

        bank_rr["n"] = 6
        accS = [banks[6], banks[7]]
        wsg_2 = [sb(f"wsg{i}", [128, KT, 640], BF16, ps1) for i in range(2)]
        dgs_2 = [sb(f"dgs{i}", [128, 5, 7, 128], BF16, ps1) for i in range(2)]
        xh = sb("xh", [128, 5, 128], BF16, ps1)
        hnb2 = [sb(f"hnb{i}", [128, 4, D], BF16, ps1) for i in range(2)]
        hnh = sb("hnh", [128, D], BF16, ps1)
        xpre_2 = [sb(f"xpre{i}", [128, 5, 520], BF16, ps1) for i in range(2)]
        xcb_2 = [sb(f"xcb{i}", [128, 5, 512], BF16, ps1) for i in range(2)]
        xpre_mb = [[Buf(f"xpm{i}_{m}") for m in range(5)] for i in range(2)]
        xpre_hb = [[Buf(f"xph{i}_{m}") for m in range(5)] for i in range(2)]
        xcb_mb = [[Buf(f"xcm{i}_{m}") for m in range(5)] for i in range(2)]
        xbt_2 = [[sb(f"xbt{j}_{i}", [128, 640], BF16, ps1) for i in range(4)] for j in range(2)]
        xdd_2 = [[[sb(f"xdd{j}_{i}_{d}", [128, 512], BF16, ps1) for d in range(2)] for i in range(4)] for j in range(2)]
        S.dma("sp", "hnh", lambda e: [e.dma_start(out=hnh[:], in_=hns[NOT])], reads=[hns_b], writes=[hnh.b])

        def build_diag(dg, dg_ap, tile, ntap, wt, eng):
            for tap in range(ntap):
                S.op(eng, (lambda tap: lambda e: e.tensor_scalar(out=dg_ap[:, tap, :], in0=ident[:], scalar1=wt[:, tile, tap:tap + 1], scalar2=None, op0=ALU.mult))(tap),
                     reads=[ident.b, wt.b], writes=[dg.b])

        nblk = 13
        def s2_prefetch(g):
            wsg_, dgs_ = wsg_2[g % 2], dgs_2[g % 2]
            wload(wsg_, wsg_[:, :, 0:512], w_in, O_X + g * 512, 512)
            wload(wsg_, wsg_[:, :, 512:640], w_in, O_B + g * 128, 128)
            tl_ = [g * 4 + m for m in range(4)] + [16 + g]
            for m in range(5):
                build_diag(dgs_, dgs_[:, m], tl_[m], 7, scw, "dve")

        s2_prefetch(0)
        for g in range(4):
            wsg, dgs = wsg_2[g % 2], dgs_2[g % 2]
            tiles = [g * 4 + m for m in range(4)] + [16 + g]
            for m in range(5):
                bk = bank()

                def f(e, bk=bk, m=m):
                    for k in range(KT):
                        r = e.matmul(bk[:, 0:128], lhsT=wsg[:, k, m * 128:(m + 1) * 128], rhs=hnh[:, k * 128:(k + 1) * 128], start=(k == 0), stop=(k == KT - 1))
                    return r
                S.op("pe", f, reads=[wsg.b, hnh.b], writes=[bk.b])
                S.op("act", (lambda bk, m: lambda e: e.activation(out=xh[:, m, :], in_=bk[:, 0:128], func=AF.Copy))(bk, m), reads=[bk.b], writes=[xh.b])
            def blkinfo(blk):
                NT = 1 if blk == 0 else 4
                t0 = 0 if blk == 0 else 1 + 4 * (blk - 1)
                return NT, t0, NT * 128

            def stageA(blk):
                NT, t0, N = blkinfo(blk)
                pc_issue(2)
                hb = hnb2[blk % 2]
                xpre, xcb = xpre_2[blk % 2], xcb_2[blk % 2]
                xpm, xph, xcm = xpre_mb[blk % 2], xpre_hb[blk % 2], xcb_mb[blk % 2]
                S.dma("sp", "hnb%d" % (blk % 2), lambda e: [e.dma_start(out=hb[:, 0:NT, :], in_=hns[t0:t0 + NT].rearrange("t p f -> p t f"))],
                      reads=[hns_b], writes=[hb.b])
                hbv = hb[:].rearrange("p t (k c) -> p t k c", k=KT)
                for m in range(5):
                    bk = bank()

                    def f(e, bk=bk, m=m):
                        for k in range(KT):
                            r = e.matmul(bk[:, 0:N], lhsT=wsg[:, k, m * 128:(m + 1) * 128], rhs=hbv[:, 0:NT, k, :], start=(k == 0), stop=(k == KT - 1))
                        return r
                    S.op("pe", f, reads=[wsg.b, hb.b], writes=[bk.b])
                    S.op("act", (lambda bk, m: lambda e: e.activation(out=xpre[:, m, 3:3 + N], in_=bk[:, 0:N], func=AF.Copy))(bk, m),
                         reads=[bk.b], writes=[xpm[m]])
                    S.op("act", (lambda m: lambda e: e.activation(out=xpre[:, m, 0:3], in_=xh[:, m, blk * 6:blk * 6 + 3], func=AF.Copy))(m), reads=[xh.b], writes=[xph[m]])
                    S.op("act", (lambda m: lambda e: e.activation(out=xpre[:, m, 3 + N:6 + N], in_=xh[:, m, blk * 6 + 3:blk * 6 + 6], func=AF.Copy))(m), reads=[xh.b], writes=[xph[m]])
                for m in range(5):
                    bk = bank()

                    def f(e, bk=bk, m=m):
                        for tap in range(7):
                            r = e.matmul(bk[:, 0:N], lhsT=dgs[:, m, tap, :], rhs=xpre[:, m, tap:tap + N], start=(tap == 0), stop=(tap == 6))
                        return r
                    S.op("pe", f, reads=[dgs.b, xpm[m], xph[m]], writes=[bk.b])
                    S.op("act", (lambda bk, m, tl: lambda e: e.activation(out=xcb[:, m, 0:N], in_=bk[:, 0:N], func=AF.Silu, bias=scb[:, tl:tl + 1], scale=1.0))(bk, m, tiles[m]),
                         reads=[bk.b, scb.b], writes=[xcm[m]])

            def stageB(blk):
                NT, t0, N = blkinfo(blk)
                xcb, xbt, xdd = xcb_2[blk % 2], xbt_2[blk % 2], xdd_2[blk % 2]
                xcm = xcb_mb[blk % 2]
                for t in range(NT):
                    c = t0 + t
                    bk = bank()

                    def f(e, bk=bk, t=t):
                        v = bfv(bk)
                        for m in range(5):
                            r = e.transpose(out=v[:, m * 128:(m + 1) * 128], in_=xcb[:, m, t * 128:(t + 1) * 128], identity=ident[:])
                        return r
                    S.op("pe", f, reads=xcm + [ident.b], writes=[bk.b])
                    S.op("dve", (lambda bk, t: lambda e: e.tensor_copy(out=xbt[t][:], in_=bfv(bk)[:, 0:640]))(bk, t), reads=[bk.b], writes=[xbt[t].b])
                    for d in range(2):
                        S.op("pool" if d == 0 else "dve",
                             (lambda t, d, c: lambda e: e.tensor_tensor(out=xdd[t][d][:].rearrange("p (h q) -> p h q", h=8),
                                                                        in0=xbt[t][:, 0:512].rearrange("p (h q) -> p h q", h=8),
                                                                        in1=mult_s[:, c, d * 32 + g * 8:d * 32 + g * 8 + 8].unsqueeze(2).to_broadcast([128, 8, 64]), op=ALU.mult))(t, d, c),
                             reads=[xbt[t].b, mult_s.b], writes=[xdd[t][d].b])

            def stageB2(blk):
                NT, t0, N = blkinfo(blk)
                xbt, xdd = xbt_2[blk % 2], xdd_2[blk % 2]
                for d in range(2):
                    def f(e, d=d):
                        for t in range(NT):
                            r = e.matmul(accS[d][:, :], lhsT=xbt[t][:, 512:640], rhs=xdd[t][d][:], start=(blk == 0 and t == 0), stop=(blk == nblk - 1 and t == NT - 1))
                        return r
                    S.op("pe", f, reads=[xbt[t].b for t in range(NT)] + [xdd[t][d].b for t in range(NT)], writes=[accS[d].b])

            stageA(0)
            for blk in range(nblk):
                stageB(blk)
                if blk + 1 < nblk:
                    stageA(blk + 1)
                if blk == 2 and g + 1 < 4:
                    s2_prefetch(g + 1)
                stageB2(blk)
            for d in range(2):
                S.op("act" if d == 0 else "dve",
                     (lambda d: (lambda e: e.activation(out=sin[:, d, g * 512:(g + 1) * 512], in_=accS[d][:, :], func=AF.Copy)) if d == 0 else
                      (lambda e: e.tensor_copy(out=sin[:, d, g * 512:(g + 1) * 512], in_=accS[d][:, :])))(d),
                     reads=[accS[d].b], writes=[sin.b])
        bank_rr["n"] = 8
        pc_issue(1000)
        S.barrier()
        ps1.close()
        dump("sin", sin[:], [128, 2, 2048], F32, [sin.b])
        if upto <= 1:
            return finish(nc, S, dbg_out)


        pm = ExitStack()
        t_dt = dtraw_o
        adto = sb("adto", [128, NCH, 64], BF16, pm)
        t_ai = sb("t_ai", [128, NCH, 64], F32, pm)
        t_tot = sb("t_tot", [128, NCH, 64], F32, pm)
        nadto = sb("nadto", [128, NCH, 64], BF16, pm)
        t_E = sb("t_E", [128, NCH, 32], F32, pm)
        t_dtd = sb("t_dtd", [128, NCH, 64], F32, pm)
        t_sc = sb("t_sc", [128, NCH, 64], F32, pm)
        softplus_inplace(t_dt, NCH)
        S.op("dve", lambda e: e.tensor_tensor(out=adto[:], in0=t_dt[:], in1=abc[:].unsqueeze(1).to_broadcast([128, NCH, 64]), op=ALU.mult),
             reads=[t_dt.b, abc.b], writes=[adto.b])
        cums(adto, NCH, t_ai, t_tot)
        F_, B_ = slice(0, 32), slice(32, 64)
        S.op("dve", lambda e: e.tensor_scalar(out=nadto[:], in0=adto[:], scalar1=-1.0, scalar2=None, op0=ALU.mult), reads=[adto.b], writes=[nadto.b])
        S.op("dve", lambda e: e.tensor_tensor(out=t_E[:], in0=t_ai[:, :, B_], in1=adto[:, :, B_], op=ALU.subtract), reads=[t_ai.b, adto.b], writes=[t_E.b])
        S.op("dve", lambda e: e.tensor_tensor(out=t_dtd[:, :, F_], in0=t_tot[:, :, F_], in1=t_ai[:, :, F_], op=ALU.subtract), reads=[t_ai.b, t_tot.b], writes=[t_dtd.b])
        S.op("dve", lambda e: e.tensor_copy(out=t_dtd[:, :, B_], in_=t_E[:]), reads=[t_E.b], writes=[t_dtd.b])
        S.op("act", lambda e: e.activation(out=t_dtd[:], in_=t_dtd[:], func=AF.Exp), reads=[t_dtd.b], writes=[t_dtd.b])
        S.op("dve", lambda e: e.tensor_tensor(out=t_dtd[:], in0=t_dtd[:], in1=t_dt[:], op=ALU.mult), reads=[t_dtd.b, t_dt.b], writes=[t_dtd.b])
        S.op("dve", lambda e: e.tensor_copy(out=t_sc[:, :, F_], in_=t_ai[:, :, F_]), reads=[t_ai.b], writes=[t_sc.b])
        S.op("dve", lambda e: e.tensor_tensor(out=t_sc[:, :, B_], in0=t_tot[:, :, B_], in1=t_E[:], op=ALU.subtract), reads=[t_tot.b, t_E.b], writes=[t_sc.b])
        S.op("act", lambda e: e.activation(out=t_sc[:], in_=t_sc[:], func=AF.Exp), reads=[t_sc.b], writes=[t_sc.b])
        S.op("act", lambda e: e.activation(out=t_tot[:], in_=t_tot[:], func=AF.Exp), reads=[t_tot.b], writes=[t_tot.b])
        t_et = t_tot

        wmg = sb("wmg", [128, KT, 768], BF16, pm)
        dgm2 = [sb(f"dgm{i}", [128, 7, 128], BF16, pm) for i in range(2)]
        xprem = sb("xprem", [128, 2056], BF16, pm)
        xc = sb("xc", [128, 6, SEG], BF16, pm)
        dpar = lambda name, shape, dt: [[sb(f"{name}{d}_{i}", shape, dt, pm) for i in range(2)] for d in range(2)]
        xbt2 = dpar("mxbt", [128, 640], BF16)
        L2 = dpar("Lb", [128, 8, 128], BF16)
        M2 = dpar("Mb", [128, 8, 128], BF16)
        xdt2 = dpar("xdt", [128, 512], BF16)
        xdd2 = dpar("mxdd", [128, 512], BF16)
        yfb2 = dpar("yfb", [128, 512], F32)
        Sbf2 = dpar("Sbf", [128, 512], BF16)
        xD2 = [sb(f"xD{i}", [128, 512], BF16, pm) for i in range(2)]
        t1_2 = [sb(f"t1_{d}", [128, 512], F32, pm) for d in range(2)]
        Sst = [sb(f"Sst{d}", [128, 512], F32, pm) for d in range(2)]
        ydram = Buf("yscr")

        def h8(ap):
            return ap.rearrange("p (h q) -> p h q", h=8)

        xc_2 = [xc, sb("xcB", [128, 6, SEG], BF16, pm)]

        def prologue_jobs(g):
            xc_t = xc_2[g % 2]
            tiles = [g * 4 + m for m in range(4)] + [16 + g, 20 + g]
            jobs = []

            def jw():
                wload(wmg, wmg[:, :, 0:512], w_in, O_X + g * 512, 512)
                wload(wmg, wmg[:, :, 512:640], w_in, O_B + g * 128, 128)
                wload(wmg, wmg[:, :, 640:768], w_in, O_C + g * 128, 128)
            jobs.append(jw)
            for m in range(6):
                dg = dgm2[m % 2]
                jobs.append((lambda dg, m: lambda: build_diag(dg, dg[:], tiles[m], 7, scw, "dve"))(dg, m))
                for j in range(5):
                    def ji(m=m, j=j):
                        s0 = 124 + 512 * j
                        w = min(512, 2180 - s0)
                        bk = bank()

                        def f(e):
                            for k in range(KT):
                                r = e.matmul(bk[:, 0:w], lhsT=wmg[:, k, m * 128:(m + 1) * 128], rhs=hnT[:, k, s0:s0 + w], start=(k == 0), stop=(k == KT - 1))
                            return r
                        S.op("pe", f, reads=[wmg.b, hnT.b], writes=[bk.b])
                        S.op("act", lambda e: e.activation(out=xprem[:, 512 * j:512 * j + w], in_=bk[:, 0:w], func=AF.Copy), reads=[bk.b], writes=[xprem.b])
                    jobs.append(ji)
                for q in range(4):
                    def jc(m=m, q=q, dg=dg):
                        bk = bank()

                        def f(e):
                            for tap in range(7):
                                r = e.matmul(bk[:, :], lhsT=dg[:, tap, :], rhs=xprem[:, 1 + 512 * q + tap:1 + 512 * q + tap + 512], start=(tap == 0), stop=(tap == 6))
                            return r
                        S.op("pe", f, reads=[dg.b, xprem.b], writes=[bk.b])
                        tl = tiles[m]
                        S.op("act", lambda e: e.activation(out=xc_t[:, m, q * 512:(q + 1) * 512], in_=bk[:, :], func=AF.Silu, bias=scb[:, tl:tl + 1], scale=1.0),
                             reads=[bk.b, scb.b], writes=[xc_t.b])
                    jobs.append(jc)
            return jobs

        next_jobs = prologue_jobs(0)
        for g in range(4):
            for jb in next_jobs:
                jb()
            next_jobs = prologue_jobs(g + 1) if g + 1 < 4 else []
            xc = xc_2[g % 2]
            if g == 0:
                dump("xc0", xc[:], [128, 6, SEG], BF16, [xc.b])

            sbi = [0, 0]
            for d in range(2):
                S.op("dve", (lambda d: lambda e: e.tensor_copy(out=Sst[d][:], in_=sin[:, d, g * 512:(g + 1) * 512]))(d), reads=[sin.b], writes=[Sst[d].b])
                S.op("act", (lambda d: lambda e: e.activation(out=Sbf2[d][0][:], in_=Sst[d][:], func=AF.Copy))(d), reads=[Sst[d].b], writes=[Sbf2[d][0].b])

            def stage1(d, it):
                c = it if d == 0 else NCH - 1 - it
                par = it % 2
                j0 = d * 32 + g * 8
                umat = uinc if d == 0 else nustr
                neg = negf if d == 0 else negb
                cs = slice(c * 128, (c + 1) * 128)
                xbt, Lb, Mb, xdt, xdd_ = xbt2[d][par], L2[d][par], M2[d][par], xdt2[d][par], xdd2[d][par]
                xD = xD2[par]
                bk = bank()

                def f(e):
                    v = bfv(bk)
                    for m in range(5):
                        e.transpose(out=v[:, m * 128:(m + 1) * 128], in_=xc[:, m, cs], identity=ident[:])
                    return e.matmul(bk[:, 320:448], lhsT=xc[:, 4, cs], rhs=xc[:, 5, cs], start=True, stop=True)
                S.op("pe", f, reads=[xc.b, ident.b], writes=[bk.b])
                S.op("act", lambda e: e.activation(out=xbt[:], in_=bfv(bk)[:, 0:640], func=AF.Copy), reads=[bk.b], writes=[xbt.b])
                for half in range(2):
                    bkl = bank()

                    def f(e, bkl=bkl, half=half):
                        jj = j0 + 4 * half
                        e.matmul(bkl[:, :], lhsT=umat[:], rhs=nadto[:, c, jj:jj + 4].unsqueeze(2).to_broadcast([128, 4, 128]), start=True, stop=False)
                        e.matmul(bkl[:, :], lhsT=ident[:], rhs=neg[:], start=False, stop=False)
                        for hh in range(4):
                            r = e.matmul(bkl[:, hh * 128:(hh + 1) * 128], lhsT=adto[:, c, jj + hh:jj + hh + 1].to_broadcast([128, 128]), rhs=umat[:], start=False, stop=(hh == 3))
                        return r
                    S.op("pe", f, reads=[umat.b, ident.b, neg.b, adto.b, nadto.b], writes=[bkl.b])
                    S.op("act", (lambda bkl, half: lambda e: e.activation(out=Lb[:, half * 4:(half + 1) * 4, :], in_=bkl[:, :].rearrange("p (h l) -> p h l", h=4), func=AF.Exp))(bkl, half),
                         reads=[bkl.b], writes=[Lb.b])
                S.op("dve", lambda e: e.tensor_tensor(out=Mb[:], in0=Lb[:], in1=bk[:, 320:448].unsqueeze(1).to_broadcast([128, 8, 128]), op=ALU.mult),
                     reads=[bk.b, Lb.b], writes=[Mb.b])
                S.op("pool", lambda e: e.tensor_tensor(out=h8(xdt[:]), in0=h8(xbt[:, 0:512]), in1=t_dt[:, c, j0:j0 + 8].unsqueeze(2).to_broadcast([128, 8, 64]), op=ALU.mult),
                     reads=[xbt.b, t_dt.b], writes=[xdt.b])
                S.op("pool", lambda e: e.tensor_tensor(out=h8(xdd_[:]), in0=h8(xbt[:, 0:512]), in1=t_dtd[:, c, j0:j0 + 8].unsqueeze(2).to_broadcast([128, 8, 64]), op=ALU.mult),
                     reads=[xbt.b, t_dtd.b], writes=[xdd_.b])
                if d == 0:
                    S.op("pool", lambda e: e.tensor_tensor(out=h8(xD[:]), in0=h8(xbt[:, 0:512]), in1=dsk[:, g * 8:g * 8 + 8].unsqueeze(2).to_broadcast([128, 8, 64]), op=ALU.mult),
                         reads=[xbt.b, dsk.b], writes=[xD.b])

            def stage2(d, it):
                c = it if d == 0 else NCH - 1 - it
                par = it % 2
                j0 = d * 32 + g * 8
                cs = slice(c * 128, (c + 1) * 128)
                xbt, Mb, xdt, xdd_ = xbt2[d][par], M2[d][par], xdt2[d][par], xdd2[d][par]
                xD = xD2[par]
                Sd, t1 = Sst[d], t1_2[d]
                bks = bank()
                S.op("pe", lambda e: e.matmul(bks[:, :], lhsT=xbt[:, 512:640], rhs=xdd_[:], start=True, stop=True), reads=[xbt.b, xdd_.b], writes=[bks.b])
                bko = bank()
                Sb_cur = Sbf2[d][sbi[d]]
                S.op("pe", lambda e: e.matmul(bko[:, :], lhsT=xc[:, 5, cs], rhs=Sb_cur[:], start=True, stop=True), reads=[xc.b, Sb_cur.b], writes=[bko.b])
                bky = bank()
                if d == 0:
                    def f(e):
                        e.matmul(bky[:, :], lhsT=ident[:], rhs=xD[:], start=True, stop=False)
                        for h in range(8):
                            r = e.matmul(bky[:, h * 64:(h + 1) * 64], lhsT=Mb[:, h, :], rhs=xdt[:, h * 64:(h + 1) * 64], start=False, stop=(h == 7))
                        return r
                    S.op("pe", f, reads=[ident.b, Mb.b, xdt.b, xD.b], writes=[bky.b])
                else:
                    def f(e):
                        for h in range(8):
                            r = e.matmul(bky[:, h * 64:(h + 1) * 64], lhsT=Mb[:, h, :], rhs=xdt[:, h * 64:(h + 1) * 64], start=True, stop=True)
                        return r
                    S.op("pe", f, reads=[Mb.b, xdt.b], writes=[bky.b])
                if it + 1 < NCH:
                    S.op("dve", lambda e: e.tensor_tensor(out=h8(Sd[:]), in0=h8(Sd[:]), in1=t_et[:, c, j0:j0 + 8].unsqueeze(2).to_broadcast([128, 8, 64]), op=ALU.mult),
                         reads=[Sd.b, t_et.b], writes=[Sd.b])
                    S.op("dve", lambda e: e.tensor_tensor(out=Sd[:], in0=Sd[:], in1=bks[:, :], op=ALU.add), reads=[Sd.b, bks.b], writes=[Sd.b])
                    sbi[d] = 1 - sbi[d]
                    nxt = Sbf2[d][sbi[d]]
                    S.op("act", lambda e: e.activation(out=nxt[:], in_=Sd[:], func=AF.Copy), reads=[Sd.b], writes=[nxt.b])
                S.op("dve", lambda e: e.tensor_tensor(out=h8(t1[:]), in0=h8(bko[:, :]), in1=t_sc[:, c, j0:j0 + 8].unsqueeze(2).to_broadcast([128, 8, 64]), op=ALU.mult),
                     reads=[bko.b, t_sc.b], writes=[t1.b])
                yfb = yfb2[d][par]
                S.op("dve", lambda e: e.tensor_tensor(out=yfb[:], in0=t1[:], in1=bky[:, :], op=ALU.add), reads=[t1.b, bky.b], writes=[yfb.b])
                if g == 0 and c in (15, 0):
                    dump("y%d_%d" % (d, c), yfb[:], [128, 512], F32, [yfb.b])
                S.dma("sp", "yout%d_%d" % (d, par), lambda e: [e.dma_start(out=yscr[d, g, c], in_=yfb[:])], reads=[yfb.b], writes=[ydram])

            stage1(0, 0)
            stage1(1, 0)
            for it in range(NCH):
                if it + 1 < NCH:
                    stage1(0, it + 1)
                    stage1(1, it + 1)
                stage2(0, it)
                for _ in range(2):
                    if next_jobs:
                        next_jobs.pop(0)()
                stage2(1, it)
                for _ in range(2):
                    if next_jobs:
                        next_jobs.pop(0)()
        S.barrier()
        pm.close()
        pA.close()
        if upto <= 2:
            return finish(nc, S, dbg_out)


        cvscr = nc.dram_tensor("cvscr", [8, 128, SEG], BF16).ap()
        cvdram = Buf("cvscr")
        pc = ExitStack()
        cdw = sb("cdw", [128, 8, 31], F32, pc)
        dgc2 = [sb(f"dgc{i}", [128, 31, 128], BF16, pc) for i in range(2)]
        wc2 = [sb(f"wc{i}", [128, KT, 256], BF16, pc) for i in range(2)]
        abuf2 = [sb(f"abuf{i}", [128, 2080], BF16, pc) for i in range(2)]
        sgc = sb("sgc", [128, 512], F32, pc)
        cvt2 = [sb(f"cvo{i}", [128, 512], BF16, pc) for i in range(2)]
        S.dma("sp", "cdw", lambda e: [e.dma_start(out=cdw[:], in_=cdw_d)], writes=[cdw.b])
        for ct in range(8):
            wc, dgc, abuf = wc2[ct % 2], dgc2[ct % 2], abuf2[ct % 2]
            wload(wc, wc[:, :, 0:128], w_in, O_VAL + ct * 128, 128)
            wload(wc, wc[:, :, 128:256], w_in, O_GATE + ct * 128, 128)
            for tap in range(31):
                S.op("dve",
                     (lambda tap: lambda e: e.tensor_scalar(out=dgc[:, tap, :], in0=ident[:], scalar1=cdw[:, ct, tap:tap + 1], scalar2=None, op0=ALU.mult))(tap),
                     reads=[ident.b, cdw.b], writes=[dgc.b])
            for j in range(5):
                s0 = 112 + 512 * j
                w = min(512, 2192 - s0)
                bv, bg = bank(), bank()
                for (bk_, off) in ((bv, 0), (bg, 128)):
                    def f(e, bk_=bk_, off=off, s0=s0, w=w):
                        for k in range(KT):
                            r = e.matmul(bk_[:, 0:w], lhsT=wc[:, k, off:off + 128], rhs=hnT[:, k, s0:s0 + w], start=(k == 0), stop=(k == KT - 1))
                        return r
                    S.op("pe", f, reads=[wc.b, hnT.b], writes=[bk_.b])
                S.op("act", (lambda bg, w: lambda e: e.activation(out=sgc[:, 0:w], in_=bg[:, 0:w], func=AF.Sigmoid))(bg, w), reads=[bg.b], writes=[sgc.b])
                S.op("dve", (lambda bv, j, w: lambda e: e.tensor_tensor(out=abuf[:, 512 * j:512 * j + w], in0=bv[:, 0:w], in1=sgc[:, 0:w], op=ALU.mult))(bv, j, w),
                     reads=[bv.b, sgc.b], writes=[abuf.b])
            for q in range(4):
                bk = bank()

                def f(e, bk=bk, q=q):
                    for tap in range(31):
                        r = e.matmul(bk[:, :], lhsT=dgc[:, tap, :], rhs=abuf[:, 1 + 512 * q + tap:1 + 512 * q + tap + 512], start=(tap == 0), stop=(tap == 30))
                    return r
                S.op("pe", f, reads=[dgc.b, abuf.b], writes=[bk.b])
                cvt = cvt2[q % 2]
                S.op("act", (lambda bk, cvt: lambda e: e.activation(out=cvt[:], in_=bk[:, :], func=AF.Identity, bias=vec8[:, 0, ct:ct + 1], scale=1.0))(bk, cvt),
                     reads=[bk.b, vec8.b], writes=[cvt.b])
                S.dma("sp", "cvo%d" % (q % 2), (lambda cvt, q: lambda e: [e.dma_start(out=cvscr[ct, :, q * 512:(q + 1) * 512], in_=cvt[:])])(cvt, q), reads=[cvt.b], writes=[cvdram])
        S.barrier()
        pc.close()
        if upto <= 3:
            return finish(nc, S, dbg_out)

        pb = ExitStack()
        hs2 = sb("hs2", [128, 4, D], F32, pb)
        wb2 = [sb(f"wb{i}", [128, 4096], BF16, pb) for i in range(3)]
        wbc = {"i": 0}

        def wchunk(W, c0, ncols, kt=KT, r0=0):
            wb = wb2[wbc["i"] % 3]
            wbc["i"] += 1
            view = wb[:, 0:kt * ncols].rearrange("p (k c) -> p k c", k=kt)
            Wd_, Wb_ = W
            src = Wd_[r0:r0 + kt * 128, c0:c0 + ncols].rearrange("(k p) c -> p k c", p=128)
            S.dma("sp", "w_" + wb.b.name, lambda e: [e.dma_start(out=view, in_=src)], reads=[Wb_], writes=[wb.b])
            return wb, view

        for T_ in range(4):
            tok0 = 512 * T_
            sc0 = 128 + tok0
            tcol = slice(sc0, sc0 + 512)
            pmix = ExitStack()
            cvt = sb("cvt", [128, 8, 512], BF16, pmix)
            aact = sb("aact", [128, 8, 512], BF16, pmix)
            mc = sb("mc", [128, 8, 512], F32, pmix)
            vT = sb("vT", [128, 16, 512], BF16, pmix)
            merged = sb("merged", [128, 8, 512], BF16, pmix)
            lnm = sb("lnm", [128, 512], F32, pmix)
            lnr = sb("lnr", [128, 512], F32, pmix)
            lt = sb("lt", [128, 512], F32, pmix)
            lt2 = sb("lt2", [128, 512], F32, pmix)
            sgb = sb("sgb", [128, 512], F32, pmix)
            sq2 = [sb(f"sq{i}", [128, 512], BF16, pmix) for i in range(2)]
            S.dma("sp", "cvt", lambda e: [e.dma_start(out=cvt[:], in_=cvscr[:, :, tok0:tok0 + 512].rearrange("c p t -> p c t"))], reads=[cvdram], writes=[cvt.b])
            NBUF = 4
            normw = sb("normw", [128, 2048], F32, pmix)
            S.dma("sp", "normw", lambda e: [e.dma_start(out=normw[:], in_=normw_d.partition_broadcast(128))], writes=[normw.b])
            yl2 = [sb(f"yl{i}", [128, 2, 512], F32, pmix) for i in range(NBUF)]
            szb2 = [sb(f"szb{i}", [128, 512], F32, pmix) for i in range(NBUF)]
            vb2 = [sb(f"vb{i}", [128, 512], F32, pmix) for i in range(NBUF)]
            vnb2 = [sb(f"vnb{i}", [128, 512], BF16, pmix) for i in range(NBUF)]
            vjunk = sb("vjunk", [128, 512], BF16, pmix)
            vst2 = [sb(f"vst{i}", [128, 4], F32, pmix) for i in range(NBUF)]
            wz_cache = {}
            wzb2 = [sb(f"wzb{i}", [128, 4096], BF16, pmix) for i in range(2)]

            def gateZ(u):
                g, j = u // 4, u % 4
                if g not in wz_cache:
                    wzb = wzb2[g % 2]
                    wzview = wzb[:].rearrange("p (k c) -> p k c", k=KT)
                    S.dma("sp", "w_" + wzb.b.name, (lambda wzview, g: lambda e: [e.dma_start(out=wzview, in_=bf_z[0][:, g * 512:(g + 1) * 512].rearrange("(k p) c -> p k c", p=128))])(wzview, g),
                          reads=[bf_z[1]], writes=[wzb.b])
                    wz_cache[g] = (wzb, wzview)
                wz_, wzv = wz_cache[g]
                c = 4 * T_ + j
                i2 = u % NBUF
                yl, szb, vb, vnb, vst = yl2[i2], szb2[i2], vb2[i2], vnb2[i2], vst2[i2]
                S.dma("sp", "yl%d" % i2, lambda e: [e.dma_start(out=yl[:], in_=yscr[:, g, c].rearrange("d p f -> p d f"))], reads=[ydram], writes=[yl.b])
                bkz = bank()

                def f(e):
                    for k in range(KT):
                        r = e.matmul(bkz[:, :], lhsT=hnT[:, k, sc0 + j * 128:sc0 + (j + 1) * 128], rhs=wzv[:, k, :], start=(k == 0), stop=(k == KT - 1))
                    return r
                S.op("pe", f, reads=[hnT.b, wz_.b], writes=[bkz.b])
                S.op("act", lambda e: e.activation(out=szb[:], in_=bkz[:, :], func=AF.Silu), reads=[bkz.b], writes=[szb.b])
                S.op("dve", lambda e: e.tensor_tensor(out=vb[:], in0=yl[:, 0, :], in1=yl[:, 1, :], op=ALU.add), reads=[yl.b], writes=[vb.b])
                S.op("dve", lambda e: e.tensor_tensor(out=vb[:], in0=vb[:], in1=szb[:], op=ALU.mult), reads=[vb.b, szb.b], writes=[vb.b])
                S.op("act", lambda e: e.activation(out=vjunk[:], in_=vb[:], func=AF.Square, accum_out=vst[:, 0:1]), reads=[vb.b], writes=[vjunk.b, vst.b])
                rstd_of(vst, vst[:, 0:1], 512)
                S.op("dve", lambda e: e.scalar_tensor_tensor(out=vnb[:], in0=vb[:], scalar=vst[:, 2:3], in1=normw[:, g * 512:(g + 1) * 512], op0=ALU.mult, op1=ALU.mult),
                     reads=[vb.b, vst.b, normw.b], writes=[vnb.b])

            def gateT(u):
                g, j = u // 4, u % 4
                vnb = vnb2[u % NBUF]
                bkt = bank()

                def f(e):
                    v = bfv(bkt)
                    for m in range(4):
                        r = e.transpose(out=v[:, m * 128:(m + 1) * 128], in_=vnb[:, m * 128:(m + 1) * 128], identity=ident[:])
                    return r
                S.op("pe", f, reads=[vnb.b, ident.b], writes=[bkt.b])
                S.op("act", lambda e: e.activation(out=vT[:, g * 4:(g + 1) * 4, j * 128:(j + 1) * 128], in_=bfv(bkt)[:, 0:512].rearrange("p (m t) -> p m t", m=4), func=AF.Copy),
                     reads=[bkt.b], writes=[vT.b])

            gate_sched = []
            for u in range(16 + 3):
                if u < 16:
                    gate_sched.append(("Z", u))
                if u >= 3:
                    gate_sched.append(("T", u - 3))

            def gate_emit(n):
                for _ in range(n):
                    if gate_sched:
                        kind, u = gate_sched.pop(0)
                        (gateZ if kind == "Z" else gateT)(u)
            bm, bq = bank(), bank()

            def f(e):
                for ct in range(8):
                    r = e.matmul(bm[:, :], lhsT=onesm[:], rhs=cvt[:, ct, :], start=(ct == 0), stop=(ct == 7))
                return r
            S.op("pe", f, reads=[onesm.b, cvt.b], writes=[bm.b])
            for ct in range(8):
                sq = sq2[ct % 2]
                S.op("act", (lambda sq, ct: lambda e: e.activation(out=sq[:], in_=cvt[:, ct, :], func=AF.Square))(sq, ct), reads=[cvt.b], writes=[sq.b])
                S.op("pe", (lambda sq, ct: lambda e: e.matmul(bq[:, :], lhsT=onesm[:], rhs=sq[:], start=(ct == 0), stop=(ct == 7)))(sq, ct), reads=[onesm.b, sq.b], writes=[bq.b])
            S.op("act", lambda e: e.activation(out=lnm[:], in_=bm[:, :], func=AF.Copy), reads=[bm.b], writes=[lnm.b])
            S.op("dve", lambda e: e.tensor_tensor(out=lt[:], in0=lnm[:], in1=lnm[:], op=ALU.mult), reads=[lnm.b], writes=[lt.b])
            S.op("dve", lambda e: e.tensor_tensor(out=lt[:], in0=bq[:, :], in1=lt[:], op=ALU.subtract), reads=[bq.b, lt.b], writes=[lt.b])
            S.op("act", lambda e: e.activation(out=lt[:], in_=lt[:], func=AF.Sqrt, bias=epsc[:, 0:1], scale=1.0), reads=[lt.b, epsc.b], writes=[lt.b])
            S.op("dve", lambda e: e.reciprocal(out=lnr[:], in_=lt[:]), reads=[lt.b], writes=[lnr.b])
            gate_emit(6)
            for ct in range(8):
                S.op("dve", (lambda ct: lambda e: e.tensor_tensor(out=lt2[:], in0=cvt[:, ct, :], in1=lnm[:], op=ALU.subtract))(ct), reads=[cvt.b, lnm.b], writes=[lt2.b])
                S.op("dve", lambda e: e.tensor_tensor(out=lt2[:], in0=lt2[:], in1=lnr[:], op=ALU.mult), reads=[lt2.b, lnr.b], writes=[lt2.b])
                S.op("act", (lambda ct: lambda e: e.activation(out=aact[:, ct, :], in_=lt2[:], func=AF.Silu, bias=vec8[:, 2, ct:ct + 1], scale=vec8[:, 1, ct:ct + 1]))(ct),
                     reads=[lt2.b, vec8.b], writes=[aact.b])
            if T_ == 0:
                dump("aact0", aact[:], [128, 8, 512], BF16, [aact.b])
            for half in range(2):
                wa, wav = wchunk(bf_co, half * 512, 512)
                wg_, wgv = wchunk(bf_gg, half * 512, 512)
                for mm in range(4):
                    m = half * 4 + mm
                    by, bg = bank(), bank()

                    def f(e, by=by, mm=mm, wav=wav):
                        for k in range(KT):
                            r = e.matmul(by[:, :], lhsT=wav[:, k, mm * 128:(mm + 1) * 128], rhs=aact[:, k, :], start=(k == 0), stop=(k == KT - 1))
                        return r
                    S.op("pe", f, reads=[wa.b, aact.b], writes=[by.b])

                    def f(e, bg=bg, mm=mm, wgv=wgv):
                        for k in range(KT):
                            r = e.matmul(bg[:, :], lhsT=wgv[:, k, mm * 128:(mm + 1) * 128], rhs=hnT[:, k, tcol], start=(k == 0), stop=(k == KT - 1))
                        return r
                    S.op("pe", f, reads=[wg_.b, hnT.b], writes=[bg.b])
                    S.op("act", (lambda bg: lambda e: e.activation(out=sgb[:], in_=bg[:, :], func=AF.Sigmoid))(bg), reads=[bg.b], writes=[sgb.b])
                    S.op("dve", (lambda by, m: lambda e: e.tensor_tensor(out=mc[:, m, :], in0=by[:, :], in1=sgb[:], op=ALU.mult))(by, m), reads=[by.b, sgb.b], writes=[mc.b])
                    gate_emit(4)
            gate_emit(100)
            for half in range(2):
                wg_, wgv = wchunk(bf_gg, 1024 + half * 512, 512)
                for pr in range(2):
                    wa, wav = wchunk(bf_so, half * 512 + pr * 256, 256, kt=16)
                    for mm in range(2):
                        m = half * 4 + pr * 2 + mm
                        gm = pr * 2 + mm
                        by, bg = bank(), bank()

                        def f(e, by=by, mm=mm, wav=wav):
                            for k in range(16):
                                r = e.matmul(by[:, :], lhsT=wav[:, k, mm * 128:(mm + 1) * 128], rhs=vT[:, k, :], start=(k == 0), stop=(k == 15))
                            return r
                        S.op("pe", f, reads=[wa.b, vT.b], writes=[by.b])

                        def f(e, bg=bg, gm=gm, wgv=wgv):
                            for k in range(KT):
                                r = e.matmul(bg[:, :], lhsT=wgv[:, k, gm * 128:(gm + 1) * 128], rhs=hnT[:, k, tcol], start=(k == 0), stop=(k == KT - 1))
                            return r
                        S.op("pe", f, reads=[wg_.b, hnT.b], writes=[bg.b])
                        S.op("act", (lambda bg: lambda e: e.activation(out=sgb[:], in_=bg[:, :], func=AF.Sigmoid))(bg), reads=[bg.b], writes=[sgb.b])
                        S.op("dve", (lambda by: lambda e: e.tensor_tensor(out=lt[:], in0=by[:, :], in1=sgb[:], op=ALU.mult))(by), reads=[by.b, sgb.b], writes=[lt.b])
                        S.op("dve", (lambda m: lambda e: e.tensor_tensor(out=merged[:, m, :], in0=lt[:], in1=mc[:, m, :], op=ALU.add))(m), reads=[lt.b, mc.b], writes=[merged.b])
            if T_ == 0:
                dump("merged0", merged[:], [128, 8, 512], BF16, [merged.b])
            for j in range(4):
                S.dma("sp", "xr%d" % (j % 2), (lambda j: lambda e: [e.dma_start(out=hs2[:, j, :], in_=xs[sc0 + j * 128:sc0 + (j + 1) * 128, :])])(j), writes=[hs2.b])
            for nh in range(2):
                wa, wav = wchunk(bf_wo, nh * 512, 512)
                for j in range(4):
                    bk = bank()

                    def f(e, bk=bk, j=j, wav=wav):
                        for k in range(KT):
                            r = e.matmul(bk[:, :], lhsT=merged[:, k, j * 128:(j + 1) * 128], rhs=wav[:, k, :], start=(k == 0), stop=(k == KT - 1))
                        return r
                    S.op("pe", f, reads=[wa.b, merged.b], writes=[bk.b])
                    S.op("dve", (lambda bk, j, nh: lambda e: e.tensor_tensor(out=hs2[:, j, nh * 512:(nh + 1) * 512], in0=bk[:, :], in1=hs2[:, j, nh * 512:(nh + 1) * 512], op=ALU.add))(bk, j, nh),
                         reads=[bk.b, hs2.b], writes=[hs2.b])
            if T_ == 0:
                dump("hs2_0", hs2[:], [128, 4, D], F32, [hs2.b])
            S.barrier()
            pmix.close()
            pf = ExitStack()
            hn2T = sb("hn2T", [128, KT, 512], BF16, pf)
            actT = sb("actT", [128, FT, 512], BF16, pf)
            wd = sb("wd", [128, FT, D], BF16, pf)
            sgf = sb("sgf", [128, 512], F32, pf)
            ob2 = [sb(f"ob{i}", [128, D], F32, pf) for i in range(2)]
            nfin = sb("nfin", [128, D], F32, pf)
            xn2 = [sb(f"fxn{i}", [128, D], BF16, pf) for i in range(2)]
            junk = sb("fjunk", [128, D], BF16, pf)
            st2 = [sb(f"fst{i}", [128, 4], F32, pf) for i in range(2)]
            xt2 = [sb(f"hs3_{i}", [128, D], F32, pf) for i in range(2)]
            S.dma("sp", "nfin", lambda e: [e.dma_start(out=nfin[:], in_=nfin_d.partition_broadcast(128))], writes=[nfin.b])
            for (f0, nf) in ((0, 8), (8, 8), (16, 6)):
                S.dma("sp", "wd", (lambda f0, nf: lambda e: [e.dma_start(out=wd[:, f0:f0 + nf, :], in_=bf_wd[0][f0 * 128:(f0 + nf) * 128, :].rearrange("(k p) c -> p k c", p=128))])(f0, nf),
                      reads=[bf_wd[1]], writes=[wd.b])
            for j in range(4):
                i = j % 2
                xn, st = xn2[i], st2[i]
                S.op("act", (lambda st, j: lambda e: e.activation(out=junk[:], in_=hs2[:, j, :], func=AF.Square, accum_out=st[:, 0:1]))(st, j), reads=[hs2.b], writes=[junk.b, st.b])
                rstd_of(st, st[:, 0:1], D)
                S.op("act", (lambda xn, st, j: lambda e: e.activation(out=xn[:], in_=hs2[:, j, :], func=AF.Copy, scale=st[:, 2:3]))(xn, st, j), reads=[hs2.b, st.b], writes=[xn.b])
                bk = bank()

                def f(e, bk=bk, xn=xn):
                    v = bfv(bk)
                    for k in range(KT):
                        r = e.transpose(out=v[:, k * 128:(k + 1) * 128], in_=xn[:, k * 128:(k + 1) * 128], identity=ident[:])
                    return r
                S.op("pe", f, reads=[xn.b, ident.b], writes=[bk.b])
                S.op("dve", (lambda bk, j: lambda e: e.tensor_tensor(out=hn2T[:, :, j * 128:(j + 1) * 128], in0=bfv(bk).rearrange("p (k t) -> p k t", k=KT),
                                                                       in1=vec8[:, 4, :].unsqueeze(2).to_broadcast([128, KT, 128]), op=ALU.mult))(bk, j),
                     reads=[bk.b, vec8.b], writes=[hn2T.b])
            for (f0, nf) in ((0, 4), (4, 4), (8, 4), (12, 4), (16, 4), (20, 2)):
                wg_, wgv = wchunk(bf_wg, f0 * 128, nf * 128)
                wu_, wuv = wchunk(bf_wu, f0 * 128, nf * 128)
                for fl in range(nf):
                    fi = f0 + fl
                    bg, bu = bank(), bank()
                    for (bk_, wv_, wt_) in ((bg, wgv, wg_), (bu, wuv, wu_)):
                        def f(e, bk_=bk_, wv_=wv_, fl=fl):
                            for k in range(KT):
                                r = e.matmul(bk_[:, :], lhsT=wv_[:, k, fl * 128:(fl + 1) * 128], rhs=hn2T[:, k, :], start=(k == 0), stop=(k == KT - 1))
                            return r
                        S.op("pe", f, reads=[wt_.b, hn2T.b], writes=[bk_.b])
                    S.op("act", (lambda bg: lambda e: e.activation(out=sgf[:], in_=bg[:, :], func=AF.Silu))(bg), reads=[bg.b], writes=[sgf.b])
                    S.op("dve", (lambda bu, fi: lambda e: e.tensor_tensor(out=actT[:, fi, :], in0=bu[:, :], in1=sgf[:], op=ALU.mult))(bu, fi), reads=[bu.b, sgf.b], writes=[actT.b])
            for j in range(4):
                i = j % 2
                hs3, ob, st = xt2[i], ob2[i], st2[i]
                for nh in range(2):
                    bk = bank()

                    def f(e, bk=bk, j=j, nh=nh):
                        for fi in range(FT):
                            r = e.matmul(bk[:, :], lhsT=actT[:, fi, j * 128:(j + 1) * 128], rhs=wd[:, fi, nh * 512:(nh + 1) * 512], start=(fi == 0), stop=(fi == FT - 1))
                        return r
                    S.op("pe", f, reads=[actT.b, wd.b], writes=[bk.b])
                    S.op("dve", (lambda bk, hs3, j, nh: lambda e: e.tensor_tensor(out=hs3[:, nh * 512:(nh + 1) * 512], in0=bk[:, :], in1=hs2[:, j, nh * 512:(nh + 1) * 512], op=ALU.add))(bk, hs3, j, nh),
                         reads=[bk.b, hs2.b], writes=[hs3.b])
                S.op("act", (lambda hs3, st: lambda e: e.activation(out=junk[:], in_=hs3[:], func=AF.Square, accum_out=st[:, 0:1]))(hs3, st), reads=[hs3.b], writes=[junk.b, st.b])
                rstd_of(st, st[:, 0:1], D)
                S.op("dve", (lambda hs3, ob, st: lambda e: e.scalar_tensor_tensor(out=ob[:], in0=hs3[:], scalar=st[:, 2:3], in1=nfin[:], op0=ALU.mult, op1=ALU.mult))(hs3, ob, st),
                     reads=[hs3.b, st.b, nfin.b], writes=[ob.b])
                S.dma("act", "out%d" % i, (lambda ob, j: lambda e: [e.dma_start(out=out_d[tok0 + j * 128:tok0 + (j + 1) * 128, :], in_=ob[:])])(ob, j), reads=[ob.b], writes=[Buf("outd")])
            S.barrier()
            pf.close()
        pb.close()
        return finish(nc, S, dbg_out)
    return nc


def finish(nc, S, dbg_out):
    S.barrier()
    with nc.Block() as block:
        @block.tensor
        def _(e):
            e.nop()

        @block.scalar
        def _(e):
            e.nop()

        @block.vector
        def _(e):
            e.nop()

        @block.gpsimd
        def _(e):
            e.nop()

        @block.sync
        def _(e):
            e.nop()
    nc._sched_counts = (dict(S.cnt), dict(S.dcnt))
    return nc


def _consts():
    k = np.arange(128)[:, None]
    l = np.arange(128)[None, :]
    ident = np.eye(128, dtype=np.float32)
    uinc = (k <= l).astype(np.float32)
    nustr = -(k < l).astype(np.float32)
    negf = np.where(l < k, -30000.0, 0.0).astype(np.float32)
    negb = np.where(l > k, -30000.0, 0.0).astype(np.float32)
    return dict(cident=ident, cuinc=uinc, cnustr=nustr, cnegf=np.tile(negf, (1, 4)), cnegb=np.tile(negb, (1, 4)))


def prep_inputs(inputs):
    f = lambda a: np.ascontiguousarray(np.asarray(a, dtype=np.float32))
    x = f(inputs["x"])
    meta = f(inputs["meta_tokens"])
    pt = lambda v: np.ascontiguousarray(f(v).reshape(-1, 128).T)
    shared = dict(
        w_in=f(inputs["w_in"][0]), conv_out_w=f(inputs["conv_out_w"][0]), ssm_out_w=f(inputs["ssm_out_w"][0]),
        w_o=f(inputs["w_o"][0]), w_gate=f(inputs["w_gate"][0]), w_up=f(inputs["w_up"][0]), w_down=f(inputs["w_down"][0]),
        cdw=np.ascontiguousarray(f(inputs["conv_dw_w"][0]).T.reshape(8, 128, 31).transpose(1, 0, 2)),
        scw=np.ascontiguousarray(f(inputs["ssm_conv_w"][0]).T.reshape(24, 128, 7).transpose(1, 0, 2)),
        vec8=np.ascontiguousarray(np.stack([pt(inputs["conv_dw_b"][0]), pt(inputs["conv_ln_g"][0]), pt(inputs["conv_ln_b"][0]),
                                            pt(inputs["norm_mix"][0]), pt(inputs["norm_ffn"][0])], axis=1)),
        scb=pt(inputs["ssm_conv_b"][0]),
        rows=np.concatenate([f(inputs["dt_bias_f"][0]), f(inputs["dt_bias_b"][0]), f(inputs["a_log_f"][0]), f(inputs["a_log_b"][0])])[None, :].copy(),
        dsk=f(inputs["ssm_d"][0])[None, :].copy(),
        normw=f(inputs["ssm_norm_w"][0])[None, :].copy(),
        nfin=f(inputs["norm_final"])[None, :].copy(),
        hmask=np.concatenate([np.zeros(112, np.float32), np.ones(16, np.float32)])[:, None].copy(),
    )
    shared.update(_consts())
    in_maps = []
    for core in range(8):
        b, k = core // 4, core % 4
        xs = np.zeros((SLAB, D), np.float32)
        xs[128:2176] = x[b, k * SEG:(k + 1) * SEG]
        if k == 0:
            xs[112:128] = meta
        else:
            xs[0:128] = x[b, k * SEG - 128:k * SEG]
        if k < 3:
            xs[2176:2304] = x[b, (k + 1) * SEG:(k + 1) * SEG + 128]
        xo = np.zeros(((NOT + 1) * 128, D), np.float32)
        xo[112:128] = meta
        hal = xo[NOT * 128:]
        hal[3:6] = x[b, 0:3]
        mrow = np.zeros((1, 2 * NOT), np.float32)
        mrow[0, 0] = 1.0
        others = [j for j in range(4) if j != k]
        for bi, j in enumerate(others):
            for q in range(4):
                blk = 1 + bi * 4 + q
                t0 = 1 + 4 * (blk - 1)
                st = j * SEG + q * 512
                xo[t0 * 128:(t0 + 4) * 128] = x[b, st:st + 512]
                hal[blk * 6:blk * 6 + 3] = meta[13:16] if st == 0 else x[b, st - 3:st]
                if st + 512 < L:
                    hal[blk * 6 + 3:blk * 6 + 6] = x[b, st + 512:st + 515]
                mrow[0, t0:t0 + 4] = 1.0 if j < k else 0.0
                mrow[0, NOT + t0:NOT + t0 + 4] = 1.0 if j > k else 0.0
        m = dict(shared)
        m.update(xs=xs, xo=xo, mrow=mrow)
        in_maps.append(m)
    return in_maps


def kernel(**inputs):
    in_maps = prep_inputs(inputs)
    nc = build_nc()
    res = run_bass_kernel_spmd(nc, in_maps, core_ids=list(range(8)))
    out = np.empty((NB, L, D), np.float32)
    for core in range(8):
        b, k = core // 4, core % 4
        out[b, k * SEG:(k + 1) * SEG] = res.results[core]["out"]
    return out
```

```python
import numpy as np
from contextlib import ExitStack
import concourse.bass as bass
import concourse.mybir as mybir
from concourse.bass_utils import run_bass_kernel_spmd

F32 = mybir.dt.float32
BF16 = mybir.dt.bfloat16
AF = mybir.ActivationFunctionType
ALU = mybir.AluOpType

D = 1024
KT = 8
L = 8192
NB = 2
SEG = 2048
NCH = 16
SLAB = 2304
NOT = 49
DFF = 2816
FT = 22
EPS = 1e-6
O_VAL, O_GATE, O_Z, O_X, O_B, O_C, O_DT, O_GC, O_GS = 0, 1024, 2048, 4096, 6144, 6656, 7168, 7232, 8256

ENGS = ["pe", "act", "dve", "pool", "sp"]


class Buf:
    __slots__ = ("name", "w", "r", "excl")

    def __init__(self, name, excl=False):
        self.name = name
        self.excl = excl
        self.w = None
        self.r = []


class Sched:
    def __init__(self, nc, es):
        self.nc = nc
        self.es = es
        self.sems = {}
        for e in ENGS:
            self.sems[e] = es.enter_context(nc.semaphore(f"sem_{e}"))
        self.cnt = {e: 0 for e in ENGS}
        self.seen = {e: {} for e in ENGS}
        self.dcnt = {}

    def dsem(self, name):
        key = f"d_{name}"
        if key not in self.sems:
            self.sems[key] = self.es.enter_context(self.nc.semaphore(f"dsem_{name}"))
            self.dcnt[key] = 0
        return key

    def _deps(self, eng, reads, writes):
        deps = {}

        def add(ev):
            if ev is None:
                return
            k, v = ev
            if deps.get(k, 0) < v:
                deps[k] = v
        for b in reads:
            add(b.w)
            if b.excl:
                for ev in b.r:
                    add(ev)
        for b in writes:
            add(b.w)
            for ev in b.r:
                add(ev)
        waits = []
        seen = self.seen[eng]
        for k, v in deps.items():
            if seen.get(k, 0) < v:
                seen[k] = v
                waits.append((k, v))
        return waits

    def _mark(self, ev, reads, writes):
        for b in reads:
            if b.excl:
                b.w = ev
                b.r = []
            else:
                b.r = [x for x in b.r if x[0] != ev[0]] + [ev]
        for b in writes:
            b.w = ev
            b.r = []

    def _emit(self, name, waits, fn, dkey, n):
        nc = self.nc
        e = {"pe": nc.tensor, "act": nc.scalar, "dve": nc.vector, "pool": nc.gpsimd, "sp": nc.sync}[name]
        sems = self.sems
        for k, v in waits:
            e.wait_ge(sems[k], v)
        if fn is None:
            return
        r = fn(e)
        if dkey is None:
            r.then_inc(sems[name], 1)
        else:
            if not isinstance(r, (list, tuple)):
                r = [r]
            assert len(r) == n, (len(r), n)
            for ins in r:
                ins.then_inc(sems[dkey], 16)

    def op(self, eng, fn, reads=(), writes=()):
        waits = self._deps(eng, reads, writes)
        self.cnt[eng] += 1
        ev = (eng, self.cnt[eng])
        self._emit(eng, waits, fn, None, 0)
        self._mark(ev, reads, writes)
        return ev

    def dma(self, q, name, fn, reads=(), writes=(), n=1):
        dkey = self.dsem(name)
        waits = self._deps(q, reads, writes)
        self.dcnt[dkey] += 16 * n
        ev = (dkey, self.dcnt[dkey])
        self._emit(q, waits, fn, dkey, n)
        self._mark(ev, reads, writes)
        return ev

    def barrier(self):
        for e in ENGS:
            waits = []
            seen = self.seen[e]
            for k in ENGS:
                if self.cnt[k] > seen.get(k, 0):
                    seen[k] = self.cnt[k]
                    waits.append((k, self.cnt[k]))
            for k, v in self.dcnt.items():
                if v > seen.get(k, 0):
                    seen[k] = v
                    waits.append((k, v))
            self._emit(e, waits, None, None, 0)


class T:
    def __init__(self, t, name, excl=False):
        self.t = t
        self.b = Buf(name, excl)

    def __getitem__(self, k):
        return self.t[k]


def build_nc(upto=99, dbg=()):
    nc = bass.Bass("TRN2", target_bir_lowering=False)

    def din(name, shape):
        return nc.dram_tensor(name, list(shape), F32, kind="ExternalInput").ap()

    xs = din("xs", [SLAB, D])
    xo = din("xo", [(NOT + 1) * 128, D])
    w_in = din("w_in", [D, 9280])
    conv_out_w = din("conv_out_w", [D, D])
    ssm_out_w = din("ssm_out_w", [2048, D])
    w_o = din("w_o", [D, D])
    w_gate = din("w_gate", [D, DFF])
    w_up = din("w_up", [D, DFF])
    w_down = din("w_down", [DFF, D])
    cdw_d = din("cdw", [128, 8, 31])
    scw_d = din("scw", [128, 24, 7])
    vec8_d = din("vec8", [128, 5, 8])
    scb_d = din("scb", [128, 24])
    rows_d = din("rows", [1, 128])
    dsk_d = din("dsk", [1, 32])
    normw_d = din("normw", [1, 2048])
    nfin_d = din("nfin", [1, D])
    hmask_d = din("hmask", [128, 1])
    mrow_d = din("mrow", [1, 2 * NOT])
    cident_d = din("cident", [128, 128])
    cuinc_d = din("cuinc", [128, 128])
    cnustr_d = din("cnustr", [128, 128])
    cnegf_d = din("cnegf", [128, 512])
    cnegb_d = din("cnegb", [128, 512])
    out_d = nc.dram_tensor("out", [SEG, D], F32, kind="ExternalOutput").ap()

    hns = nc.dram_tensor("hns", [NOT + 1, 128, D], BF16).ap()
    yscr = nc.dram_tensor("yscr", [2, 4, NCH, 128, 512], F32).ap()

    dbg_out = {}

    with ExitStack() as es:
        S = Sched(nc, es)

        uniq = {"n": 0}

        def sb(name, shape, dt, st=es):
            uniq["n"] += 1
            return T(st.enter_context(nc.sbuf_tensor("s%d_%s" % (uniq["n"], name), list(shape), dt)), name)

        banks = [T(es.enter_context(nc.psum_tensor(f"pb{i}", [128, 512], F32)), f"pb{i}", excl=True) for i in range(8)]
        bank_rr = {"i": 0, "n": 8}

        def bank():
            b = banks[bank_rr["i"] % bank_rr["n"]]
            bank_rr["i"] += 1
            return b

        def bfv(b):
            return b.t[:].bitcast(BF16)

        def dump(name, src, shape, dt, reads):
            if name not in dbg:
                return
            dd = nc.dram_tensor("dbg_" + name, list(shape), dt, kind="ExternalOutput").ap()
            dbg_out[name] = dd
            S.dma("sp", "dbg_" + name, lambda e: [e.dma_start(out=dd, in_=src)], reads=reads, writes=[Buf("dbgo")])

        ident = sb("ident", [128, 128], BF16)
        ones = sb("ones", [128, 128], BF16)
        onesm = sb("onesm", [128, 128], BF16)
        epsc = sb("epsc", [128, 1], F32)
        onec = sb("onec", [128, 1], F32)
        vec8 = sb("vec8", [128, 5, 8], F32)
        hnT = sb("hnT", [128, KT, SLAB], BF16)
        pA = ExitStack()
        uinc = sb("uinc", [128, 128], BF16, pA)
        nustr = sb("nustr", [128, 128], BF16, pA)
        negf = sb("negf", [128, 512], BF16, pA)
        negb = sb("negb", [128, 512], BF16, pA)
        scb = sb("scb", [128, 24], F32, pA)
        scw = sb("scw", [128, 24, 7], F32, pA)
        rows = sb("rows", [128, 128], F32, pA)
        abc = sb("abc", [128, 64], F32, pA)
        dsk = sb("dsk", [128, 32], F32, pA)
        hmask = sb("hmask", [128, 1], F32, pA)
        mrow = sb("mrow", [128, 2 * NOT], F32, pA)
        wdt = sb("wdt", [128, KT, 64], BF16, pA)
        sin = sb("sin", [128, 2, 2048], F32, pA)
        dtraw_o = sb("dtraw_o", [128, NCH, 64], F32, pA)
        ps1 = ExitStack()
        dtraw_s = sb("dtraw_s", [128, NOT, 64], F32, ps1)
        p0 = ExitStack()
        xt2 = [sb(f"xt{i}", [128, D], F32, p0) for i in range(8)]
        xn2 = [sb(f"xn{i}", [128, D], BF16, p0) for i in range(8)]
        junk = sb("junk", [128, D], BF16, p0)
        st2 = [sb(f"st{i}", [128, 4], F32, p0) for i in range(8)]
        hnt2 = [sb(f"hnt{i}", [128, KT, 128], BF16, p0) for i in range(4)]

        for (t_, d_) in [(ident, cident_d), (uinc, cuinc_d), (nustr, cnustr_d), (negf, cnegf_d), (negb, cnegb_d)]:
            S.dma("pool", "c_" + t_.b.name, (lambda t_, d_: lambda e: [e.dma_start(out=t_[:], in_=d_)])(t_, d_), writes=[t_.b])
        for (t_, d_) in [(vec8, vec8_d), (scb, scb_d), (scw, scw_d), (hmask, hmask_d)]:
            S.dma("sp", "c_" + t_.b.name, (lambda t_, d_: lambda e: [e.dma_start(out=t_[:], in_=d_)])(t_, d_), writes=[t_.b])
        for (t_, d_, n_) in [(rows, rows_d, 128), (dsk, dsk_d, 32), (mrow, mrow_d, 2 * NOT)]:
            S.dma("sp", "c_" + t_.b.name, (lambda t_, d_: lambda e: [e.dma_start(out=t_[:], in_=d_.partition_broadcast(128))])(t_, d_), writes=[t_.b])
        S.op("dve", lambda e: e.memset(ones[:], 1.0), writes=[ones.b])
        S.op("dve", lambda e: e.memset(onesm[:], 1.0 / 1024.0), writes=[onesm.b])
        S.op("dve", lambda e: e.memset(epsc[:], EPS), writes=[epsc.b])
        S.op("dve", lambda e: e.memset(onec[:], 1.0), writes=[onec.b])
        S.op("act", lambda e: e.activation(out=abc[:], in_=rows[:, 64:128], func=AF.Exp), reads=[rows.b], writes=[abc.b])
        S.op("dve", lambda e: e.tensor_scalar(out=abc[:], in0=abc[:], scalar1=-1.0, scalar2=None, op0=ALU.mult), reads=[abc.b], writes=[abc.b])

        def wload(dst, dst_ap, W, c0, ncols, kt=KT, r0=0):
            src = W[r0:r0 + kt * 128, c0:c0 + ncols].rearrange("(k p) c -> p k c", p=128)
            S.dma("pool", "w_" + dst.b.name, lambda e: [e.dma_start(out=dst_ap, in_=src)], writes=[dst.b])

        wload(wdt, wdt[:], w_in, O_DT, 64)

        pc_jobs = []

        def pc_issue(n):
            for _ in range(n):
                if pc_jobs:
                    pc_jobs.pop(0)()

        def precast(name, W, c0, ncols, nrows):
            dst = nc.dram_tensor("bf_" + name, [nrows, ncols], BF16).ap()
            b = Buf("bf_" + name)
            for r0 in range(0, nrows, 128):
                for cc in range(0, ncols, 2048):
                    w = min(2048, ncols - cc)
                    pc_jobs.append((lambda r0, cc, w: lambda: S.dma("pool", "pc_" + name, lambda e: [e.dma_start(out=dst[r0:r0 + 128, cc:cc + w], in_=W[r0:r0 + 128, c0 + cc:c0 + cc + w])],
                                                                   writes=[b]))(r0, cc, w))
            return dst, b
        bf_co = precast("co", conv_out_w, 0, D, D)
        bf_gg = precast("gg", w_in, O_GC, 2048, D)
        bf_so = precast("so", ssm_out_w, 0, D, 2048)
        bf_wo = precast("wo", w_o, 0, D, D)
        bf_wg = precast("wg", w_gate, 0, DFF, D)
        bf_wu = precast("wu", w_up, 0, DFF, D)
        bf_wd = precast("wd", w_down, 0, D, DFF)
        bf_z = precast("z", w_in, O_Z, 2048, D)

        ntc = {"i": 0}

        def rstd_of(st, ssq_ap, dim):
            S.op("act", lambda e: e.activation(out=st[:, 1:2], in_=ssq_ap, func=AF.Sqrt, bias=epsc[:, 0:1], scale=1.0 / dim),
                 reads=[st.b, epsc.b], writes=[st.b])
            S.op("dve", lambda e: e.reciprocal(out=st[:, 2:3], in_=st[:, 1:2]), reads=[st.b], writes=[st.b])

        def norm_stage1(src_rows):
            i = ntc["i"] % 8
            ntc["i"] += 1
            xt, xn, st = xt2[i], xn2[i], st2[i]
            S.dma("sp", "xt%d" % i, lambda e: [e.dma_start(out=xt[:], in_=src_rows)], writes=[xt.b])
            S.op("act", lambda e: e.activation(out=junk[:], in_=xt[:], func=AF.Square, accum_out=st[:, 0:1]),
                 reads=[xt.b], writes=[junk.b, st.b])
            rstd_of(st, st[:, 0:1], D)
            S.op("dve", lambda e: e.tensor_scalar(out=xn[:], in0=xt[:], scalar1=st[:, 2:3], scalar2=None, op0=ALU.mult), reads=[xt.b, st.b], writes=[xn.b])
            return xn

        def norm_stage2(xn, dst, dst_ap, gidx):
            bk = bank()

            def tr(e):
                v = bfv(bk)
                for k in range(KT):
                    r = e.transpose(out=v[:, k * 128:(k + 1) * 128], in_=xn[:, k * 128:(k + 1) * 128], identity=ident[:])
                return r
            S.op("pe", tr, reads=[xn.b, ident.b], writes=[bk.b])
            S.op("dve", lambda e: e.tensor_tensor(out=dst_ap, in0=bfv(bk).rearrange("p (k t) -> p k t", k=KT),
                                                  in1=vec8[:, gidx, :].unsqueeze(2).to_broadcast([128, KT, 128]), op=ALU.mult),
                 reads=[bk.b, vec8.b], writes=[dst.b])

        def dt_mm(lhs_fn, lhs_reads, dst, dst_ap):
            bk = bank()

            def f(e):
                for k in range(KT):
                    r = e.matmul(bk[:, 0:64], lhsT=lhs_fn(k), rhs=wdt[:, k, :], start=(k == 0), stop=(k == KT - 1))
                return r
            S.op("pe", f, reads=lhs_reads + [wdt.b], writes=[bk.b])
            S.op("dve", lambda e: e.tensor_tensor(out=dst_ap, in0=bk[:, 0:64], in1=rows[:, 0:64], op=ALU.add),
                 reads=[bk.b, rows.b], writes=[dst.b])

        hns_b = Buf("hns_all")
        NT_OWN = SLAB // 128
        items = [("own", i) for i in range(NT_OWN)] + [("oth", i) for i in range(NOT + 1)]

        def nst1(it):
            kind, i = it
            src = xs[i * 128:(i + 1) * 128, :] if kind == "own" else xo[i * 128:(i + 1) * 128, :]
            return norm_stage1(src)

        def nst2(it, xn):
            kind, i = it
            if kind == "own":
                norm_stage2(xn, hnT, hnT[:, :, i * 128:(i + 1) * 128], 3)
            else:
                h = hnt2[i % 4]
                norm_stage2(xn, h, h[:], 3)

        def nst3(it):
            kind, i = it
            if kind == "own":
                if 1 <= i <= NCH:
                    c = i - 1
                    dt_mm(lambda k: hnT[:, k, (c + 1) * 128:(c + 2) * 128], [hnT.b], dtraw_o, dtraw_o[:, c, :])
            else:
                h = hnt2[i % 4]
                if i < NOT:
                    dt_mm(lambda k: h[:, k, :], [h.b], dtraw_s, dtraw_s[:, i, :])
                S.dma("pool", "hns%d" % (i % 4), lambda e: [e.dma_start(out=hns[i], in_=h[:].rearrange("p k t -> p (k t)"))],
                      reads=[h.b], writes=[Buf("hns_dram%d" % i)])

        xns = {}
        n_it = len(items)
        for step in range(n_it + 3):
            if step < n_it:
                xns[step] = nst1(items[step])
            if 0 <= step - 2 < n_it:
                nst2(items[step - 2], xns.pop(step - 2))
            if 0 <= step - 3 < n_it:
                nst3(items[step - 3])
        dump("hnT", hnT[:], [128, KT, SLAB], BF16, [hnT.b])
        dump("dtraw_o", dtraw_o[:], [128, NCH, 64], F32, [dtraw_o.b])
        S.barrier()
        p0.close()
        dump("dtraw_s", dtraw_s[:], [128, NOT, 64], F32, [dtraw_s.b])
        if upto <= 0:
            return finish(nc, S, dbg_out)

        ps1a = ExitStack()
        tB = sb("tB", [128, NOT, 64], F32, ps1a)
        tC = sb("tC", [128, NOT, 64], F32, ps1a)
        tW = sb("tW", [128, NOT, 64], F32, ps1a)
        adts = sb("adts", [128, NOT, 64], BF16, ps1a)
        tA = dtraw_s

        def softplus_inplace(t, n):
            S.op("act", lambda e: e.activation(out=t[:], in_=t[:], func=AF.Exp), reads=[t.b], writes=[t.b])
            S.op("act", lambda e: e.activation(out=t[:], in_=t[:], func=AF.Ln, bias=onec[:, 0:1], scale=1.0), reads=[t.b, onec.b], writes=[t.b])

        def cums(adt, n, tinc, ttot):
            flat = adt[:].rearrange("p c j -> p (c j)")
            fi = tinc[:].rearrange("p c j -> p (c j)")
            ft = ttot[:].rearrange("p c j -> p (c j)")
            tot = n * 64
            for c0 in range(0, tot, 512):
                w = min(512, tot - c0)
                b1 = bank()
                S.op("pe", (lambda b1, c0, w: lambda e: e.matmul(b1[:, 0:w], lhsT=uinc[:], rhs=flat[:, c0:c0 + w], start=True, stop=True))(b1, c0, w),
                     reads=[adt.b, uinc.b], writes=[b1.b])
                S.op("act", (lambda b1, c0, w: lambda e: e.activation(out=fi[:, c0:c0 + w], in_=b1[:, 0:w], func=AF.Copy))(b1, c0, w),
                     reads=[b1.b], writes=[tinc.b])
                b2 = bank()
                S.op("pe", (lambda b2, c0, w: lambda e: e.matmul(b2[:, 0:w], lhsT=ones[:], rhs=flat[:, c0:c0 + w], start=True, stop=True))(b2, c0, w),
                     reads=[adt.b, ones.b], writes=[b2.b])
                S.op("dve", (lambda b2, c0, w: lambda e: e.tensor_copy(out=ft[:, c0:c0 + w], in_=b2[:, 0:w]))(b2, c0, w),
                     reads=[b2.b], writes=[ttot.b])

        softplus_inplace(tA, NOT)
        S.op("dve", lambda e: e.tensor_scalar(out=tA[:, 0, :], in0=tA[:, 0, :], scalar1=hmask[:, 0:1], scalar2=None, op0=ALU.mult),
             reads=[tA.b, hmask.b], writes=[tA.b])
        for d in range(2):
            S.op("dve", (lambda d: lambda e: e.tensor_tensor(out=tA[:, :, d * 32:(d + 1) * 32], in0=tA[:, :, d * 32:(d + 1) * 32],
                                                               in1=mrow[:, d * NOT:(d + 1) * NOT].unsqueeze(2).to_broadcast([128, NOT, 32]), op=ALU.mult))(d),
                 reads=[tA.b, mrow.b], writes=[tA.b])
        S.op("dve", lambda e: e.tensor_tensor(out=adts[:], in0=tA[:], in1=abc[:].unsqueeze(1).to_broadcast([128, NOT, 64]), op=ALU.mult),
             reads=[tA.b, abc.b], writes=[adts.b])
        cums(adts, NOT, tB, tC)
        S.op("dve", lambda e: e.tensor_tensor(out=tB[:, :, 0:32], in0=tC[:, :, 0:32], in1=tB[:, :, 0:32], op=ALU.subtract), reads=[tB.b, tC.b], writes=[tB.b])
        S.op("dve", lambda e: e.tensor_tensor(out=tB[:, :, 32:64], in0=tB[:, :, 32:64], in1=adts[:, :, 32:64], op=ALU.subtract), reads=[tB.b, adts.b], writes=[tB.b])
        S.op("dve", lambda e: e.memset(tW[:, NOT - 1, 0:32], 0.0), writes=[tW.b])
        S.op("dve", lambda e: e.memset(tW[:, 0, 32:64], 0.0), writes=[tW.b])
        for c in range(NOT - 2, -1, -1):
            S.op("dve", (lambda c: lambda e: e.tensor_tensor(out=tW[:, c, 0:32], in0=tW[:, c + 1, 0:32], in1=tC[:, c + 1, 0:32], op=ALU.add))(c),
                 reads=[tW.b, tC.b], writes=[tW.b])
        for c in range(1, NOT):
            S.op("pool", (lambda c: lambda e: e.tensor_tensor(out=tW[:, c, 32:64], in0=tW[:, c - 1, 32:64], in1=tC[:, c - 1, 32:64], op=ALU.add))(c),
                 reads=[tW.b, tC.b], writes=[tW.b])
        S.op("dve", lambda e: e.tensor_tensor(out=tB[:], in0=tB[:], in1=tW[:], op=ALU.add), reads=[tB.b, tW.b], writes=[tB.b])
        S.op("act", lambda e: e.activation(out=tB[:], in_=tB[:], func=AF.Exp), reads=[tB.b], writes=[tB.b])
        S.op("dve", lambda e: e.tensor_tensor(out=tA[:], in0=tA[:], in1=tB[:], op=ALU.mult), reads=[tA.b, tB.b], writes=[tA.b])
        mult_s = tA
        dump("mult_s", mult_s[:], [128, NOT, 64], F32, [mult_s.b])
        S.barrier()
        ps1a.close()

        bank_rr["n"] = 6
        accS = [banks[6], banks[7]]
        wsg_2 = [sb(f"wsg{i}", [128, KT, 640], BF16, ps1) for i in range(2)]
        dgs_2 = [sb(f"dgs{i}", [128, 5, 7, 128], BF16, ps1) for i in range(2)]
        xh = sb("xh", [128, 5, 128], BF16, ps1)
        hnb2 = [sb(f"hnb{i}", [128, 4, D], BF16, ps1) for i in range(2)]
        hnh = sb("hnh", [128, D], BF16, ps1)
        xpre_2 = [sb(f"xpre{i}", [128, 5, 520], BF16, ps1) for i in range(2)]
        xcb_2 = [sb(f"xcb{i}", [128, 5, 512], BF16, ps1) for i in range(2)]
        xpre_mb = [[Buf(f"xpm{i}_{m}") for m in range(5)] for i in range(2)]
        xpre_hb = [[Buf(f"xph{i}_{m}") for m in range(5)] for i in range(2)]
        xcb_mb = [[Buf(f"xcm{i}_{m}") for m in range(5)] for i in range(2)]
        xbt_2 = [[sb(f"xbt{j}_{i}", [128, 640], BF16, ps1) for i in range(4)] for j in range(2)]
        xdd_2 = [[[sb(f"xdd{j}_{i}_{d}", [128, 512], BF16, ps1) for d in range(2)] for i in range(4)] for j in range(2)]
        S.dma("sp", "hnh", lambda e: [e.dma_start(out=hnh[:], in_=hns[NOT])], reads=[hns_b], writes=[hnh.b])

        def build_diag(dg, dg_ap, tile, ntap, wt, eng):
            for tap in range(ntap):
                S.op(eng, (lambda tap: lambda e: e.tensor_scalar(out=dg_ap[:, tap, :], in0=ident[:], scalar1=wt[:, tile, tap:tap + 1], scalar2=None, op0=ALU.mult))(tap),
                     reads=[ident.b, wt.b], writes=[dg.b])

        nblk = 13
        def s2_prefetch(g):
            wsg_, dgs_ = wsg_2[g % 2], dgs_2[g % 2]
            wload(wsg_, wsg_[:, :, 0:512], w_in, O_X + g * 512, 512)
            wload(wsg_, wsg_[:, :, 512:640], w_in, O_B + g * 128, 128)
            tl_ = [g * 4 + m for m in range(4)] + [16 + g]
            for m in range(5):
                build_diag(dgs_, dgs_[:, m], tl_[m], 7, scw, "dve")

        s2_prefetch(0)
        for g in range(4):
            wsg, dgs = wsg_2[g % 2], dgs_2[g % 2]
            tiles = [g * 4 + m for m in range(4)] + [16 + g]
            for m in range(5):
                bk = bank()

                def f(e, bk=bk, m=m):
                    for k in range(KT):
                        r = e.matmul(bk[:, 0:128], lhsT=wsg[:, k, m * 128:(m + 1) * 128], rhs=hnh[:, k * 128:(k + 1) * 128], start=(k == 0), stop=(k == KT - 1))
                    return r
                S.op("pe", f, reads=[wsg.b, hnh.b], writes=[bk.b])
                S.op("act", (lambda bk, m: lambda e: e.activation(out=xh[:, m, :], in_=bk[:, 0:128], func=AF.Copy))(bk, m), reads=[bk.b], writes=[xh.b])
            def blkinfo(blk):
                NT = 1 if blk == 0 else 4
                t0 = 0 if blk == 0 else 1 + 4 * (blk - 1)
                return NT, t0, NT * 128

            def stageA(blk):
                NT, t0, N = blkinfo(blk)
                pc_issue(2)
                hb = hnb2[blk % 2]
                xpre, xcb = xpre_2[blk % 2], xcb_2[blk % 2]
                xpm, xph, xcm = xpre_mb[blk % 2], xpre_hb[blk % 2], xcb_mb[blk % 2]
                S.dma("sp", "hnb%d" % (blk % 2), lambda e: [e.dma_start(out=hb[:, 0:NT, :], in_=hns[t0:t0 + NT].rearrange("t p f -> p t f"))],
                      reads=[hns_b], writes=[hb.b])
                hbv = hb[:].rearrange("p t (k c) -> p t k c", k=KT)
                for m in range(5):
                    bk = bank()

                    def f(e, bk=bk, m=m):
                        for k in range(KT):
                            r = e.matmul(bk[:, 0:N], lhsT=wsg[:, k, m * 128:(m + 1) * 128], rhs=hbv[:, 0:NT, k, :], start=(k == 0), stop=(k == KT - 1))
                        return r
                    S.op("pe", f, reads=[wsg.b, hb.b], writes=[bk.b])
                    S.op("act", (lambda bk, m: lambda e: e.activation(out=xpre[:, m, 3:3 + N], in_=bk[:, 0:N], func=AF.Copy))(bk, m),
                         reads=[bk.b], writes=[xpm[m]])
                    S.op("act", (lambda m: lambda e: e.activation(out=xpre[:, m, 0:3], in_=xh[:, m, blk * 6:blk * 6 + 3], func=AF.Copy))(m), reads=[xh.b], writes=[xph[m]])
                    S.op("act", (lambda m: lambda e: e.activation(out=xpre[:, m, 3 + N:6 + N], in_=xh[:, m, blk * 6 + 3:blk * 6 + 6], func=AF.Copy))(m), reads=[xh.b], writes=[xph[m]])
                for m in range(5):
                    bk = bank()

                    def f(e, bk=bk, m=m):
                        for tap in range(7):
                            r = e.matmul(bk[:, 0:N], lhsT=dgs[:, m, tap, :], rhs=xpre[:, m, tap:tap + N], start=(tap == 0), stop=(tap == 6))
                        return r
                    S.op("pe", f, reads=[dgs.b, xpm[m], xph[m]], writes=[bk.b])
                    S.op("act", (lambda bk, m, tl: lambda e: e.activation(out=xcb[:, m, 0:N], in_=bk[:, 0:N], func=AF.Silu, bias=scb[:, tl:tl + 1], scale=1.0))(bk, m, tiles[m]),
                         reads=[bk.b, scb.b], writes=[xcm[m]])

            def stageB(blk):
                NT, t0, N = blkinfo(blk)
                xcb, xbt, xdd = xcb_2[blk % 2], xbt_2[blk % 2], xdd_2[blk % 2]
                xcm = xcb_mb[blk % 2]
                for t in range(NT):
                    c = t0 + t
                    bk = bank()

                    def f(e, bk=bk, t=t):
                        v = bfv(bk)
                        for m in range(5):
                            r = e.transpose(out=v[:, m * 128:(m + 1) * 128], in_=xcb[:, m, t * 128:(t + 1) * 128], identity=ident[:])
                        return r
                    S.op("pe", f, reads=xcm + [ident.b], writes=[bk.b])
                    S.op("dve", (lambda bk, t: lambda e: e.tensor_copy(out=xbt[t][:], in_=bfv(bk)[:, 0:640]))(bk, t), reads=[bk.b], writes=[xbt[t].b])
                    for d in range(2):
                        S.op("pool" if d == 0 else "dve",
                             (lambda t, d, c: lambda e: e.tensor_tensor(out=xdd[t][d][:].rearrange("p (h q) -> p h q", h=8),
                                                                        in0=xbt[t][:, 0:512].rearrange("p (h q) -> p h q", h=8),
                                                                        in1=mult_s[:, c, d * 32 + g * 8:d * 32 + g * 8 + 8].unsqueeze(2).to_broadcast([128, 8, 64]), op=ALU.mult))(t, d, c),
                             reads=[xbt[t].b, mult_s.b], writes=[xdd[t][d].b])

            def stageB2(blk):
                NT, t0, N = blkinfo(blk)
                xbt, xdd = xbt_2[blk % 2], xdd_2[blk % 2]
                for d in range(2):
                    def f(e, d=d):
                        for t in range(NT):
                            r = e.matmul(accS[d][:, :], lhsT=xbt[t][:, 512:640], rhs=xdd[t][d][:], start=(blk == 0 and t == 0), stop=(blk == nblk - 1 and t == NT - 1))
                        return r
                    S.op("pe", f, reads=[xbt[t].b for t in range(NT)] + [xdd[t][d].b for t in range(NT)], writes=[accS[d].b])

            stageA(0)
            for blk in range(nblk):
                stageB(blk)
                if blk + 1 < nblk:
                    stageA(blk + 1)
                if blk == 2 and g + 1 < 4:
                    s2_prefetch(g + 1)
                stageB2(blk)
            for d in range(2):
                S.op("act" if d == 0 else "dve",
                     (lambda d: (lambda e: e.activation(out=sin[:, d, g * 512:(g + 1) * 512], in_=accS[d][:, :], func=AF.Copy)) if d == 0 else
                      (lambda e: e.tensor_copy(out=sin[:, d, g * 512:(g + 1) * 512], in_=accS[d][:, :])))(d),
                     reads=[accS[d].b], writes=[sin.b])
        bank_rr["n"] = 8
        pc_issue(1000)
        S.barrier()
        ps1.close()
        dump("sin", sin[:], [128, 2, 2048], F32, [sin.b])
        if upto <= 1:
            return finish(nc, S, dbg_out)


        pm = ExitStack()
        t_dt = dtraw_o
        adto = sb("adto", [128, NCH, 64], BF16, pm)
        t_ai = sb("t_ai", [128, NCH, 64], F32, pm)
        t_tot = sb("t_tot", [128, NCH, 64], F32, pm)
        nadto = sb("nadto", [128, NCH, 64], BF16, pm)
        t_E = sb("t_E", [128, NCH, 32], F32, pm)
        t_dtd = sb("t_dtd", [128, NCH, 64], F32, pm)
        t_sc = sb("t_sc", [128, NCH, 64], F32, pm)
        softplus_inplace(t_dt, NCH)
        S.op("dve", lambda e: e.tensor_tensor(out=adto[:], in0=t_dt[:], in1=abc[:].unsqueeze(1).to_broadcast([128, NCH, 64]), op=ALU.mult),
             reads=[t_dt.b, abc.b], writes=[adto.b])
        cums(adto, NCH, t_ai, t_tot)
        F_, B_ = slice(0, 32), slice(32, 64)
        S.op("dve", lambda e: e.tensor_scalar(out=nadto[:], in0=adto[:], scalar1=-1.0, scalar2=None, op0=ALU.mult), reads=[adto.b], writes=[nadto.b])
        S.op("dve", lambda e: e.tensor_tensor(out=t_E[:], in0=t_ai[:, :, B_], in1=adto[:, :, B_], op=ALU.subtract), reads=[t_ai.b, adto.b], writes=[t_E.b])
        S.op("dve", lambda e: e.tensor_tensor(out=t_dtd[:, :, F_], in0=t_tot[:, :, F_], in1=t_ai[:, :, F_], op=ALU.subtract), reads=[t_ai.b, t_tot.b], writes=[t_dtd.b])
        S.op("dve", lambda e: e.tensor_copy(out=t_dtd[:, :, B_], in_=t_E[:]), reads=[t_E.b], writes=[t_dtd.b])
        S.op("act", lambda e: e.activation(out=t_dtd[:], in_=t_dtd[:], func=AF.Exp), reads=[t_dtd.b], writes=[t_dtd.b])
        S.op("dve", lambda e: e.tensor_tensor(out=t_dtd[:], in0=t_dtd[:], in1=t_dt[:], op=ALU.mult), reads=[t_dtd.b, t_dt.b], writes=[t_dtd.b])
        S.op("dve", lambda e: e.tensor_copy(out=t_sc[:, :, F_], in_=t_ai[:, :, F_]), reads=[t_ai.b], writes=[t_sc.b])
        S.op("dve", lambda e: e.tensor_tensor(out=t_sc[:, :, B_], in0=t_tot[:, :, B_], in1=t_E[:], op=ALU.subtract), reads=[t_tot.b, t_E.b], writes=[t_sc.b])
        S.op("act", lambda e: e.activation(out=t_sc[:], in_=t_sc[:], func=AF.Exp), reads=[t_sc.b], writes=[t_sc.b])
        S.op("act", lambda e: e.activation(out=t_tot[:], in_=t_tot[:], func=AF.Exp), reads=[t_tot.b], writes=[t_tot.b])
        t_et = t_tot

        wmg = sb("wmg", [128, KT, 768], BF16, pm)
        dgm2 = [sb(f"dgm{i}", [128, 7, 128], BF16, pm) for i in range(2)]
        xprem = sb("xprem", [128, 2056], BF16, pm)
        xc = sb("xc", [128, 6, SEG], BF16, pm)
        dpar = lambda name, shape, dt: [[sb(f"{name}{d}_{i}", shape, dt, pm) for i in range(2)] for d in range(2)]
        xbt2 = dpar("mxbt", [128, 640], BF16)
        L2 = dpar("Lb", [128, 8, 128], BF16)
        M2 = dpar("Mb", [128, 8, 128], BF16)
        xdt2 = dpar("xdt", [128, 512], BF16)
        xdd2 = dpar("mxdd", [128, 512], BF16)
        yfb2 = dpar("yfb", [128, 512], F32)
        Sbf2 = dpar("Sbf", [128, 512], BF16)
        xD2 = [sb(f"xD{i}", [128, 512], BF16, pm) for i in range(2)]
        t1_2 = [sb(f"t1_{d}", [128, 512], F32, pm) for d in range(2)]
        Sst = [sb(f"Sst{d}", [128, 512], F32, pm) for d in range(2)]
        ydram = Buf("yscr")

        def h8(ap):
            return ap.rearrange("p (h q) -> p h q", h=8)

        xc_2 = [xc, sb("xcB", [128, 6, SEG], BF16, pm)]

        def prologue_jobs(g):
            xc_t = xc_2[g % 2]
            tiles = [g * 4 + m for m in range(4)] + [16 + g, 20 + g]
            jobs = []

            def jw():
                wload(wmg, wmg[:, :, 0:512], w_in, O_X + g * 512, 512)
                wload(wmg, wmg[:, :, 512:640], w_in, O_B + g * 128, 128)
                wload(wmg, wmg[:, :, 640:768], w_in, O_C + g * 128, 128)
            jobs.append(jw)
            for m in range(6):
                dg = dgm2[m % 2]
                jobs.append((lambda dg, m: lambda: build_diag(dg, dg[:], tiles[m], 7, scw, "dve"))(dg, m))
                for j in range(5):
                    def ji(m=m, j=j):
                        s0 = 124 + 512 * j
                        w = min(512, 2180 - s0)
                        bk = bank()

                        def f(e):
                            for k in range(KT):
                                r = e.matmul(bk[:, 0:w], lhsT=wmg[:, k, m * 128:(m + 1) * 128], rhs=hnT[:, k, s0:s0 + w], start=(k == 0), stop=(k == KT - 1))
                            return r
                        S.op("pe", f, reads=[wmg.b, hnT.b], writes=[bk.b])
                        S.op("act", lambda e: e.activation(out=xprem[:, 512 * j:512 * j + w], in_=bk[:, 0:w], func=AF.Copy), reads=[bk.b], writes=[xprem.b])
                    jobs.append(ji)
                for q in range(4):
                    def jc(m=m, q=q, dg=dg):
                        bk = bank()

                        def f(e):
                            for tap in range(7):
                                r = e.matmul(bk[:, :], lhsT=dg[:, tap, :], rhs=xprem[:, 1 + 512 * q + tap:1 + 512 * q + tap + 512], start=(tap == 0), stop=(tap == 6))
                            return r
                        S.op("pe", f, reads=[dg.b, xprem.b], writes=[bk.b])
                        tl = tiles[m]
                        S.op("act", lambda e: e.activation(out=xc_t[:, m, q * 512:(q + 1) * 512], in_=bk[:, :], func=AF.Silu, bias=scb[:, tl:tl + 1], scale=1.0),
                             reads=[bk.b, scb.b], writes=[xc_t.b])
                    jobs.append(jc)
            return jobs

        next_jobs = prologue_jobs(0)
        for g in range(4):
            for jb in next_jobs:
                jb()
            next_jobs = prologue_jobs(g + 1) if g + 1 < 4 else []
            xc = xc_2[g % 2]
            if g == 0:
                dump("xc0", xc[:], [128, 6, SEG], BF16, [xc.b])

            sbi = [0, 0]
            for d in range(2):
                S.op("dve", (lambda d: lambda e: e.tensor_copy(out=Sst[d][:], in_=sin[:, d, g * 512:(g + 1) * 512]))(d), reads=[sin.b], writes=[Sst[d].b])
                S.op("act", (lambda d: lambda e: e.activation(out=Sbf2[d][0][:], in_=Sst[d][:], func=AF.Copy))(d), reads=[Sst[d].b], writes=[Sbf2[d][0].b])

            def stage1(d, it):
                c = it if d == 0 else NCH - 1 - it
                par = it % 2
                j0 = d * 32 + g * 8
                umat = uinc if d == 0 else nustr
                neg = negf if d == 0 else negb
                cs = slice(c * 128, (c + 1) * 128)
                xbt, Lb, Mb, xdt, xdd_ = xbt2[d][par], L2[d][par], M2[d][par], xdt2[d][par], xdd2[d][par]
                xD = xD2[par]
                bk = bank()

                def f(e):
                    v = bfv(bk)
                    for m in range(5):
                        e.transpose(out=v[:, m * 128:(m + 1) * 128], in_=xc[:, m, cs], identity=ident[:])
                    return e.matmul(bk[:, 320:448], lhsT=xc[:, 4, cs], rhs=xc[:, 5, cs], start=True, stop=True)
                S.op("pe", f, reads=[xc.b, ident.b], writes=[bk.b])
                S.op("act", lambda e: e.activation(out=xbt[:], in_=bfv(bk)[:, 0:640], func=AF.Copy), reads=[bk.b], writes=[xbt.b])
                for half in range(2):
                    bkl = bank()

                    def f(e, bkl=bkl, half=half):
                        jj = j0 + 4 * half
                        e.matmul(bkl[:, :], lhsT=umat[:], rhs=nadto[:, c, jj:jj + 4].unsqueeze(2).to_broadcast([128, 4, 128]), start=True, stop=False)
                        e.matmul(bkl[:, :], lhsT=ident[:], rhs=neg[:], start=False, stop=False)
                        for hh in range(4):
                            r = e.matmul(bkl[:, hh * 128:(hh + 1) * 128], lhsT=adto[:, c, jj + hh:jj + hh + 1].to_broadcast([128, 128]), rhs=umat[:], start=False, stop=(hh == 3))
                        return r
                    S.op("pe", f, reads=[umat.b, ident.b, neg.b, adto.b, nadto.b], writes=[bkl.b])
                    S.op("act", (lambda bkl, half: lambda e: e.activation(out=Lb[:, half * 4:(half + 1) * 4, :], in_=bkl[:, :].rearrange("p (h l) -> p h l", h=4), func=AF.Exp))(bkl, half),
                         reads=[bkl.b], writes=[Lb.b])
                S.op("dve", lambda e: e.tensor_tensor(out=Mb[:], in0=Lb[:], in1=bk[:, 320:448].unsqueeze(1).to_broadcast([128, 8, 128]), op=ALU.mult),
                     reads=[bk.b, Lb.b], writes=[Mb.b])
                S.op("pool", lambda e: e.tensor_tensor(out=h8(xdt[:]), in0=h8(xbt[:, 0:512]), in1=t_dt[:, c, j0:j0 + 8].unsqueeze(2).to_broadcast([128, 8, 64]), op=ALU.mult),
                     reads=[xbt.b, t_dt.b], writes=[xdt.b])
                S.op("pool", lambda e: e.tensor_tensor(out=h8(xdd_[:]), in0=h8(xbt[:, 0:512]), in1=t_dtd[:, c, j0:j0 + 8].unsqueeze(2).to_broadcast([128, 8, 64]), op=ALU.mult),
                     reads=[xbt.b, t_dtd.b], writes=[xdd_.b])
                if d == 0:
                    S.op("pool", lambda e: e.tensor_tensor(out=h8(xD[:]), in0=h8(xbt[:, 0:512]), in1=dsk[:, g * 8:g * 8 + 8].unsqueeze(2).to_broadcast([128, 8, 64]), op=ALU.mult),
                         reads=[xbt.b, dsk.b], writes=[xD.b])

            def stage2(d, it):
                c = it if d == 0 else NCH - 1 - it
                par = it % 2
                j0 = d * 32 + g * 8
                cs = slice(c * 128, (c + 1) * 128)
                xbt, Mb, xdt, xdd_ = xbt2[d][par], M2[d][par], xdt2[d][par], xdd2[d][par]
                xD = xD2[par]
                Sd, t1 = Sst[d], t1_2[d]
                bks = bank()
                S.op("pe", lambda e: e.matmul(bks[:, :], lhsT=xbt[:, 512:640], rhs=xdd_[:], start=True, stop=True), reads=[xbt.b, xdd_.b], writes=[bks.b])
                bko = bank()
                Sb_cur = Sbf2[d][sbi[d]]
                S.op("pe", lambda e: e.matmul(bko[:, :], lhsT=xc[:, 5, cs], rhs=Sb_cur[:], start=True, stop=True), reads=[xc.b, Sb_cur.b], writes=[bko.b])
                bky = bank()
                if d == 0:
                    def f(e):
                        e.matmul(bky[:, :], lhsT=ident[:], rhs=xD[:], start=True, stop=False)
                        for h in range(8):
                            r = e.matmul(bky[:, h * 64:(h + 1) * 64], lhsT=Mb[:, h, :], rhs=xdt[:, h * 64:(h + 1) * 64], start=False, stop=(h == 7))
                        return r
                    S.op("pe", f, reads=[ident.b, Mb.b, xdt.b, xD.b], writes=[bky.b])
                else:
                    def f(e):
                        for h in range(8):
                            r = e.matmul(bky[:, h * 64:(h + 1) * 64], lhsT=Mb[:, h, :], rhs=xdt[:, h * 64:(h + 1) * 64], start=True, stop=True)
                        return r
                    S.op("pe", f, reads=[Mb.b, xdt.b], writes=[bky.b])
                if it + 1 < NCH:
                    S.op("dve", lambda e: e.tensor_tensor(out=h8(Sd[:]), in0=h8(Sd[:]), in1=t_et[:, c, j0:j0 + 8].unsqueeze(2).to_broadcast([128, 8, 64]), op=ALU.mult),
                         reads=[Sd.b, t_et.b], writes=[Sd.b])
                    S.op("dve", lambda e: e.tensor_tensor(out=Sd[:], in0=Sd[:], in1=bks[:, :], op=ALU.add), reads=[Sd.b, bks.b], writes=[Sd.b])
                    sbi[d] = 1 - sbi[d]
                    nxt = Sbf2[d][sbi[d]]
                    S.op("act", lambda e: e.activation(out=nxt[:], in_=Sd[:], func=AF.Copy), reads=[Sd.b], writes=[nxt.b])
                S.op("dve", lambda e: e.tensor_tensor(out=h8(t1[:]), in0=h8(bko[:, :]), in1=t_sc[:, c, j0:j0 + 8].unsqueeze(2).to_broadcast([128, 8, 64]), op=ALU.mult),
                     reads=[bko.b, t_sc.b], writes=[t1.b])
                yfb = yfb2[d][par]
                S.op("dve", lambda e: e.tensor_tensor(out=yfb[:], in0=t1[:], in1=bky[:, :], op=ALU.add), reads=[t1.b, bky.b], writes=[yfb.b])
                if g == 0 and c in (15, 0):
                    dump("y%d_%d" % (d, c), yfb[:], [128, 512], F32, [yfb.b])
                S.dma("sp", "yout%d_%d" % (d, par), lambda e: [e.dma_start(out=yscr[d, g, c], in_=yfb[:])], reads=[yfb.b], writes=[ydram])

            stage1(0, 0)
            stage1(1, 0)
            for it in range(NCH):
                if it + 1 < NCH:
                    stage1(0, it + 1)
                    stage1(1, it + 1)
                stage2(0, it)
                for _ in range(2):
                    if next_jobs:
                        next_jobs.pop(0)()
                stage2(1, it)
                for _ in range(2):
                    if next_jobs:
                        next_jobs.pop(0)()
        S.barrier()
        pm.close()
        pA.close()
        if upto <= 2:
            return finish(nc, S, dbg_out)


        cvscr = nc.dram_tensor("cvscr", [8, 128, SEG], BF16).ap()
        cvdram = Buf("cvscr")
        pc = ExitStack()
        cdw = sb("cdw", [128, 8, 31], F32, pc)
        dgc2 = [sb(f"dgc{i}", [128, 31, 128], BF16, pc) for i in range(2)]
        wc2 = [sb(f"wc{i}", [128, KT, 256], BF16, pc) for i in range(2)]
        abuf2 = [sb(f"abuf{i}", [128, 2080], BF16, pc) for i in range(2)]
        sgc = sb("sgc", [128, 512], F32, pc)
        cvt2 = [sb(f"cvo{i}", [128, 512], BF16, pc) for i in range(2)]
        S.dma("sp", "cdw", lambda e: [e.dma_start(out=cdw[:], in_=cdw_d)], writes=[cdw.b])
        for ct in range(8):
            wc, dgc, abuf = wc2[ct % 2], dgc2[ct % 2], abuf2[ct % 2]
            wload(wc, wc[:, :, 0:128], w_in, O_VAL + ct * 128, 128)
            wload(wc, wc[:, :, 128:256], w_in, O_GATE + ct * 128, 128)
            for tap in range(31):
                S.op("dve",
                     (lambda tap: lambda e: e.tensor_scalar(out=dgc[:, tap, :], in0=ident[:], scalar1=cdw[:, ct, tap:tap + 1], scalar2=None, op0=ALU.mult))(tap),
                     reads=[ident.b, cdw.b], writes=[dgc.b])
            for j in range(5):
                s0 = 112 + 512 * j
                w = min(512, 2192 - s0)
                bv, bg = bank(), bank()
                for (bk_, off) in ((bv, 0), (bg, 128)):
                    def f(e, bk_=bk_, off=off, s0=s0, w=w):
                        for k in range(KT):
                            r = e.matmul(bk_[:, 0:w], lhsT=wc[:, k, off:off + 128], rhs=hnT[:, k, s0:s0 + w], start=(k == 0), stop=(k == KT - 1))
                        return r
                    S.op("pe", f, reads=[wc.b, hnT.b], writes=[bk_.b])
                S.op("act", (lambda bg, w: lambda e: e.activation(out=sgc[:, 0:w], in_=bg[:, 0:w], func=AF.Sigmoid))(bg, w), reads=[bg.b], writes=[sgc.b])
                S.op("dve", (lambda bv, j, w: lambda e: e.tensor_tensor(out=abuf[:, 512 * j:512 * j + w], in0=bv[:, 0:w], in1=sgc[:, 0:w], op=ALU.mult))(bv, j, w),
                     reads=[bv.b, sgc.b], writes=[abuf.b])
            for q in range(4):
                bk = bank()

                def f(e, bk=bk, q=q):
                    for tap in range(31):
                        r = e.matmul(bk[:, :], lhsT=dgc[:, tap, :], rhs=abuf[:, 1 + 512 * q + tap:1 + 512 * q + tap + 512], start=(tap == 0), stop=(tap == 30))
                    return r
                S.op("pe", f, reads=[dgc.b, abuf.b], writes=[bk.b])
                cvt = cvt2[q % 2]
                S.op("act", (lambda bk, cvt: lambda e: e.activation(out=cvt[:], in_=bk[:, :], func=AF.Identity, bias=vec8[:, 0, ct:ct + 1], scale=1.0))(bk, cvt),
                     reads=[bk.b, vec8.b], writes=[cvt.b])
                S.dma("sp", "cvo%d" % (q % 2), (lambda cvt, q: lambda e: [e.dma_start(out=cvscr[ct, :, q * 512:(q + 1) * 512], in_=cvt[:])])(cvt, q), reads=[cvt.b], writes=[cvdram])
        S.barrier()
        pc.close()
        if upto <= 3:
            return finish(nc, S, dbg_out)

        pb = ExitStack()
        hs2 = sb("hs2", [128, 4, D], F32, pb)
        wb2 = [sb(f"wb{i}", [128, 4096], BF16, pb) for i in range(3)]
        wbc = {"i": 0}

        def wchunk(W, c0, ncols, kt=KT, r0=0):
            wb = wb2[wbc["i"] % 3]
            wbc["i"] += 1
            view = wb[:, 0:kt * ncols].rearrange("p (k c) -> p k c", k=kt)
            Wd_, Wb_ = W
            src = Wd_[r0:r0 + kt * 128, c0:c0 + ncols].rearrange("(k p) c -> p k c", p=128)
            S.dma("sp", "w_" + wb.b.name, lambda e: [e.dma_start(out=view, in_=src)], reads=[Wb_], writes=[wb.b])
            return wb, view

        for T_ in range(4):
            tok0 = 512 * T_
            sc0 = 128 + tok0
            tcol = slice(sc0, sc0 + 512)
            pmix = ExitStack()
            cvt = sb("cvt", [128, 8, 512], BF16, pmix)
            aact = sb("aact", [128, 8, 512], BF16, pmix)
            mc = sb("mc", [128, 8, 512], F32, pmix)
            vT = sb("vT", [128, 16, 512], BF16, pmix)
            merged = sb("merged", [128, 8, 512], BF16, pmix)
            lnm = sb("lnm", [128, 512], F32, pmix)
            lnr = sb("lnr", [128, 512], F32, pmix)
            lt = sb("lt", [128, 512], F32, pmix)
            lt2 = sb("lt2", [128, 512], F32, pmix)
            sgb = sb("sgb", [128, 512], F32, pmix)
            sq2 = [sb(f"sq{i}", [128, 512], BF16, pmix) for i in range(2)]
            S.dma("sp", "cvt", lambda e: [e.dma_start(out=cvt[:], in_=cvscr[:, :, tok0:tok0 + 512].rearrange("c p t -> p c t"))], reads=[cvdram], writes=[cvt.b])
            NBUF = 4
            normw = sb("normw", [128, 2048], F32, pmix)
            S.dma("sp", "normw", lambda e: [e.dma_start(out=normw[:], in_=normw_d.partition_broadcast(128))], writes=[normw.b])
            yl2 = [sb(f"yl{i}", [128, 2, 512], F32, pmix) for i in range(NBUF)]
            szb2 = [sb(f"szb{i}", [128, 512], F32, pmix) for i in range(NBUF)]
            vb2 = [sb(f"vb{i}", [128, 512], F32, pmix) for i in range(NBUF)]
            vnb2 = [sb(f"vnb{i}", [128, 512], BF16, pmix) for i in range(NBUF)]
            vjunk = sb("vjunk", [128, 512], BF16, pmix)
            vst2 = [sb(f"vst{i}", [128, 4], F32, pmix) for i in range(NBUF)]
            wz_cache = {}
            wzb2 = [sb(f"wzb{i}", [128, 4096], BF16, pmix) for i in range(2)]

            def gateZ(u):
                g, j = u // 4, u % 4
                if g not in wz_cache:
                    wzb = wzb2[g % 2]
                    wzview = wzb[:].rearrange("p (k c) -> p k c", k=KT)
                    S.dma("sp", "w_" + wzb.b.name, (lambda wzview, g: lambda e: [e.dma_start(out=wzview, in_=bf_z[0][:, g * 512:(g + 1) * 512].rearrange("(k p) c -> p k c", p=128))])(wzview, g),
                          reads=[bf_z[1]], writes=[wzb.b])
                    wz_cache[g] = (wzb, wzview)
                wz_, wzv = wz_cache[g]
                c = 4 * T_ + j
                i2 = u % NBUF
                yl, szb, vb, vnb, vst = yl2[i2], szb2[i2], vb2[i2], vnb2[i2], vst2[i2]
                S.dma("sp", "yl%d" % i2, lambda e: [e.dma_start(out=yl[:], in_=yscr[:, g, c].rearrange("d p f -> p d f"))], reads=[ydram], writes=[yl.b])
                bkz = bank()

                def f(e):
                    for k in range(KT):
                        r = e.matmul(bkz[:, :], lhsT=hnT[:, k, sc0 + j * 128:sc0 + (j + 1) * 128], rhs=wzv[:, k, :], start=(k == 0), stop=(k == KT - 1))
                    return r
                S.op("pe", f, reads=[hnT.b, wz_.b], writes=[bkz.b])
                S.op("act", lambda e: e.activation(out=szb[:], in_=bkz[:, :], func=AF.Silu), reads=[bkz.b], writes=[szb.b])
                S.op("dve", lambda e: e.tensor_tensor(out=vb[:], in0=yl[:, 0, :], in1=yl[:, 1, :], op=ALU.add), reads=[yl.b], writes=[vb.b])
                S.op("dve", lambda e: e.tensor_tensor(out=vb[:], in0=vb[:], in1=szb[:], op=ALU.mult), reads=[vb.b, szb.b], writes=[vb.b])
                S.op("act", lambda e: e.activation(out=vjunk[:], in_=vb[:], func=AF.Square, accum_out=vst[:, 0:1]), reads=[vb.b], writes=[vjunk.b, vst.b])
                rstd_of(vst, vst[:, 0:1], 512)
                S.op("dve", lambda e: e.scalar_tensor_tensor(out=vnb[:], in0=vb[:], scalar=vst[:, 2:3], in1=normw[:, g * 512:(g + 1) * 512], op0=ALU.mult, op1=ALU.mult),
                     reads=[vb.b, vst.b, normw.b], writes=[vnb.b])

            def gateT(u):
                g, j = u // 4, u % 4
                vnb = vnb2[u % NBUF]
                bkt = bank()

                def f(e):
                    v = bfv(bkt)
                    for m in range(4):
                        r = e.transpose(out=v[:, m * 128:(m + 1) * 128], in_=vnb[:, m * 128:(m + 1) * 128], identity=ident[:])
                    return r
                S.op("pe", f, reads=[vnb.b, ident.b], writes=[bkt.b])
                S.op("act", lambda e: e.activation(out=vT[:, g * 4:(g + 1) * 4, j * 128:(j + 1) * 128], in_=bfv(bkt)[:, 0:512].rearrange("p (m t) -> p m t", m=4), func=AF.Copy),
                     reads=[bkt.b], writes=[vT.b])

            gate_sched = []
            for u in range(16 + 3):
                if u < 16:
                    gate_sched.append(("Z", u))
                if u >= 3:
                    gate_sched.append(("T", u - 3))

            def gate_emit(n):
                for _ in range(n):
                    if gate_sched:
                        kind, u = gate_sched.pop(0)
                        (gateZ if kind == "Z" else gateT)(u)
            bm, bq = bank(), bank()

            def f(e):
                for ct in range(8):
                    r = e.matmul(bm[:, :], lhsT=onesm[:], rhs=cvt[:, ct, :], start=(ct == 0), stop=(ct == 7))
                return r
            S.op("pe", f, reads=[onesm.b, cvt.b], writes=[bm.b])
            for ct in range(8):
                sq = sq2[ct % 2]
                S.op("act", (lambda sq, ct: lambda e: e.activation(out=sq[:], in_=cvt[:, ct, :], func=AF.Square))(sq, ct), reads=[cvt.b], writes=[sq.b])
                S.op("pe", (lambda sq, ct: lambda e: e.matmul(bq[:, :], lhsT=onesm[:], rhs=sq[:], start=(ct == 0), stop=(ct == 7)))(sq, ct), reads=[onesm.b, sq.b], writes=[bq.b])
            S.op("act", lambda e: e.activation(out=lnm[:], in_=bm[:, :], func=AF.Copy), reads=[bm.b], writes=[lnm.b])
            S.op("dve", lambda e: e.tensor_tensor(out=lt[:], in0=lnm[:], in1=lnm[:], op=ALU.mult), reads=[lnm.b], writes=[lt.b])
            S.op("dve", lambda e: e.tensor_tensor(out=lt[:], in0=bq[:, :], in1=lt[:], op=ALU.subtract), reads=[bq.b, lt.b], writes=[lt.b])
            S.op("act", lambda e: e.activation(out=lt[:], in_=lt[:], func=AF.Sqrt, bias=epsc[:, 0:1], scale=1.0), reads=[lt.b, epsc.b], writes=[lt.b])
            S.op("dve", lambda e: e.reciprocal(out=lnr[:], in_=lt[:]), reads=[lt.b], writes=[lnr.b])
            gate_emit(6)
            for ct in range(8):
                S.op("dve", (lambda ct: lambda e: e.tensor_tensor(out=lt2[:], in0=cvt[:, ct, :], in1=lnm[:], op=ALU.subtract))(ct), reads=[cvt.b, lnm.b], writes=[lt2.b])
                S.op("dve", lambda e: e.tensor_tensor(out=lt2[:], in0=lt2[:], in1=lnr[:], op=ALU.mult), reads=[lt2.b, lnr.b], writes=[lt2.b])
                S.op("act", (lambda ct: lambda e: e.activation(out=aact[:, ct, :], in_=lt2[:], func=AF.Silu, bias=vec8[:, 2, ct:ct + 1], scale=vec8[:, 1, ct:ct + 1]))(ct),
                     reads=[lt2.b, vec8.b], writes=[aact.b])
            if T_ == 0:
                dump("aact0", aact[:], [128, 8, 512], BF16, [aact.b])
            for half in range(2):
                wa, wav = wchunk(bf_co, half * 512, 512)
                wg_, wgv = wchunk(bf_gg, half * 512, 512)
                for mm in range(4):
                    m = half * 4 + mm
                    by, bg = bank(), bank()

                    def f(e, by=by, mm=mm, wav=wav):
                        for k in range(KT):
                            r = e.matmul(by[:, :], lhsT=wav[:, k, mm * 128:(mm + 1) * 128], rhs=aact[:, k, :], start=(k == 0), stop=(k == KT - 1))
                        return r
                    S.op("pe", f, reads=[wa.b, aact.b], writes=[by.b])

                    def f(e, bg=bg, mm=mm, wgv=wgv):
                        for k in range(KT):
                            r = e.matmul(bg[:, :], lhsT=wgv[:, k, mm * 128:(mm + 1) * 128], rhs=hnT[:, k, tcol], start=(k == 0), stop=(k == KT - 1))
                        return r
                    S.op("pe", f, reads=[wg_.b, hnT.b], writes=[bg.b])
                    S.op("act", (lambda bg: lambda e: e.activation(out=sgb[:], in_=bg[:, :], func=AF.Sigmoid))(bg), reads=[bg.b], writes=[sgb.b])
                    S.op("dve", (lambda by, m: lambda e: e.tensor_tensor(out=mc[:, m, :], in0=by[:, :], in1=sgb[:], op=ALU.mult))(by, m), reads=[by.b, sgb.b], writes=[mc.b])
                    gate_emit(4)
            gate_emit(100)
            for half in range(2):
                wg_, wgv = wchunk(bf_gg, 1024 + half * 512, 512)
                for pr in range(2):
                    wa, wav = wchunk(bf_so, half * 512 + pr * 256, 256, kt=16)
                    for mm in range(2):
                        m = half * 4 + pr * 2 + mm
                        gm = pr * 2 + mm
                        by, bg = bank(), bank()

                        def f(e, by=by, mm=mm, wav=wav):
                            for k in range(16):
                                r = e.matmul(by[:, :], lhsT=wav[:, k, mm * 128:(mm + 1) * 128], rhs=vT[:, k, :], start=(k == 0), stop=(k == 15))
                            return r
                        S.op("pe", f, reads=[wa.b, vT.b], writes=[by.b])

                        def f(e, bg=bg, gm=gm, wgv=wgv):
                            for k in range(KT):
                                r = e.matmul(bg[:, :], lhsT=wgv[:, k, gm * 128:(gm + 1) * 128], rhs=hnT[:, k, tcol], start=(k == 0), stop=(k == KT - 1))
                            return r
                        S.op("pe", f, reads=[wg_.b, hnT.b], writes=[bg.b])
                        S.op("act", (lambda bg: lambda e: e.activation(out=sgb[:], in_=bg[:, :], func=AF.Sigmoid))(bg), reads=[bg.b], writes=[sgb.b])
                        S.op("dve", (lambda by: lambda e: e.tensor_tensor(out=lt[:], in0=by[:, :], in1=sgb[:], op=ALU.mult))(by), reads=[by.b, sgb.b], writes=[lt.b])
                        S.op("dve", (lambda m: lambda e: e.tensor_tensor(out=merged[:, m, :], in0=lt[:], in1=mc[:, m, :], op=ALU.add))(m), reads=[lt.b, mc.b], writes=[merged.b])
            if T_ == 0:
                dump("merged0", merged[:], [128, 8, 512], BF16, [merged.b])
            for j in range(4):
                S.dma("sp", "xr%d" % (j % 2), (lambda j: lambda e: [e.dma_start(out=hs2[:, j, :], in_=xs[sc0 + j * 128:sc0 + (j + 1) * 128, :])])(j), writes=[hs2.b])
            for nh in range(2):
                wa, wav = wchunk(bf_wo, nh * 512, 512)
                for j in range(4):
                    bk = bank()

                    def f(e, bk=bk, j=j, wav=wav):
                        for k in range(KT):
                            r = e.matmul(bk[:, :], lhsT=merged[:, k, j * 128:(j + 1) * 128], rhs=wav[:, k, :], start=(k == 0), stop=(k == KT - 1))
                        return r
                    S.op("pe", f, reads=[wa.b, merged.b], writes=[bk.b])
                    S.op("dve", (lambda bk, j, nh: lambda e: e.tensor_tensor(out=hs2[:, j, nh * 512:(nh + 1) * 512], in0=bk[:, :], in1=hs2[:, j, nh * 512:(nh + 1) * 512], op=ALU.add))(bk, j, nh),
                         reads=[bk.b, hs2.b], writes=[hs2.b])
            if T_ == 0:
                dump("hs2_0", hs2[:], [128, 4, D], F32, [hs2.b])
            S.barrier()
            pmix.close()
            pf = ExitStack()
            hn2T = sb("hn2T", [128, KT, 512], BF16, pf)
            actT = sb("actT", [128, FT, 512], BF16, pf)
            wd = sb("wd", [128, FT, D], BF16, pf)
            sgf = sb("sgf", [128, 512], F32, pf)
            ob2 = [sb(f"ob{i}", [128, D], F32, pf) for i in range(2)]
            nfin = sb("nfin", [128, D], F32, pf)
            xn2 = [sb(f"fxn{i}", [128, D], BF16, pf) for i in range(2)]
            junk = sb("fjunk", [128, D], BF16, pf)
            st2 = [sb(f"fst{i}", [128, 4], F32, pf) for i in range(2)]
            xt2 = [sb(f"hs3_{i}", [128, D], F32, pf) for i in range(2)]
            S.dma("sp", "nfin", lambda e: [e.dma_start(out=nfin[:], in_=nfin_d.partition_broadcast(128))], writes=[nfin.b])
            for (f0, nf) in ((0, 8), (8, 8), (16, 6)):
                S.dma("sp", "wd", (lambda f0, nf: lambda e: [e.dma_start(out=wd[:, f0:f0 + nf, :], in_=bf_wd[0][f0 * 128:(f0 + nf) * 128, :].rearrange("(k p) c -> p k c", p=128))])(f0, nf),
                      reads=[bf_wd[1]], writes=[wd.b])
            for j in range(4):
                i = j % 2
                xn, st = xn2[i], st2[i]
                S.op("act", (lambda st, j: lambda e: e.activation(out=junk[:], in_=hs2[:, j, :], func=AF.Square, accum_out=st[:, 0:1]))(st, j), reads=[hs2.b], writes=[junk.b, st.b])
                rstd_of(st, st[:, 0:1], D)
                S.op("act", (lambda xn, st, j: lambda e: e.activation(out=xn[:], in_=hs2[:, j, :], func=AF.Copy, scale=st[:, 2:3]))(xn, st, j), reads=[hs2.b, st.b], writes=[xn.b])
                bk = bank()

                def f(e, bk=bk, xn=xn):
                    v = bfv(bk)
                    for k in range(KT):
                        r = e.transpose(out=v[:, k * 128:(k + 1) * 128], in_=xn[:, k * 128:(k + 1) * 128], identity=ident[:])
                    return r
                S.op("pe", f, reads=[xn.b, ident.b], writes=[bk.b])
                S.op("dve", (lambda bk, j: lambda e: e.tensor_tensor(out=hn2T[:, :, j * 128:(j + 1) * 128], in0=bfv(bk).rearrange("p (k t) -> p k t", k=KT),
                                                                       in1=vec8[:, 4, :].unsqueeze(2).to_broadcast([128, KT, 128]), op=ALU.mult))(bk, j),
                     reads=[bk.b, vec8.b], writes=[hn2T.b])
            for (f0, nf) in ((0, 4), (4, 4), (8, 4), (12, 4), (16, 4), (20, 2)):
                wg_, wgv = wchunk(bf_wg, f0 * 128, nf * 128)
                wu_, wuv = wchunk(bf_wu, f0 * 128, nf * 128)
                for fl in range(nf):
                    fi = f0 + fl
                    bg, bu = bank(), bank()
                    for (bk_, wv_, wt_) in ((bg, wgv, wg_), (bu, wuv, wu_)):
                        def f(e, bk_=bk_, wv_=wv_, fl=fl):
                            for k in range(KT):
                                r = e.matmul(bk_[:, :], lhsT=wv_[:, k, fl * 128:(fl + 1) * 128], rhs=hn2T[:, k, :], start=(k == 0), stop=(k == KT - 1))
                            return r
                        S.op("pe", f, reads=[wt_.b, hn2T.b], writes=[bk_.b])
                    S.op("act", (lambda bg: lambda e: e.activation(out=sgf[:], in_=bg[:, :], func=AF.Silu))(bg), reads=[bg.b], writes=[sgf.b])
                    S.op("dve", (lambda bu, fi: lambda e: e.tensor_tensor(out=actT[:, fi, :], in0=bu[:, :], in1=sgf[:], op=ALU.mult))(bu, fi), reads=[bu.b, sgf.b], writes=[actT.b])
            for j in range(4):
                i = j % 2
                hs3, ob, st = xt2[i], ob2[i], st2[i]
                for nh in range(2):
                    bk = bank()

                    def f(e, bk=bk, j=j, nh=nh):
                        for fi in range(FT):
                            r = e.matmul(bk[:, :], lhsT=actT[:, fi, j * 128:(j + 1) * 128], rhs=wd[:, fi, nh * 512:(nh + 1) * 512], start=(fi == 0), stop=(fi == FT - 1))
                        return r
                    S.op("pe", f, reads=[actT.b, wd.b], writes=[bk.b])
                    S.op("dve", (lambda bk, hs3, j, nh: lambda e: e.tensor_tensor(out=hs3[:, nh * 512:(nh + 1) * 512], in0=bk[:, :], in1=hs2[:, j, nh * 512:(nh + 1) * 512], op=ALU.add))(bk, hs3, j, nh),
                         reads=[bk.b, hs2.b], writes=[hs3.b])
                S.op("act", (lambda hs3, st: lambda e: e.activation(out=junk[:], in_=hs3[:], func=AF.Square, accum_out=st[:, 0:1]))(hs3, st), reads=[hs3.b], writes=[junk.b, st.b])
                rstd_of(st, st[:, 0:1], D)
                S.op("dve", (lambda hs3, ob, st: lambda e: e.scalar_tensor_tensor(out=ob[:], in0=hs3[:], scalar=st[:, 2:3], in1=nfin[:], op0=ALU.mult, op1=ALU.mult))(hs3, ob, st),
                     reads=[hs3.b, st.b, nfin.b], writes=[ob.b])
                S.dma("act", "out%d" % i, (lambda ob, j: lambda e: [e.dma_start(out=out_d[tok0 + j * 128:tok0 + (j + 1) * 128, :], in_=ob[:])])(ob, j), reads=[ob.b], writes=[Buf("outd")])
            S.barrier()
            pf.close()
        pb.close()
        return finish(nc, S, dbg_out)
    return nc


def finish(nc, S, dbg_out):
    S.barrier()
    with nc.Block() as block:
        @block.tensor
        def _(e):
            e.nop()

        @block.scalar
        def _(e):
            e.nop()

        @block.vector
        def _(e):
            e.nop()

        @block.gpsimd
        def _(e):
            e.nop()

        @block.sync
        def _(e):
            e.nop()
    nc._sched_counts = (dict(S.cnt), dict(S.dcnt))
    return nc


def _consts():
    k = np.arange(128)[:, None]
    l = np.arange(128)[None, :]
    ident = np.eye(128, dtype=np.float32)
    uinc = (k <= l).astype(np.float32)
    nustr = -(k < l).astype(np.float32)
    negf = np.where(l < k, -30000.0, 0.0).astype(np.float32)
    negb = np.where(l > k, -30000.0, 0.0).astype(np.float32)
    return dict(cident=ident, cuinc=uinc, cnustr=nustr, cnegf=np.tile(negf, (1, 4)), cnegb=np.tile(negb, (1, 4)))


def prep_inputs(inputs):
    f = lambda a: np.ascontiguousarray(np.asarray(a, dtype=np.float32))
    x = f(inputs["x"])
    meta = f(inputs["meta_tokens"])
    pt = lambda v: np.ascontiguousarray(f(v).reshape(-1, 128).T)
    shared = dict(
        w_in=f(inputs["w_in"][0]), conv_out_w=f(inputs["conv_out_w"][0]), ssm_out_w=f(inputs["ssm_out_w"][0]),
        w_o=f(inputs["w_o"][0]), w_gate=f(inputs["w_gate"][0]), w_up=f(inputs["w_up"][0]), w_down=f(inputs["w_down"][0]),
        cdw=np.ascontiguousarray(f(inputs["conv_dw_w"][0]).T.reshape(8, 128, 31).transpose(1, 0, 2)),
        scw=np.ascontiguousarray(f(inputs["ssm_conv_w"][0]).T.reshape(24, 128, 7).transpose(1, 0, 2)),
        vec8=np.ascontiguousarray(np.stack([pt(inputs["conv_dw_b"][0]), pt(inputs["conv_ln_g"][0]), pt(inputs["conv_ln_b"][0]),
                                            pt(inputs["norm_mix"][0]), pt(inputs["norm_ffn"][0])], axis=1)),
        scb=pt(inputs["ssm_conv_b"][0]),
        rows=np.concatenate([f(inputs["dt_bias_f"][0]), f(inputs["dt_bias_b"][0]), f(inputs["a_log_f"][0]), f(inputs["a_log_b"][0])])[None, :].copy(),
        dsk=f(inputs["ssm_d"][0])[None, :].copy(),
        normw=f(inputs["ssm_norm_w"][0])[None, :].copy(),
        nfin=f(inputs["norm_final"])[None, :].copy(),
        hmask=np.concatenate([np.zeros(112, np.float32), np.ones(16, np.float32)])[:, None].copy(),
    )
    shared.update(_consts())
    in_maps = []
    for core in range(8):
        b, k = core // 4, core % 4
        xs = np.zeros((SLAB, D), np.float32)
        xs[128:2176] = x[b, k * SEG:(k + 1) * SEG]
        if k == 0:
            xs[112:128] = meta
        else:
            xs[0:128] = x[b, k * SEG - 128:k * SEG]
        if k < 3:
            xs[2176:2304] = x[b, (k + 1) * SEG:(k + 1) * SEG + 128]
        xo = np.zeros(((NOT + 1) * 128, D), np.float32)
        xo[112:128] = meta
        hal = xo[NOT * 128:]
        hal[3:6] = x[b, 0:3]
        mrow = np.zeros((1, 2 * NOT), np.float32)
        mrow[0, 0] = 1.0
        others = [j for j in range(4) if j != k]
        for bi, j in enumerate(others):
            for q in range(4):
                blk = 1 + bi * 4 + q
                t0 = 1 + 4 * (blk - 1)
                st = j * SEG + q * 512
                xo[t0 * 128:(t0 + 4) * 128] = x[b, st:st + 512]
                hal[blk * 6:blk * 6 + 3] = meta[13:16] if st == 0 else x[b, st - 3:st]
                if st + 512 < L:
                    hal[blk * 6 + 3:blk * 6 + 6] = x[b, st + 512:st + 515]
                mrow[0, t0:t0 + 4] = 1.0 if j < k else 0.0
                mrow[0, NOT + t0:NOT + t0 + 4] = 1.0 if j > k else 0.0
        m = dict(shared)
        m.update(xs=xs, xo=xo, mrow=mrow)
        in_maps.append(m)
    return in_maps


def kernel(**inputs):
    in_maps = prep_inputs(inputs)
    nc = build_nc()
    res = run_bass_kernel_spmd(nc, in_maps, core_ids=list(range(8)))
    out = np.empty((NB, L, D), np.float32)
    for core in range(8):
        b, k = core // 4, core % 4
        out[b, k * SEG:(k + 1) * SEG] = res.results[core]["out"]
    return out
```

```python
import numpy as np
from contextlib import ExitStack
import concourse.bass as bass
import concourse.mybir as mybir
from concourse.bass_utils import run_bass_kernel_spmd

F32 = mybir.dt.float32
BF16 = mybir.dt.bfloat16
AF = mybir.ActivationFunctionType
ALU = mybir.AluOpType

D = 1024
KT = 8
L = 8192
NB = 2
SEG = 2048
NCH = 16
SLAB = 2304
NOT = 49
DFF = 2816
FT = 22
EPS = 1e-6
O_VAL, O_GATE, O_Z, O_X, O_B, O_C, O_DT, O_GC, O_GS = 0, 1024, 2048, 4096, 6144, 6656, 7168, 7232, 8256

ENGS = ["pe", "act", "dve", "pool", "sp"]


class Buf:
    __slots__ = ("name", "w", "r", "excl")

    def __init__(self, name, excl=False):
        self.name = name
        self.excl = excl
        self.w = None
        self.r = []


class Sched:
    def __init__(self, nc, es):
        self.nc = nc
        self.es = es
        self.sems = {}
        for e in ENGS:
            self.sems[e] = es.enter_context(nc.semaphore(f"sem_{e}"))
        self.cnt = {e: 0 for e in ENGS}
        self.seen = {e: {} for e in ENGS}
        self.dcnt = {}

    def dsem(self, name):
        key = f"d_{name}"
        if key not in self.sems:
            self.sems[key] = self.es.enter_context(self.nc.semaphore(f"dsem_{name}"))
            self.dcnt[key] = 0
        return key

    def _deps(self, eng, reads, writes):
        deps = {}

        def add(ev):
            if ev is None:
                return
            k, v = ev
            if deps.get(k, 0) < v:
                deps[k] = v
        for b in reads:
            add(b.w)
            if b.excl:
                for ev in b.r:
                    add(ev)
        for b in writes:
            add(b.w)
            for ev in b.r:
                add(ev)
        waits = []
        seen = self.seen[eng]
        for k, v in deps.items():
            if seen.get(k, 0) < v:
                seen[k] = v
                waits.append((k, v))
        return waits

    def _mark(self, ev, reads, writes):
        for b in reads:
            if b.excl:
                b.w = ev
                b.r = []
            else:
                b.r = [x for x in b.r if x[0] != ev[0]] + [ev]
        for b in writes:
            b.w = ev
            b.r = []

    def _emit(self, name, waits, fn, dkey, n):
        nc = self.nc
        e = {"pe": nc.tensor, "act": nc.scalar, "dve": nc.vector, "pool": nc.gpsimd, "sp": nc.sync}[name]
        sems = self.sems
        for k, v in waits:
            e.wait_ge(sems[k], v)
        if fn is None:
            return
        r = fn(e)
        if dkey is None:
            r.then_inc(sems[name], 1)
        else:
            if not isinstance(r, (list, tuple)):
                r = [r]
            assert len(r) == n, (len(r), n)
            for ins in r:
                ins.then_inc(sems[dkey], 16)

    def op(self, eng, fn, reads=(), writes=()):
        waits = self._deps(eng, reads, writes)
        self.cnt[eng] += 1
        ev = (eng, self.cnt[eng])
        self._emit(eng, waits, fn, None, 0)
        self._mark(ev, reads, writes)
        return ev

    def dma(self, q, name, fn, reads=(), writes=(), n=1):
        dkey = self.dsem(name)
        waits = self._deps(q, reads, writes)
        self.dcnt[dkey] += 16 * n
        ev = (dkey, self.dcnt[dkey])
        self._emit(q, waits, fn, dkey, n)
        self._mark(ev, reads, writes)
        return ev

    def barrier(self):
        for e in ENGS:
            waits = []
            seen = self.seen[e]
            for k in ENGS:
                if self.cnt[k] > seen.get(k, 0):
                    seen[k] = self.cnt[k]
                    waits.append((k, self.cnt[k]))
            for k, v in self.dcnt.items():
                if v > seen.get(k, 0):
                    seen[k] = v
                    waits.append((k, v))
            self._emit(e, waits, None, None, 0)


class T:
    def __init__(self, t, name, excl=False):
        self.t = t
        self.b = Buf(name, excl)

    def __getitem__(self, k):
        return self.t[k]


def build_nc(upto=99, dbg=()):
    nc = bass.Bass("TRN2", target_bir_lowering=False)

    def din(name, shape):
        return nc.dram_tensor(name, list(shape), F32, kind="ExternalInput").ap()

    xs = din("xs", [SLAB, D])
    xo = din("xo", [(NOT + 1) * 128, D])
    w_in = din("w_in", [D, 9280])
    conv_out_w = din("conv_out_w", [D, D])
    ssm_out_w = din("ssm_out_w", [2048, D])
    w_o = din("w_o", [D, D])
    w_gate = din("w_gate", [D, DFF])
    w_up = din("w_up", [D, DFF])
    w_down = din("w_down", [DFF, D])
    cdw_d = din("cdw", [128, 8, 31])
    scw_d = din("scw", [128, 24, 7])
    vec8_d = din("vec8", [128, 5, 8])
    scb_d = din("scb", [128, 24])
    rows_d = din("rows", [1, 128])
    dsk_d = din("dsk", [1, 32])
    normw_d = din("normw", [1, 2048])
    nfin_d = din("nfin", [1, D])
    hmask_d = din("hmask", [128, 1])
    mrow_d = din("mrow", [1, 2 * NOT])
    cident_d = din("cident", [128, 128])
    cuinc_d = din("cuinc", [128, 128])
    cnustr_d = din("cnustr", [128, 128])
    cnegf_d = din("cnegf", [128, 512])
    cnegb_d = din("cnegb", [128, 512])
    out_d = nc.dram_tensor("out", [SEG, D], F32, kind="ExternalOutput").ap()

    hns = nc.dram_tensor("hns", [NOT + 1, 128, D], BF16).ap()
    yscr = nc.dram_tensor("yscr", [2, 4, NCH, 128, 512], F32).ap()

    dbg_out = {}

    with ExitStack() as es:
        S = Sched(nc, es)

        uniq = {"n": 0}

        def sb(name, shape, dt, st=es):
            uniq["n"] += 1
            return T(st.enter_context(nc.sbuf_tensor("s%d_%s" % (uniq["n"], name), list(shape), dt)), name)

        banks = [T(es.enter_context(nc.psum_tensor(f"pb{i}", [128, 512], F32)), f"pb{i}", excl=True) for i in range(8)]
        bank_rr = {"i": 0, "n": 8}

        def bank():
            b = banks[bank_rr["i"] % bank_rr["n"]]
            bank_rr["i"] += 1
            return b

        def bfv(b):
            return b.t[:].bitcast(BF16)

        def dump(name, src, shape, dt, reads):
            if name not in dbg:
                return
            dd = nc.dram_tensor("dbg_" + name, list(shape), dt, kind="ExternalOutput").ap()
            dbg_out[name] = dd
            S.dma("sp", "dbg_" + name, lambda e: [e.dma_start(out=dd, in_=src)], reads=reads, writes=[Buf("dbgo")])

        ident = sb("ident", [128, 128], BF16)
        ones = sb("ones", [128, 128], BF16)
        onesm = sb("onesm", [128, 128], BF16)
        epsc = sb("epsc", [128, 1], F32)
        onec = sb("onec", [128, 1], F32)
        vec8 = sb("vec8", [128, 5, 8], F32)
        hnT = sb("hnT", [128, KT, SLAB], BF16)
        pA = ExitStack()
        uinc = sb("uinc", [128, 128], BF16, pA)
        nustr = sb("nustr", [128, 128], BF16, pA)
        negf = sb("negf", [128, 512], BF16, pA)
        negb = sb("negb", [128, 512], BF16, pA)
        scb = sb("scb", [128, 24], F32, pA)
        scw = sb("scw", [128, 24, 7], F32, pA)
        rows = sb("rows", [128, 128], F32, pA)
        abc = sb("abc", [128, 64], F32, pA)
        dsk = sb("dsk", [128, 32], F32, pA)
        hmask = sb("hmask", [128, 1], F32, pA)
        mrow = sb("mrow", [128, 2 * NOT], F32, pA)
        wdt = sb("wdt", [128, KT, 64], BF16, pA)
        sin = sb("sin", [128, 2, 2048], F32, pA)
        dtraw_o = sb("dtraw_o", [128, NCH, 64], F32, pA)
        ps1 = ExitStack()
        dtraw_s = sb("dtraw_s", [128, NOT, 64], F32, ps1)
        p0 = ExitStack()
        xt2 = [sb(f"xt{i}", [128, D], F32, p0) for i in range(8)]
        xn2 = [sb(f"xn{i}", [128, D], BF16, p0) for i in range(8)]
        junk = sb("junk", [128, D], BF16, p0)
        st2 = [sb(f"st{i}", [128, 4], F32, p0) for i in range(8)]
        hnt2 = [sb(f"hnt{i}", [128, KT, 128], BF16, p0) for i in range(4)]

        for (t_, d_) in [(ident, cident_d), (uinc, cuinc_d), (nustr, cnustr_d), (negf, cnegf_d), (negb, cnegb_d)]:
            S.dma("pool", "c_" + t_.b.name, (lambda t_, d_: lambda e: [e.dma_start(out=t_[:], in_=d_)])(t_, d_), writes=[t_.b])
        for (t_, d_) in [(vec8, vec8_d), (scb, scb_d), (scw, scw_d), (hmask, hmask_d)]:
            S.dma("sp", "c_" + t_.b.name, (lambda t_, d_: lambda e: [e.dma_start(out=t_[:], in_=d_)])(t_, d_), writes=[t_.b])
        for (t_, d_, n_) in [(rows, rows_d, 128), (dsk, dsk_d, 32), (mrow, mrow_d, 2 * NOT)]:
            S.dma("sp", "c_" + t_.b.name, (lambda t_, d_: lambda e: [e.dma_start(out=t_[:], in_=d_.partition_broadcast(128))])(t_, d_), writes=[t_.b])
        S.op("dve", lambda e: e.memset(ones[:], 1.0), writes=[ones.b])
        S.op("dve", lambda e: e.memset(onesm[:], 1.0 / 1024.0), writes=[onesm.b])
        S.op("dve", lambda e: e.memset(epsc[:], EPS), writes=[epsc.b])
        S.op("dve", lambda e: e.memset(onec[:], 1.0), writes=[onec.b])
        S.op("act", lambda e: e.activation(out=abc[:], in_=rows[:, 64:128], func=AF.Exp), reads=[rows.b], writes=[abc.b])
        S.op("dve", lambda e: e.tensor_scalar(out=abc[:], in0=abc[:], scalar1=-1.0, scalar2=None, op0=ALU.mult), reads=[abc.b], writes=[abc.b])

        def wload(dst, dst_ap, W, c0, ncols, kt=KT, r0=0):
            src = W[r0:r0 + kt * 128, c0:c0 + ncols].rearrange("(k p) c -> p k c", p=128)
            S.dma("pool", "w_" + dst.b.name, lambda e: [e.dma_start(out=dst_ap, in_=src)], writes=[dst.b])

        wload(wdt, wdt[:], w_in, O_DT, 64)

        pc_jobs = []

        def pc_issue(n):
            for _ in range(n):
                if pc_jobs:
                    pc_jobs.pop(0)()

        def precast(name, W, c0, ncols, nrows):
            dst = nc.dram_tensor("bf_" + name, [nrows, ncols], BF16).ap()
            b = Buf("bf_" + name)
            for r0 in range(0, nrows, 128):
                for cc in range(0, ncols, 2048):
                    w = min(2048, ncols - cc)
                    pc_jobs.append((lambda r0, cc, w: lambda: S.dma("pool", "pc_" + name, lambda e: [e.dma_start(out=dst[r0:r0 + 128, cc:cc + w], in_=W[r0:r0 + 128, c0 + cc:c0 + cc + w])],
                                                                   writes=[b]))(r0, cc, w))
            return dst, b
        bf_co = precast("co", conv_out_w, 0, D, D)
        bf_gg = precast("gg", w_in, O_GC, 2048, D)
        bf_so = precast("so", ssm_out_w, 0, D, 2048)
        bf_wo = precast("wo", w_o, 0, D, D)
        bf_wg = precast("wg", w_gate, 0, DFF, D)
        bf_wu = precast("wu", w_up, 0, DFF, D)
        bf_wd = precast("wd", w_down, 0, D, DFF)
        bf_z = precast("z", w_in, O_Z, 2048, D)

        ntc = {"i": 0}

        def rstd_of(st, ssq_ap, dim):
            S.op("act", lambda e: e.activation(out=st[:, 1:2], in_=ssq_ap, func=AF.Sqrt, bias=epsc[:, 0:1], scale=1.0 / dim),
                 reads=[st.b, epsc.b], writes=[st.b])
            S.op("dve", lambda e: e.reciprocal(out=st[:, 2:3], in_=st[:, 1:2]), reads=[st.b], writes=[st.b])

        def norm_stage1(src_rows):
            i = ntc["i"] % 8
            ntc["i"] += 1
            xt, xn, st = xt2[i], xn2[i], st2[i]
            S.dma("sp", "xt%d" % i, lambda e: [e.dma_start(out=xt[:], in_=src_rows)], writes=[xt.b])
            S.op("act", lambda e: e.activation(out=junk[:], in_=xt[:], func=AF.Square, accum_out=st[:, 0:1]),
                 reads=[xt.b], writes=[junk.b, st.b])
            rstd_of(st, st[:, 0:1], D)
            S.op("dve", lambda e: e.tensor_scalar(out=xn[:], in0=xt[:], scalar1=st[:, 2:3], scalar2=None, op0=ALU.mult), reads=[xt.b, st.b], writes=[xn.b])
            return xn

        def norm_stage2(xn, dst, dst_ap, gidx):
            bk = bank()

            def tr(e):
                v = bfv(bk)
                for k in range(KT):
                    r = e.transpose(out=v[:, k * 128:(k + 1) * 128], in_=xn[:, k * 128:(k + 1) * 128], identity=ident[:])
                return r
            S.op("pe", tr, reads=[xn.b, ident.b], writes=[bk.b])
            S.op("dve", lambda e: e.tensor_tensor(out=dst_ap, in0=bfv(bk).rearrange("p (k t) -> p k t", k=KT),
                                                  in1=vec8[:, gidx, :].unsqueeze(2).to_broadcast([128, KT, 128]), op=ALU.mult),
                 reads=[bk.b, vec8.b], writes=[dst.b])

        def dt_mm(lhs_fn, lhs_reads, dst, dst_ap):
            bk = bank()

            def f(e):
                for k in range(KT):
                    r = e.matmul(bk[:, 0:64], lhsT=lhs_fn(k), rhs=wdt[:, k, :], start=(k == 0), stop=(k == KT - 1))
                return r
            S.op("pe", f, reads=lhs_reads + [wdt.b], writes=[bk.b])
            S.op("dve", lambda e: e.tensor_tensor(out=dst_ap, in0=bk[:, 0:64], in1=rows[:, 0:64], op=ALU.add),
                 reads=[bk.b, rows.b], writes=[dst.b])

        hns_b = Buf("hns_all")
        NT_OWN = SLAB // 128
        items = [("own", i) for i in range(NT_OWN)] + [("oth", i) for i in range(NOT + 1)]

        def nst1(it):
            kind, i = it
            src = xs[i * 128:(i + 1) * 128, :] if kind == "own" else xo[i * 128:(i + 1) * 128, :]
            return norm_stage1(src)

        def nst2(it, xn):
            kind, i = it
            if kind == "own":
                norm_stage2(xn, hnT, hnT[:, :, i * 128:(i + 1) * 128], 3)
            else:
                h = hnt2[i % 4]
                norm_stage2(xn, h, h[:], 3)

        def nst3(it):
            kind, i = it
            if kind == "own":
                if 1 <= i <= NCH:
                    c = i - 1
                    dt_mm(lambda k: hnT[:, k, (c + 1) * 128:(c + 2) * 128], [hnT.b], dtraw_o, dtraw_o[:, c, :])
            else:
                h = hnt2[i % 4]
                if i < NOT:
                    dt_mm(lambda k: h[:, k, :], [h.b], dtraw_s, dtraw_s[:, i, :])
                S.dma("pool", "hns%d" % (i % 4), lambda e: [e.dma_start(out=hns[i], in_=h[:].rearrange("p k t -> p (k t)"))],
                      reads=[h.b], writes=[Buf("hns_dram%d" % i)])

        xns = {}
        n_it = len(items)
        for step in range(n_it + 3):
            if step < n_it:
                xns[step] = nst1(items[step])
            if 0 <= step - 2 < n_it:
                nst2(items[step - 2], xns.pop(step - 2))
            if 0 <= step - 3 < n_it:
                nst3(items[step - 3])
        dump("hnT", hnT[:], [128, KT, SLAB], BF16, [hnT.b])
        dump("dtraw_o", dtraw_o[:], [128, NCH, 64], F32, [dtraw_o.b])
        S.barrier()
        p0.close()
        dump("dtraw_s", dtraw_s[:], [128, NOT, 64], F32, [dtraw_s.b])
        if upto <= 0:
            return finish(nc, S, dbg_out)

        ps1a = ExitStack()
        tB = sb("tB", [128, NOT, 64], F32, ps1a)
        tC = sb("tC", [128, NOT, 64], F32, ps1a)
        tW = sb("tW", [128, NOT, 64], F32, ps1a)
        adts = sb("adts", [128, NOT, 64], BF16, ps1a)
        tA = dtraw_s

        def softplus_inplace(t, n):
            S.op("act", lambda e: e.activation(out=t[:], in_=t[:], func=AF.Exp), reads=[t.b], writes=[t.b])
            S.op("act", lambda e: e.activation(out=t[:], in_=t[:], func=AF.Ln, bias=onec[:, 0:1], scale=1.0), reads=[t.b, onec.b], writes=[t.b])

        def cums(adt, n, tinc, ttot):
            flat = adt[:].rearrange("p c j -> p (c j)")
            fi = tinc[:].rearrange("p c j -> p (c j)")
            ft = ttot[:].rearrange("p c j -> p (c j)")
            tot = n * 64
            for c0 in range(0, tot, 512):
                w = min(512, tot - c0)
                b1 = bank()
                S.op("pe", (lambda b1, c0, w: lambda e: e.matmul(b1[:, 0:w], lhsT=uinc[:], rhs=flat[:, c0:c0 + w], start=True, stop=True))(b1, c0, w),
                     reads=[adt.b, uinc.b], writes=[b1.b])
                S.op("act", (lambda b1, c0, w: lambda e: e.activation(out=fi[:, c0:c0 + w], in_=b1[:, 0:w], func=AF.Copy))(b1, c0, w),
                     reads=[b1.b], writes=[tinc.b])
                b2 = bank()
                S.op("pe", (lambda b2, c0, w: lambda e: e.matmul(b2[:, 0:w], lhsT=ones[:], rhs=flat[:, c0:c0 + w], start=True, stop=True))(b2, c0, w),
                     reads=[adt.b, ones.b], writes=[b2.b])
                S.op("dve", (lambda b2, c0, w: lambda e: e.tensor_copy(out=ft[:, c0:c0 + w], in_=b2[:, 0:w]))(b2, c0, w),
                     reads=[b2.b], writes=[ttot.b])

        softplus_inplace(tA, NOT)
        S.op("dve", lambda e: e.tensor_scalar(out=tA[:, 0, :], in0=tA[:, 0, :], scalar1=hmask[:, 0:1], scalar2=None, op0=ALU.mult),
             reads=[tA.b, hmask.b], writes=[tA.b])
        for d in range(2):
            S.op("dve", (lambda d: lambda e: e.tensor_tensor(out=tA[:, :, d * 32:(d + 1) * 32], in0=tA[:, :, d * 32:(d + 1) * 32],
                                                               in1=mrow[:, d * NOT:(d + 1) * NOT].unsqueeze(2).to_broadcast([128, NOT, 32]), op=ALU.mult))(d),
                 reads=[tA.b, mrow.b], writes=[tA.b])
        S.op("dve", lambda e: e.tensor_tensor(out=adts[:], in0=tA[:], in1=abc[:].unsqueeze(1).to_broadcast([128, NOT, 64]), op=ALU.mult),
             reads=[tA.b, abc.b], writes=[adts.b])
        cums(adts, NOT, tB, tC)
        S.op("dve", lambda e: e.tensor_tensor(out=tB[:, :, 0:32], in0=tC[:, :, 0:32], in1=tB[:, :, 0:32], op=ALU.subtract), reads=[tB.b, tC.b], writes=[tB.b])
        S.op("dve", lambda e: e.tensor_tensor(out=tB[:, :, 32:64], in0=tB[:, :, 32:64], in1=adts[:, :, 32:64], op=ALU.subtract), reads=[tB.b, adts.b], writes=[tB.b])
        S.op("dve", lambda e: e.memset(tW[:, NOT - 1, 0:32], 0.0), writes=[tW.b])
        S.op("dve", lambda e: e.memset(tW[:, 0, 32:64], 0.0), writes=[tW.b])
        for c in range(NOT - 2, -1, -1):
            S.op("dve", (lambda c: lambda e: e.tensor_tensor(out=tW[:, c, 0:32], in0=tW[:, c + 1, 0:32], in1=tC[:, c + 1, 0:32], op=ALU.add))(c),
                 reads=[tW.b, tC.b], writes=[tW.b])
        for c in range(1, NOT):
            S.op("pool", (lambda c: lambda e: e.tensor_tensor(out=tW[:, c, 32:64], in0=tW[:, c - 1, 32:64], in1=tC[:, c - 1, 32:64], op=ALU.add))(c),
                 reads=[tW.b, tC.b], writes=[tW.b])
        S.op("dve", lambda e: e.tensor_tensor(out=tB[:], in0=tB[:], in1=tW[:], op=ALU.add), reads=[tB.b, tW.b], writes=[tB.b])
        S.op("act", lambda e: e.activation(out=tB[:], in_=tB[:], func=AF.Exp), reads=[tB.b], writes=[tB.b])
        S.op("dve", lambda e: e.tensor_tensor(out=tA[:], in0=tA[:], in1=tB[:], op=ALU.mult), reads=[tA.b, tB.b], writes=[tA.b])
        mult_s = tA
        dump("mult_s", mult_s[:], [128, NOT, 64], F32, [mult_s.b])
        S.barrier()
        ps1a.close()

        bank_rr["n"] = 6
        accS = [banks[6], banks[7]]
        wsg_2 = [sb(f"wsg{i}", [128, KT, 640], BF16, ps1) for i in range(2)]
        dgs_2 = [sb(f"dgs{i}", [128, 5, 7, 128], BF16, ps1) for i in range(2)]
        xh = sb("xh", [128, 5, 128], BF16, ps1)
        hnb2 = [sb(f"hnb{i}", [128, 4, D], BF16, ps1) for i in range(2)]
        hnh = sb("hnh", [128, D], BF16, ps1)
        xpre_2 = [sb(f"xpre{i}", [128, 5, 520], BF16, ps1) for i in range(2)]
        xcb_2 = [sb(f"xcb{i}", [128, 5, 512], BF16, ps1) for i in range(2)]
        xpre_mb = [[Buf(f"xpm{i}_{m}") for m in range(5)] for i in range(2)]
        xpre_hb = [[Buf(f"xph{i}_{m}") for m in range(5)] for i in range(2)]
        xcb_mb = [[Buf(f"xcm{i}_{m}") for m in range(5)] for i in range(2)]
        xbt_2 = [[sb(f"xbt{j}_{i}", [128, 640], BF16, ps1) for i in range(4)] for j in range(2)]
        xdd_2 = [[[sb(f"xdd{j}_{i}_{d}", [128, 512], BF16, ps1) for d in range(2)] for i in range(4)] for j in range(2)]
        S.dma("sp", "hnh", lambda e: [e.dma_start(out=hnh[:], in_=hns[NOT])], reads=[hns_b], writes=[hnh.b])

        def build_diag(dg, dg_ap, tile, ntap, wt, eng):
            for tap in range(ntap):
                S.op(eng, (lambda tap: lambda e: e.tensor_scalar(out=dg_ap[:, tap, :], in0=ident[:], scalar1=wt[:, tile, tap:tap + 1], scalar2=None, op0=ALU.mult))(tap),
                     reads=[ident.b, wt.b], writes=[dg.b])

        nblk = 13
        def s2_prefetch(g):
            wsg_, dgs_ = wsg_2[g % 2], dgs_2[g % 2]
            wload(wsg_, wsg_[:, :, 0:512], w_in, O_X + g * 512, 512)
            wload(wsg_, wsg_[:, :, 512:640], w_in, O_B + g * 128, 128)
            tl_ = [g * 4 + m for m in range(4)] + [16 + g]
            for m in range(5):
                build_diag(dgs_, dgs_[:, m], tl_[m], 7, scw, "dve")

        s2_prefetch(0)
        for g in range(4):
            wsg, dgs = wsg_2[g % 2], dgs_2[g % 2]
            tiles = [g * 4 + m for m in range(4)] + [16 + g]
            for m in range(5):
                bk = bank()

                def f(e, bk=bk, m=m):
                    for k in range(KT):
                        r = e.matmul(bk[:, 0:128], lhsT=wsg[:, k, m * 128:(m + 1) * 128], rhs=hnh[:, k * 128:(k + 1) * 128], start=(k == 0), stop=(k == KT - 1))
                    return r
                S.op("pe", f, reads=[wsg.b, hnh.b], writes=[bk.b])
                S.op("act", (lambda bk, m: lambda e: e.activation(out=xh[:, m, :], in_=bk[:, 0:128], func=AF.Copy))(bk, m), reads=[bk.b], writes=[xh.b])
            def blkinfo(blk):
                NT = 1 if blk == 0 else 4
                t0 = 0 if blk == 0 else 1 + 4 * (blk - 1)
                return NT, t0, NT * 128

            def stageA(blk):
                NT, t0, N = blkinfo(blk)
                pc_issue(2)
                hb = hnb2[blk % 2]
                xpre, xcb = xpre_2[blk % 2], xcb_2[blk % 2]
                xpm, xph, xcm = xpre_mb[blk % 2], xpre_hb[blk % 2], xcb_mb[blk % 2]
                S.dma("sp", "hnb%d" % (blk % 2), lambda e: [e.dma_start(out=hb[:, 0:NT, :], in_=hns[t0:t0 + NT].rearrange("t p f -> p t f"))],
                      reads=[hns_b], writes=[hb.b])
                hbv = hb[:].rearrange("p t (k c) -> p t k c", k=KT)
                for m in range(5):
                    bk = bank()

                    def f(e, bk=bk, m=m):
                        for k in range(KT):
                            r = e.matmul(bk[:, 0:N], lhsT=wsg[:, k, m * 128:(m + 1) * 128], rhs=hbv[:, 0:NT, k, :], start=(k == 0), stop=(k == KT - 1))
                        return r
                    S.op("pe", f, reads=[wsg.b, hb.b], writes=[bk.b])
                    S.op("act", (lambda bk, m: lambda e: e.activation(out=xpre[:, m, 3:3 + N], in_=bk[:, 0:N], func=AF.Copy))(bk, m),
                         reads=[bk.b], writes=[xpm[m]])
                    S.op("act", (lambda m: lambda e: e.activation(out=xpre[:, m, 0:3], in_=xh[:, m, blk * 6:blk * 6 + 3], func=AF.Copy))(m), reads=[xh.b], writes=[xph[m]])
                    S.op("act", (lambda m: lambda e: e.activation(out=xpre[:, m, 3 + N:6 + N], in_=xh[:, m, blk * 6 + 3:blk * 6 + 6], func=AF.Copy))(m), reads=[xh.b], writes=[xph[m]])
                for m in range(5):
                    bk = bank()

                    def f(e, bk=bk, m=m):
                        for tap in range(7):
                            r = e.matmul(bk[:, 0:N], lhsT=dgs[:, m, tap, :], rhs=xpre[:, m, tap:tap + N], start=(tap == 0), stop=(tap == 6))
                        return r
                    S.op("pe", f, reads=[dgs.b, xpm[m], xph[m]], writes=[bk.b])
                    S.op("act", (lambda bk, m, tl: lambda e: e.activation(out=xcb[:, m, 0:N], in_=bk[:, 0:N], func=AF.Silu, bias=scb[:, tl:tl + 1], scale=1.0))(bk, m, tiles[m]),
                         reads=[bk.b, scb.b], writes=[xcm[m]])

            def stageB(blk):
                NT, t0, N = blkinfo(blk)
                xcb, xbt, xdd = xcb_2[blk % 2], xbt_2[blk % 2], xdd_2[blk % 2]
                xcm = xcb_mb[blk % 2]
                for t in range(NT):
                    c = t0 + t
                    bk = bank()

                    def f(e, bk=bk, t=t):
                        v = bfv(bk)
                        for m in range(5):
                            r = e.transpose(out=v[:, m * 128:(m + 1) * 128], in_=xcb[:, m, t * 128:(t + 1) * 128], identity=ident[:])
                        return r
                    S.op("pe", f, reads=xcm + [ident.b], writes=[bk.b])
                    S.op("dve", (lambda bk, t: lambda e: e.tensor_copy(out=xbt[t][:], in_=bfv(bk)[:, 0:640]))(bk, t), reads=[bk.b], writes=[xbt[t].b])
                    for d in range(2):
                        S.op("pool" if d == 0 else "dve",
                             (lambda t, d, c: lambda e: e.tensor_tensor(out=xdd[t][d][:].rearrange("p (h q) -> p h q", h=8),
                                                                        in0=xbt[t][:, 0:512].rearrange("p (h q) -> p h q", h=8),
                                                                        in1=mult_s[:, c, d * 32 + g * 8:d * 32 + g * 8 + 8].unsqueeze(2).to_broadcast([128, 8, 64]), op=ALU.mult))(t, d, c),
                             reads=[xbt[t].b, mult_s.b], writes=[xdd[t][d].b])

            def stageB2(blk):
                NT, t0, N = blkinfo(blk)
                xbt, xdd = xbt_2[blk % 2], xdd_2[blk % 2]
                for d in range(2):
                    def f(e, d=d):
                        for t in range(NT):
                            r = e.matmul(accS[d][:, :], lhsT=xbt[t][:, 512:640], rhs=xdd[t][d][:], start=(blk == 0 and t == 0), stop=(blk == nblk - 1 and t == NT - 1))
                        return r
                    S.op("pe", f, reads=[xbt[t].b for t in range(NT)] + [xdd[t][d].b for t in range(NT)], writes=[accS[d].b])

            stageA(0)
            for blk in range(nblk):
                stageB(blk)
                if blk + 1 < nblk:
                    stageA(blk + 1)
                if blk == 2 and g + 1 < 4:
                    s2_prefetch(g + 1)
                stageB2(blk)
            for d in range(2):
                S.op("act" if d == 0 else "dve",
                     (lambda d: (lambda e: e.activation(out=sin[:, d, g * 512:(g + 1) * 512], in_=accS[d][:, :], func=AF.Copy)) if d == 0 else
                      (lambda e: e.tensor_copy(out=sin[:, d, g * 512:(g + 1) * 512], in_=accS[d][:, :])))(d),
                     reads=[accS[d].b], writes=[sin.b])
        bank_rr["n"] = 8
        pc_issue(1000)
        S.barrier()
        ps1.close()
        dump("sin", sin[:], [128, 2, 2048], F32, [sin.b])
        if upto <= 1:
            return finish(nc, S, dbg_out)


        pm = ExitStack()
        t_dt = dtraw_o
        adto = sb("adto", [128, NCH, 64], BF16, pm)
        t_ai = sb("t_ai", [128, NCH, 64], F32, pm)
        t_tot = sb("t_tot", [128, NCH, 64], F32, pm)
        nadto = sb("nadto", [128, NCH, 64], BF16, pm)
        t_E = sb("t_E", [128, NCH, 32], F32, pm)
        t_dtd = sb("t_dtd", [128, NCH, 64], F32, pm)
        t_sc = sb("t_sc", [128, NCH, 64], F32, pm)
        softplus_inplace(t_dt, NCH)
        S.op("dve", lambda e: e.tensor_tensor(out=adto[:], in0=t_dt[:], in1=abc[:].unsqueeze(1).to_broadcast([128, NCH, 64]), op=ALU.mult),
             reads=[t_dt.b, abc.b], writes=[adto.b])
        cums(adto, NCH, t_ai, t_tot)
        F_, B_ = slice(0, 32), slice(32, 64)
        S.op("dve", lambda e: e.tensor_scalar(out=nadto[:], in0=adto[:], scalar1=-1.0, scalar2=None, op0=ALU.mult), reads=[adto.b], writes=[nadto.b])
        S.op("dve", lambda e: e.tensor_tensor(out=t_E[:], in0=t_ai[:, :, B_], in1=adto[:, :, B_], op=ALU.subtract), reads=[t_ai.b, adto.b], writes=[t_E.b])
        S.op("dve", lambda e: e.tensor_tensor(out=t_dtd[:, :, F_], in0=t_tot[:, :, F_], in1=t_ai[:, :, F_], op=ALU.subtract), reads=[t_ai.b, t_tot.b], writes=[t_dtd.b])
        S.op("dve", lambda e: e.tensor_copy(out=t_dtd[:, :, B_], in_=t_E[:]), reads=[t_E.b], writes=[t_dtd.b])
        S.op("act", lambda e: e.activation(out=t_dtd[:], in_=t_dtd[:], func=AF.Exp), reads=[t_dtd.b], writes=[t_dtd.b])
        S.op("dve", lambda e: e.tensor_tensor(out=t_dtd[:], in0=t_dtd[:], in1=t_dt[:], op=ALU.mult), reads=[t_dtd.b, t_dt.b], writes=[t_dtd.b])
        S.op("dve", lambda e: e.tensor_copy(out=t_sc[:, :, F_], in_=t_ai[:, :, F_]), reads=[t_ai.b], writes=[t_sc.b])
        S.op("dve", lambda e: e.tensor_tensor(out=t_sc[:, :, B_], in0=t_tot[:, :, B_], in1=t_E[:], op=ALU.subtract), reads=[t_tot.b, t_E.b], writes=[t_sc.b])
        S.op("act", lambda e: e.activation(out=t_sc[:], in_=t_sc[:], func=AF.Exp), reads=[t_sc.b], writes=[t_sc.b])
        S.op("act", lambda e: e.activation(out=t_tot[:], in_=t_tot[:], func=AF.Exp), reads=[t_tot.b], writes=[t_tot.b])
        t_et = t_tot

        wmg = sb("wmg", [128, KT, 768], BF16, pm)
        dgm2 = [sb(f"dgm{i}", [128, 7, 128], BF16, pm) for i in range(2)]
        xprem = sb("xprem", [128, 2056], BF16, pm)
        xc = sb("xc", [128, 6, SEG], BF16, pm)
        dpar = lambda name, shape, dt: [[sb(f"{name}{d}_{i}", shape, dt, pm) for i in range(2)] for d in range(2)]
        xbt2 = dpar("mxbt", [128, 640], BF16)
        L2 = dpar("Lb", [128, 8, 128], BF16)
        M2 = dpar("Mb", [128, 8, 128], BF16)
        xdt2 = dpar("xdt", [128, 512], BF16)
        xdd2 = dpar("mxdd", [128, 512], BF16)
        yfb2 = dpar("yfb", [128, 512], F32)
        Sbf2 = dpar("Sbf", [128, 512], BF16)
        xD2 = [sb(f"xD{i}", [128, 512], BF16, pm) for i in range(2)]
        t1_2 = [sb(f"t1_{d}", [128, 512], F32, pm) for d in range(2)]
        Sst = [sb(f"Sst{d}", [128, 512], F32, pm) for d in range(2)]
        ydram = Buf("yscr")

        def h8(ap):
            return ap.rearrange("p (h q) -> p h q", h=8)

        xc_2 = [xc, sb("xcB", [128, 6, SEG], BF16, pm)]

        def prologue_jobs(g):
            xc_t = xc_2[g % 2]
            tiles = [g * 4 + m for m in range(4)] + [16 + g, 20 + g]
            jobs = []

            def jw():
                wload(wmg, wmg[:, :, 0:512], w_in, O_X + g * 512, 512)
                wload(wmg, wmg[:, :, 512:640], w_in, O_B + g * 128, 128)
                wload(wmg, wmg[:, :, 640:768], w_in, O_C + g * 128, 128)
            jobs.append(jw)
            for m in range(6):
                dg = dgm2[m % 2]
                jobs.append((lambda dg, m: lambda: build_diag(dg, dg[:], tiles[m], 7, scw, "dve"))(dg, m))
                for j in range(5):
                    def ji(m=m, j=j):
                        s0 = 124 + 512 * j
                        w = min(512, 2180 - s0)
                        bk = bank()

                        def f(e):
                            for k in range(KT):
                                r = e.matmul(bk[:, 0:w], lhsT=wmg[:, k, m * 128:(m + 1) * 128], rhs=hnT[:, k, s0:s0 + w], start=(k == 0), stop=(k == KT - 1))
                            return r
                        S.op("pe", f, reads=[wmg.b, hnT.b], writes=[bk.b])
                        S.op("act", lambda e: e.activation(out=xprem[:, 512 * j:512 * j + w], in_=bk[:, 0:w], func=AF.Copy), reads=[bk.b], writes=[xprem.b])
                    jobs.append(ji)
                for q in range(4):
                    def jc(m=m, q=q, dg=dg):
                        bk = bank()

                        def f(e):
                            for tap in range(7):
                                r = e.matmul(bk[:, :], lhsT=dg[:, tap, :], rhs=xprem[:, 1 + 512 * q + tap:1 + 512 * q + tap + 512], start=(tap == 0), stop=(tap == 6))
                            return r
                        S.op("pe", f, reads=[dg.b, xprem.b], writes=[bk.b])
                        tl = tiles[m]
                        S.op("act", lambda e: e.activation(out=xc_t[:, m, q * 512:(q + 1) * 512], in_=bk[:, :], func=AF.Silu, bias=scb[:, tl:tl + 1], scale=1.0),
                             reads=[bk.b, scb.b], writes=[xc_t.b])
                    jobs.append(jc)
            return jobs

        next_jobs = prologue_jobs(0)
        for g in range(4):
            for jb in next_jobs:
                jb()
            next_jobs = prologue_jobs(g + 1) if g + 1 < 4 else []
            xc = xc_2[g % 2]
            if g == 0:
                dump("xc0", xc[:], [128, 6, SEG], BF16, [xc.b])

            sbi = [0, 0]
            for d in range(2):
                S.op("dve", (lambda d: lambda e: e.tensor_copy(out=Sst[d][:], in_=sin[:, d, g * 512:(g + 1) * 512]))(d), reads=[sin.b], writes=[Sst[d].b])
                S.op("act", (lambda d: lambda e: e.activation(out=Sbf2[d][0][:], in_=Sst[d][:], func=AF.Copy))(d), reads=[Sst[d].b], writes=[Sbf2[d][0].b])

            def stage1(d, it):
                c = it if d == 0 else NCH - 1 - it
                par = it % 2
                j0 = d * 32 + g * 8
                umat = uinc if d == 0 else nustr
                neg = negf if d == 0 else negb
                cs = slice(c * 128, (c + 1) * 128)
                xbt, Lb, Mb, xdt, xdd_ = xbt2[d][par], L2[d][par], M2[d][par], xdt2[d][par], xdd2[d][par]
                xD = xD2[par]
                bk = bank()

                def f(e):
                    v = bfv(bk)
                    for m in range(5):
                        e.transpose(out=v[:, m * 128:(m + 1) * 128], in_=xc[:, m, cs], identity=ident[:])
                    return e.matmul(bk[:, 320:448], lhsT=xc[:, 4, cs], rhs=xc[:, 5, cs], start=True, stop=True)
                S.op("pe", f, reads=[xc.b, ident.b], writes=[bk.b])
                S.op("act", lambda e: e.activation(out=xbt[:], in_=bfv(bk)[:, 0:640], func=AF.Copy), reads=[bk.b], writes=[xbt.b])
                for half in range(2):
                    bkl = bank()

                    def f(e, bkl=bkl, half=half):
                        jj = j0 + 4 * half
                        e.matmul(bkl[:, :], lhsT=umat[:], rhs=nadto[:, c, jj:jj + 4].unsqueeze(2).to_broadcast([128, 4, 128]), start=True, stop=False)
                        e.matmul(bkl[:, :], lhsT=ident[:], rhs=neg[:], start=False, stop=False)
                        for hh in range(4):
                            r = e.matmul(bkl[:, hh * 128:(hh + 1) * 128], lhsT=adto[:, c, jj + hh:jj + hh + 1].to_broadcast([128, 128]), rhs=umat[:], start=False, stop=(hh == 3))
                        return r
                    S.op("pe", f, reads=[umat.b, ident.b, neg.b, adto.b, nadto.b], writes=[bkl.b])
                    S.op("act", (lambda bkl, half: lambda e: e.activation(out=Lb[:, half * 4:(half + 1) * 4, :], in_=bkl[:, :].rearrange("p (h l) -> p h l", h=4), func=AF.Exp))(bkl, half),
                         reads=[bkl.b], writes=[Lb.b])
                S.op("dve", lambda e: e.tensor_tensor(out=Mb[:], in0=Lb[:], in1=bk[:, 320:448].unsqueeze(1).to_broadcast([128, 8, 128]), op=ALU.mult),
                     reads=[bk.b, Lb.b], writes=[Mb.b])
                S.op("pool", lambda e: e.tensor_tensor(out=h8(xdt[:]), in0=h8(xbt[:, 0:512]), in1=t_dt[:, c, j0:j0 + 8].unsqueeze(2).to_broadcast([128, 8, 64]), op=ALU.mult),
                     reads=[xbt.b, t_dt.b], writes=[xdt.b])
                S.op("pool", lambda e: e.tensor_tensor(out=h8(xdd_[:]), in0=h8(xbt[:, 0:512]), in1=t_dtd[:, c, j0:j0 + 8].unsqueeze(2).to_broadcast([128, 8, 64]), op=ALU.mult),
                     reads=[xbt.b, t_dtd.b], writes=[xdd_.b])
                if d == 0:
                    S.op("pool", lambda e: e.tensor_tensor(out=h8(xD[:]), in0=h8(xbt[:, 0:512]), in1=dsk[:, g * 8:g * 8 + 8].unsqueeze(2).to_broadcast([128, 8, 64]), op=ALU.mult),
                         reads=[xbt.b, dsk.b], writes=[xD.b])

            def stage2(d, it):
                c = it if d == 0 else NCH - 1 - it
                par = it % 2
                j0 = d * 32 + g * 8
                cs = slice(c * 128, (c + 1) * 128)
                xbt, Mb, xdt, xdd_ = xbt2[d][par], M2[d][par], xdt2[d][par], xdd2[d][par]
                xD = xD2[par]
                Sd, t1 = Sst[d], t1_2[d]
                bks = bank()
                S.op("pe", lambda e: e.matmul(bks[:, :], lhsT=xbt[:, 512:640], rhs=xdd_[:], start=True, stop=True), reads=[xbt.b, xdd_.b], writes=[bks.b])
                bko = bank()
                Sb_cur = Sbf2[d][sbi[d]]
                S.op("pe", lambda e: e.matmul(bko[:, :], lhsT=xc[:, 5, cs], rhs=Sb_cur[:], start=True, stop=True), reads=[xc.b, Sb_cur.b], writes=[bko.b])
                bky = bank()
                if d == 0:
                    def f(e):
                        e.matmul(bky[:, :], lhsT=ident[:], rhs=xD[:], start=True, stop=False)
                        for h in range(8):
                            r = e.matmul(bky[:, h * 64:(h + 1) * 64], lhsT=Mb[:, h, :], rhs=xdt[:, h * 64:(h + 1) * 64], start=False, stop=(h == 7))
                        return r
                    S.op("pe", f, reads=[ident.b, Mb.b, xdt.b, xD.b], writes=[bky.b])
                else:
                    def f(e):
                        for h in range(8):
                            r = e.matmul(bky[:, h * 64:(h + 1) * 64], lhsT=Mb[:, h, :], rhs=xdt[:, h * 64:(h + 1) * 64], start=True, stop=True)
                        return r
                    S.op("pe", f, reads=[Mb.b, xdt.b], writes=[bky.b])
                if it + 1 < NCH:
                    S.op("dve", lambda e: e.tensor_tensor(out=h8(Sd[:]), in0=h8(Sd[:]), in1=t_et[:, c, j0:j0 + 8].unsqueeze(2).to_broadcast([128, 8, 64]), op=ALU.mult),
                         reads=[Sd.b, t_et.b], writes=[Sd.b])
                    S.op("dve", lambda e: e.tensor_tensor(out=Sd[:], in0=Sd[:], in1=bks[:, :], op=ALU.add), reads=[Sd.b, bks.b], writes=[Sd.b])
                    sbi[d] = 1 - sbi[d]
                    nxt = Sbf2[d][sbi[d]]
                    S.op("act", lambda e: e.activation(out=nxt[:], in_=Sd[:], func=AF.Copy), reads=[Sd.b], writes=[nxt.b])
                S.op("dve", lambda e: e.tensor_tensor(out=h8(t1[:]), in0=h8(bko[:, :]), in1=t_sc[:, c, j0:j0 + 8].unsqueeze(2).to_broadcast([128, 8, 64]), op=ALU.mult),
                     reads=[bko.b, t_sc.b], writes=[t1.b])
                yfb = yfb2[d][par]
                S.op("dve", lambda e: e.tensor_tensor(out=yfb[:], in0=t1[:], in1=bky[:, :], op=ALU.add), reads=[t1.b, bky.b], writes=[yfb.b])
                if g == 0 and c in (15, 0):
                    dump("y%d_%d" % (d, c), yfb[:], [128, 512], F32, [yfb.b])
                S.dma("sp", "yout%d_%d" % (d, par), lambda e: [e.dma_start(out=yscr[d, g, c], in_=yfb[:])], reads=[yfb.b], writes=[ydram])

            stage1(0, 0)
            stage1(1, 0)
            for it in range(NCH):
                if it + 1 < NCH:
                    stage1(0, it + 1)
                    stage1(1, it + 1)
                stage2(0, it)
                for _ in range(2):
                    if next_jobs:
                        next_jobs.pop(0)()
                stage2(1, it)
                for _ in range(2):
                    if next_jobs:
                        next_jobs.pop(0)()
        S.barrier()
        pm.close()
        pA.close()
        if upto <= 2:
            return finish(nc, S, dbg_out)


        cvscr = nc.dram_tensor("cvscr", [8, 128, SEG], BF16).ap()
        cvdram = Buf("cvscr")
        pc = ExitStack()
        cdw = sb("cdw", [128, 8, 31], F32, pc)
        dgc2 = [sb(f"dgc{i}", [128, 31, 128], BF16, pc) for i in range(2)]
        wc2 = [sb(f"wc{i}", [128, KT, 256], BF16, pc) for i in range(2)]
        abuf2 = [sb(f"abuf{i}", [128, 2080], BF16, pc) for i in range(2)]
        sgc = sb("sgc", [128, 512], F32, pc)
        cvt2 = [sb(f"cvo{i}", [128, 512], BF16, pc) for i in range(2)]
        S.dma("sp", "cdw", lambda e: [e.dma_start(out=cdw[:], in_=cdw_d)], writes=[cdw.b])
        for ct in range(8):
            wc, dgc, abuf = wc2[ct % 2], dgc2[ct % 2], abuf2[ct % 2]
            wload(wc, wc[:, :, 0:128], w_in, O_VAL + ct * 128, 128)
            wload(wc, wc[:, :, 128:256], w_in, O_GATE + ct * 128, 128)
            for tap in range(31):
                S.op("dve",
                     (lambda tap: lambda e: e.tensor_scalar(out=dgc[:, tap, :], in0=ident[:], scalar1=cdw[:, ct, tap:tap + 1], scalar2=None, op0=ALU.mult))(tap),
                     reads=[ident.b, cdw.b], writes=[dgc.b])
            for j in range(5):
                s0 = 112 + 512 * j
                w = min(512, 2192 - s0)
                bv, bg = bank(), bank()
                for (bk_, off) in ((bv, 0), (bg, 128)):
                    def f(e, bk_=bk_, off=off, s0=s0, w=w):
                        for k in range(KT):
                            r = e.matmul(bk_[:, 0:w], lhsT=wc[:, k, off:off + 128], rhs=hnT[:, k, s0:s0 + w], start=(k == 0), stop=(k == KT - 1))
                        return r
                    S.op("pe", f, reads=[wc.b, hnT.b], writes=[bk_.b])
                S.op("act", (lambda bg, w: lambda e: e.activation(out=sgc[:, 0:w], in_=bg[:, 0:w], func=AF.Sigmoid))(bg, w), reads=[bg.b], writes=[sgc.b])
                S.op("dve", (lambda bv, j, w: lambda e: e.tensor_tensor(out=abuf[:, 512 * j:512 * j + w], in0=bv[:, 0:w], in1=sgc[:, 0:w], op=ALU.mult))(bv, j, w),
                     reads=[bv.b, sgc.b], writes=[abuf.b])
            for q in range(4):
                bk = bank()

                def f(e, bk=bk, q=q):
                    for tap in range(31):
                        r = e.matmul(bk[:, :], lhsT=dgc[:, tap, :], rhs=abuf[:, 1 + 512 * q + tap:1 + 512 * q + tap + 512], start=(tap == 0), stop=(tap == 30))
                    return r
                S.op("pe", f, reads=[dgc.b, abuf.b], writes=[bk.b])
                cvt = cvt2[q % 2]
                S.op("act", (lambda bk, cvt: lambda e: e.activation(out=cvt[:], in_=bk[:, :], func=AF.Identity, bias=vec8[:, 0, ct:ct + 1], scale=1.0))(bk, cvt),
                     reads=[bk.b, vec8.b], writes=[cvt.b])
                S.dma("sp", "cvo%d" % (q % 2), (lambda cvt, q: lambda e: [e.dma_start(out=cvscr[ct, :, q * 512:(q + 1) * 512], in_=cvt[:])])(cvt, q), reads=[cvt.b], writes=[cvdram])
        S.barrier()
        pc.close()
        if upto <= 3:
            return finish(nc, S, dbg_out)

        pb = ExitStack()
        hs2 = sb("hs2", [128, 4, D], F32, pb)
        wb2 = [sb(f"wb{i}", [128, 4096], BF16, pb) for i in range(3)]
        wbc = {"i": 0}
        normw = sb("normw", [128, 2048], F32, pb)
        S.dma("sp", "normw", lambda e: [e.dma_start(out=normw[:], in_=normw_d.partition_broadcast(128))], writes=[normw.b])
        cvt_2 = [sb("cvtp", [128, 8, 512], BF16, pb)] * 2

        def cvt_load(T_):
            cv = cvt_2[T_ % 2]
            S.dma("sp", "cvt%d" % (T_ % 2), lambda e: [e.dma_start(out=cv[:], in_=cvscr[:, :, 512 * T_:512 * T_ + 512].rearrange("c p t -> p c t"))], reads=[cvdram], writes=[cv.b])
        cvt_load(0)

        def wchunk(W, c0, ncols, kt=KT, r0=0):
            wb = wb2[wbc["i"] % 3]
            wbc["i"] += 1
            view = wb[:, 0:kt * ncols].rearrange("p (k c) -> p k c", k=kt)
            Wd_, Wb_ = W
            src = Wd_[r0:r0 + kt * 128, c0:c0 + ncols].rearrange("(k p) c -> p k c", p=128)
            S.dma("sp", "w_" + wb.b.name, lambda e: [e.dma_start(out=view, in_=src)], reads=[Wb_], writes=[wb.b])
            return wb, view

        for T_ in range(4):
            tok0 = 512 * T_
            sc0 = 128 + tok0
            tcol = slice(sc0, sc0 + 512)
            pmix = ExitStack()
            cvt = cvt_2[T_ % 2]
            aact = sb("aact", [128, 8, 512], BF16, pmix)
            mc = sb("mc", [128, 8, 512], F32, pmix)
            vT = sb("vT", [128, 16, 512], BF16, pmix)
            merged = sb("merged", [128, 8, 512], BF16, pmix)
            lnm = sb("lnm", [128, 512], F32, pmix)
            lnr = sb("lnr", [128, 512], F32, pmix)
            lt = sb("lt", [128, 512], F32, pmix)
            lt2 = sb("lt2", [128, 512], F32, pmix)
            sgb = sb("sgb", [128, 512], F32, pmix)
            sq2 = [sb(f"sq{i}", [128, 512], BF16, pmix) for i in range(2)]
            NBUF = 4
            yl2 = [sb(f"yl{i}", [128, 2, 512], F32, pmix) for i in range(NBUF)]
            szb2 = [sb(f"szb{i}", [128, 512], F32, pmix) for i in range(NBUF)]
            vb2 = [sb(f"vb{i}", [128, 512], F32, pmix) for i in range(NBUF)]
            vnb2 = [sb(f"vnb{i}", [128, 512], BF16, pmix) for i in range(NBUF)]
            vjunk = sb("vjunk", [128, 512], BF16, pmix)
            vst2 = [sb(f"vst{i}", [128, 4], F32, pmix) for i in range(NBUF)]
            wz_cache = {}
            wzb2 = [sb(f"wzb{i}", [128, 4096], BF16, pmix) for i in range(2)]

            def gateZ(u):
                g, j = u // 4, u % 4
                if g not in wz_cache:
                    wzb = wzb2[g % 2]
                    wzview = wzb[:].rearrange("p (k c) -> p k c", k=KT)
                    S.dma("sp", "w_" + wzb.b.name, (lambda wzview, g: lambda e: [e.dma_start(out=wzview, in_=bf_z[0][:, g * 512:(g + 1) * 512].rearrange("(k p) c -> p k c", p=128))])(wzview, g),
                          reads=[bf_z[1]], writes=[wzb.b])
                    wz_cache[g] = (wzb, wzview)
                wz_, wzv = wz_cache[g]
                c = 4 * T_ + j
                i2 = u % NBUF
                yl, szb, vb, vnb, vst = yl2[i2], szb2[i2], vb2[i2], vnb2[i2], vst2[i2]
                S.dma("sp", "yl%d" % i2, lambda e: [e.dma_start(out=yl[:], in_=yscr[:, g, c].rearrange("d p f -> p d f"))], reads=[ydram], writes=[yl.b])
                bkz = bank()

                def f(e):
                    for k in range(KT):
                        r = e.matmul(bkz[:, :], lhsT=hnT[:, k, sc0 + j * 128:sc0 + (j + 1) * 128], rhs=wzv[:, k, :], start=(k == 0), stop=(k == KT - 1))
                    return r
                S.op("pe", f, reads=[hnT.b, wz_.b], writes=[bkz.b])
                S.op("act", lambda e: e.activation(out=szb[:], in_=bkz[:, :], func=AF.Silu), reads=[bkz.b], writes=[szb.b])
                S.op("dve", lambda e: e.tensor_tensor(out=vb[:], in0=yl[:, 0, :], in1=yl[:, 1, :], op=ALU.add), reads=[yl.b], writes=[vb.b])
                S.op("dve", lambda e: e.tensor_tensor(out=vb[:], in0=vb[:], in1=szb[:], op=ALU.mult), reads=[vb.b, szb.b], writes=[vb.b])
                S.op("act", lambda e: e.activation(out=vjunk[:], in_=vb[:], func=AF.Square, accum_out=vst[:, 0:1]), reads=[vb.b], writes=[vjunk.b, vst.b])
                rstd_of(vst, vst[:, 0:1], 512)
                S.op("dve", lambda e: e.scalar_tensor_tensor(out=vnb[:], in0=vb[:], scalar=vst[:, 2:3], in1=normw[:, g * 512:(g + 1) * 512], op0=ALU.mult, op1=ALU.mult),
                     reads=[vb.b, vst.b, normw.b], writes=[vnb.b])

            def gateT(u):
                g, j = u // 4, u % 4
                vnb = vnb2[u % NBUF]
                bkt = bank()

                def f(e):
                    v = bfv(bkt)
                    for m in range(4):
                        r = e.transpose(out=v[:, m * 128:(m + 1) * 128], in_=vnb[:, m * 128:(m + 1) * 128], identity=ident[:])
                    return r
                S.op("pe", f, reads=[vnb.b, ident.b], writes=[bkt.b])
                S.op("act", lambda e: e.activation(out=vT[:, g * 4:(g + 1) * 4, j * 128:(j + 1) * 128], in_=bfv(bkt)[:, 0:512].rearrange("p (m t) -> p m t", m=4), func=AF.Copy),
                     reads=[bkt.b], writes=[vT.b])

            gate_sched = []
            for u in range(16 + 3):
                if u < 16:
                    gate_sched.append(("Z", u))
                if u >= 3:
                    gate_sched.append(("T", u - 3))

            def gate_emit(n):
                for _ in range(n):
                    if gate_sched:
                        kind, u = gate_sched.pop(0)
                        (gateZ if kind == "Z" else gateT)(u)
            bm, bq = bank(), bank()

            def f(e):
                for ct in range(8):
                    r = e.matmul(bm[:, :], lhsT=onesm[:], rhs=cvt[:, ct, :], start=(ct == 0), stop=(ct == 7))
                return r
            S.op("pe", f, reads=[onesm.b, cvt.b], writes=[bm.b])
            for ct in range(8):
                sq = sq2[ct % 2]
                S.op("act", (lambda sq, ct: lambda e: e.activation(out=sq[:], in_=cvt[:, ct, :], func=AF.Square))(sq, ct), reads=[cvt.b], writes=[sq.b])
                S.op("pe", (lambda sq, ct: lambda e: e.matmul(bq[:, :], lhsT=onesm[:], rhs=sq[:], start=(ct == 0), stop=(ct == 7)))(sq, ct), reads=[onesm.b, sq.b], writes=[bq.b])
            S.op("act", lambda e: e.activation(out=lnm[:], in_=bm[:, :], func=AF.Copy), reads=[bm.b], writes=[lnm.b])
            S.op("dve", lambda e: e.tensor_tensor(out=lt[:], in0=lnm[:], in1=lnm[:], op=ALU.mult), reads=[lnm.b], writes=[lt.b])
            S.op("dve", lambda e: e.tensor_tensor(out=lt[:], in0=bq[:, :], in1=lt[:], op=ALU.subtract), reads=[bq.b, lt.b], writes=[lt.b])
            S.op("act", lambda e: e.activation(out=lt[:], in_=lt[:], func=AF.Sqrt, bias=epsc[:, 0:1], scale=1.0), reads=[lt.b, epsc.b], writes=[lt.b])
            S.op("dve", lambda e: e.reciprocal(out=lnr[:], in_=lt[:]), reads=[lt.b], writes=[lnr.b])
            gate_emit(6)
            for ct in range(8):
                S.op("dve", (lambda ct: lambda e: e.tensor_tensor(out=lt2[:], in0=cvt[:, ct, :], in1=lnm[:], op=ALU.subtract))(ct), reads=[cvt.b, lnm.b], writes=[lt2.b])
                S.op("dve", lambda e: e.tensor_tensor(out=lt2[:], in0=lt2[:], in1=lnr[:], op=ALU.mult), reads=[lt2.b, lnr.b], writes=[lt2.b])
                S.op("act", (lambda ct: lambda e: e.activation(out=aact[:, ct, :], in_=lt2[:], func=AF.Silu, bias=vec8[:, 2, ct:ct + 1], scale=vec8[:, 1, ct:ct + 1]))(ct),
                     reads=[lt2.b, vec8.b], writes=[aact.b])
            if T_ == 0:
                dump("aact0", aact[:], [128, 8, 512], BF16, [aact.b])
            for half in range(2):
                wa, wav = wchunk(bf_co, half * 512, 512)
                wg_, wgv = wchunk(bf_gg, half * 512, 512)
                for mm in range(4):
                    m = half * 4 + mm
                    by, bg = bank(), bank()

                    def f(e, by=by, mm=mm, wav=wav):
                        for k in range(KT):
                            r = e.matmul(by[:, :], lhsT=wav[:, k, mm * 128:(mm + 1) * 128], rhs=aact[:, k, :], start=(k == 0), stop=(k == KT - 1))
                        return r
                    S.op("pe", f, reads=[wa.b, aact.b], writes=[by.b])

                    def f(e, bg=bg, mm=mm, wgv=wgv):
                        for k in range(KT):
                            r = e.matmul(bg[:, :], lhsT=wgv[:, k, mm * 128:(mm + 1) * 128], rhs=hnT[:, k, tcol], start=(k == 0), stop=(k == KT - 1))
                        return r
                    S.op("pe", f, reads=[wg_.b, hnT.b], writes=[bg.b])
                    S.op("act", (lambda bg: lambda e: e.activation(out=sgb[:], in_=bg[:, :], func=AF.Sigmoid))(bg), reads=[bg.b], writes=[sgb.b])
                    S.op("dve", (lambda by, m: lambda e: e.tensor_tensor(out=mc[:, m, :], in0=by[:, :], in1=sgb[:], op=ALU.mult))(by, m), reads=[by.b, sgb.b], writes=[mc.b])
                    gate_emit(4)
            gate_emit(100)
            for half in range(2):
                wg_, wgv = wchunk(bf_gg, 1024 + half * 512, 512)
                for pr in range(2):
                    wa, wav = wchunk(bf_so, half * 512 + pr * 256, 256, kt=16)
                    for mm in range(2):
                        m = half * 4 + pr * 2 + mm
                        gm = pr * 2 + mm
                        by, bg = bank(), bank()

                        def f(e, by=by, mm=mm, wav=wav):
                            for k in range(16):
                                r = e.matmul(by[:, :], lhsT=wav[:, k, mm * 128:(mm + 1) * 128], rhs=vT[:, k, :], start=(k == 0), stop=(k == 15))
                            return r
                        S.op("pe", f, reads=[wa.b, vT.b], writes=[by.b])

                        def f(e, bg=bg, gm=gm, wgv=wgv):
                            for k in range(KT):
                                r = e.matmul(bg[:, :], lhsT=wgv[:, k, gm * 128:(gm + 1) * 128], rhs=hnT[:, k, tcol], start=(k == 0), stop=(k == KT - 1))
                            return r
                        S.op("pe", f, reads=[wg_.b, hnT.b], writes=[bg.b])
                        S.op("act", (lambda bg: lambda e: e.activation(out=sgb[:], in_=bg[:, :], func=AF.Sigmoid))(bg), reads=[bg.b], writes=[sgb.b])
                        S.op("dve", (lambda by: lambda e: e.tensor_tensor(out=lt[:], in0=by[:, :], in1=sgb[:], op=ALU.mult))(by), reads=[by.b, sgb.b], writes=[lt.b])
                        S.op("dve", (lambda m: lambda e: e.tensor_tensor(out=merged[:, m, :], in0=lt[:], in1=mc[:, m, :], op=ALU.add))(m), reads=[lt.b, mc.b], writes=[merged.b])
            if T_ == 0:
                dump("merged0", merged[:], [128, 8, 512], BF16, [merged.b])
            for j in range(4):
                S.dma("sp", "xr%d" % (j % 2), (lambda j: lambda e: [e.dma_start(out=hs2[:, j, :], in_=xs[sc0 + j * 128:sc0 + (j + 1) * 128, :])])(j), writes=[hs2.b])
            for nh in range(2):
                wa, wav = wchunk(bf_wo, nh * 512, 512)
                for j in range(4):
                    bk = bank()

                    def f(e, bk=bk, j=j, wav=wav):
                        for k in range(KT):
                            r = e.matmul(bk[:, :], lhsT=merged[:, k, j * 128:(j + 1) * 128], rhs=wav[:, k, :], start=(k == 0), stop=(k == KT - 1))
                        return r
                    S.op("pe", f, reads=[wa.b, merged.b], writes=[bk.b])
                    S.op("dve", (lambda bk, j, nh: lambda e: e.tensor_tensor(out=hs2[:, j, nh * 512:(nh + 1) * 512], in0=bk[:, :], in1=hs2[:, j, nh * 512:(nh + 1) * 512], op=ALU.add))(bk, j, nh),
                         reads=[bk.b, hs2.b], writes=[hs2.b])
            if T_ == 0:
                dump("hs2_0", hs2[:], [128, 4, D], F32, [hs2.b])
            S.barrier()
            pmix.close()
            pf = ExitStack()
            hn2T = sb("hn2T", [128, KT, 512], BF16, pf)
            actT = sb("actT", [128, FT, 512], BF16, pf)
            wd = sb("wd", [128, FT, D], BF16, pf)
            sgf = sb("sgf", [128, 512], F32, pf)
            ob2 = [sb(f"ob{i}", [128, D], F32, pf) for i in range(2)]
            nfin = sb("nfin", [128, D], F32, pf)
            if T_ + 1 < 4:
                cvt_load(T_ + 1)
            xn2 = [sb(f"fxn{i}", [128, D], BF16, pf) for i in range(2)]
            junk = sb("fjunk", [128, D], BF16, pf)
            st2 = [sb(f"fst{i}", [128, 4], F32, pf) for i in range(2)]
            xt2 = [sb(f"hs3_{i}", [128, D], F32, pf) for i in range(2)]
            S.dma("sp", "nfin", lambda e: [e.dma_start(out=nfin[:], in_=nfin_d.partition_broadcast(128))], writes=[nfin.b])
            for (f0, nf) in ((0, 8), (8, 8), (16, 6)):
                S.dma("sp", "wd", (lambda f0, nf: lambda e: [e.dma_start(out=wd[:, f0:f0 + nf, :], in_=bf_wd[0][f0 * 128:(f0 + nf) * 128, :].rearrange("(k p) c -> p k c", p=128))])(f0, nf),
                      reads=[bf_wd[1]], writes=[wd.b])
            for j in range(4):
                i = j % 2
                xn, st = xn2[i], st2[i]
                S.op("act", (lambda st, j: lambda e: e.activation(out=junk[:], in_=hs2[:, j, :], func=AF.Square, accum_out=st[:, 0:1]))(st, j), reads=[hs2.b], writes=[junk.b, st.b])
                rstd_of(st, st[:, 0:1], D)
                S.op("act", (lambda xn, st, j: lambda e: e.activation(out=xn[:], in_=hs2[:, j, :], func=AF.Copy, scale=st[:, 2:3]))(xn, st, j), reads=[hs2.b, st.b], writes=[xn.b])
                bk = bank()

                def f(e, bk=bk, xn=xn):
                    v = bfv(bk)
                    for k in range(KT):
                        r = e.transpose(out=v[:, k * 128:(k + 1) * 128], in_=xn[:, k * 128:(k + 1) * 128], identity=ident[:])
                    return r
                S.op("pe", f, reads=[xn.b, ident.b], writes=[bk.b])
                S.op("dve", (lambda bk, j: lambda e: e.tensor_tensor(out=hn2T[:, :, j * 128:(j + 1) * 128], in0=bfv(bk).rearrange("p (k t) -> p k t", k=KT),
                                                                       in1=vec8[:, 4, :].unsqueeze(2).to_broadcast([128, KT, 128]), op=ALU.mult))(bk, j),
                     reads=[bk.b, vec8.b], writes=[hn2T.b])
            for (f0, nf) in ((0, 4), (4, 4), (8, 4), (12, 4), (16, 4), (20, 2)):
                wg_, wgv = wchunk(bf_wg, f0 * 128, nf * 128)
                wu_, wuv = wchunk(bf_wu, f0 * 128, nf * 128)
                for fl in range(nf):
                    fi = f0 + fl
                    bg, bu = bank(), bank()
                    for (bk_, wv_, wt_) in ((bg, wgv, wg_), (bu, wuv, wu_)):
                        def f(e, bk_=bk_, wv_=wv_, fl=fl):
                            for k in range(KT):
                                r = e.matmul(bk_[:, :], lhsT=wv_[:, k, fl * 128:(fl + 1) * 128], rhs=hn2T[:, k, :], start=(k == 0), stop=(k == KT - 1))
                            return r
                        S.op("pe", f, reads=[wt_.b, hn2T.b], writes=[bk_.b])
                    S.op("act", (lambda bg: lambda e: e.activation(out=sgf[:], in_=bg[:, :], func=AF.Silu))(bg), reads=[bg.b], writes=[sgf.b])
                    S.op("dve", (lambda bu, fi: lambda e: e.tensor_tensor(out=actT[:, fi, :], in0=bu[:, :], in1=sgf[:], op=ALU.mult))(bu, fi), reads=[bu.b, sgf.b], writes=[actT.b])
            for j in range(4):
                i = j % 2
                hs3, ob, st = xt2[i], ob2[i], st2[i]
                for nh in range(2):
                    bk = bank()

                    def f(e, bk=bk, j=j, nh=nh):
                        for fi in range(FT):
                            r = e.matmul(bk[:, :], lhsT=actT[:, fi, j * 128:(j + 1) * 128], rhs=wd[:, fi, nh * 512:(nh + 1) * 512], start=(fi == 0), stop=(fi == FT - 1))
                        return r
                    S.op("pe", f, reads=[actT.b, wd.b], writes=[bk.b])
                    S.op("dve", (lambda bk, hs3, j, nh: lambda e: e.tensor_tensor(out=hs3[:, nh * 512:(nh + 1) * 512], in0=bk[:, :], in1=hs2[:, j, nh * 512:(nh + 1) * 512], op=ALU.add))(bk, hs3, j, nh),
                         reads=[bk.b, hs2.b], writes=[hs3.b])
                S.op("act", (lambda hs3, st: lambda e: e.activation(out=junk[:], in_=hs3[:], func=AF.Square, accum_out=st[:, 0:1]))(hs3, st), reads=[hs3.b], writes=[junk.b, st.b])
                rstd_of(st, st[:, 0:1], D)
                S.op("dve", (lambda hs3, ob, st: lambda e: e.scalar_tensor_tensor(out=ob[:], in0=hs3[:], scalar=st[:, 2:3], in1=nfin[:], op0=ALU.mult, op1=ALU.mult))(hs3, ob, st),
                     reads=[hs3.b, st.b, nfin.b], writes=[ob.b])
                S.dma("act", "out%d" % i, (lambda ob, j: lambda e: [e.dma_start(out=out_d[tok0 + j * 128:tok0 + (j + 1) * 128, :], in_=ob[:])])(ob, j), reads=[ob.b], writes=[Buf("outd")])
            S.barrier()
            pf.close()
        pb.close()
        return finish(nc, S, dbg_out)
    return nc


def finish(nc, S, dbg_out):
    S.barrier()
    with nc.Block() as block:
        @block.tensor
        def _(e):
            e.nop()

        @block.scalar
        def _(e):
            e.nop()

        @block.vector
        def _(e):
            e.nop()

        @block.gpsimd
        def _(e):
            e.nop()

        @block.sync
        def _(e):
            e.nop()
    nc._sched_counts = (dict(S.cnt), dict(S.dcnt))
    return nc


def _consts():
    k = np.arange(128)[:, None]
    l = np.arange(128)[None, :]
    ident = np.eye(128, dtype=np.float32)
    uinc = (k <= l).astype(np.float32)
    nustr = -(k < l).astype(np.float32)
    negf = np.where(l < k, -30000.0, 0.0).astype(np.float32)
    negb = np.where(l > k, -30000.0, 0.0).astype(np.float32)
    return dict(cident=ident, cuinc=uinc, cnustr=nustr, cnegf=np.tile(negf, (1, 4)), cnegb=np.tile(negb, (1, 4)))


def prep_inputs(inputs):
    f = lambda a: np.ascontiguousarray(np.asarray(a, dtype=np.float32))
    x = f(inputs["x"])
    meta = f(inputs["meta_tokens"])
    pt = lambda v: np.ascontiguousarray(f(v).reshape(-1, 128).T)
    shared = dict(
        w_in=f(inputs["w_in"][0]), conv_out_w=f(inputs["conv_out_w"][0]), ssm_out_w=f(inputs["ssm_out_w"][0]),
        w_o=f(inputs["w_o"][0]), w_gate=f(inputs["w_gate"][0]), w_up=f(inputs["w_up"][0]), w_down=f(inputs["w_down"][0]),
        cdw=np.ascontiguousarray(f(inputs["conv_dw_w"][0]).T.reshape(8, 128, 31).transpose(1, 0, 2)),
        scw=np.ascontiguousarray(f(inputs["ssm_conv_w"][0]).T.reshape(24, 128, 7).transpose(1, 0, 2)),
        vec8=np.ascontiguousarray(np.stack([pt(inputs["conv_dw_b"][0]), pt(inputs["conv_ln_g"][0]), pt(inputs["conv_ln_b"][0]),
                                            pt(inputs["norm_mix"][0]), pt(inputs["norm_ffn"][0])], axis=1)),
        scb=pt(inputs["ssm_conv_b"][0]),
        rows=np.concatenate([f(inputs["dt_bias_f"][0]), f(inputs["dt_bias_b"][0]), f(inputs["a_log_f"][0]), f(inputs["a_log_b"][0])])[None, :].copy(),
        dsk=f(inputs["ssm_d"][0])[None, :].copy(),
        normw=f(inputs["ssm_norm_w"][0])[None, :].copy(),
        nfin=f(inputs["norm_final"])[None, :].copy(),
        hmask=np.concatenate([np.zeros(112, np.float32), np.ones(16, np.float32)])[:, None].copy(),
    )
    shared.update(_consts())
    in_maps = []
    for core in range(8):
        b, k = core // 4, core % 4
        xs = np.zeros((SLAB, D), np.float32)
        xs[128:2176] = x[b, k * SEG:(k + 1) * SEG]
        if k == 0:
            xs[112:128] = meta
        else:
            xs[0:128] = x[b, k * SEG - 128:k * SEG]
        if k < 3:
            xs[2176:2304] = x[b, (k + 1) * SEG:(k + 1) * SEG + 128]
        xo = np.zeros(((NOT + 1) * 128, D), np.float32)
        xo[112:128] = meta
        hal = xo[NOT * 128:]
        hal[3:6] = x[b, 0:3]
        mrow = np.zeros((1, 2 * NOT), np.float32)
        mrow[0, 0] = 1.0
        others = [j for j in range(4) if j != k]
        for bi, j in enumerate(others):
            for q in range(4):
                blk = 1 + bi * 4 + q
                t0 = 1 + 4 * (blk - 1)
                st = j * SEG + q * 512
                xo[t0 * 128:(t0 + 4) * 128] = x[b, st:st + 512]
                hal[blk * 6:blk * 6 + 3] = meta[13:16] if st == 0 else x[b, st - 3:st]
                if st + 512 < L:
                    hal[blk * 6 + 3:blk * 6 + 6] = x[b, st + 512:st + 515]
                mrow[0, t0:t0 + 4] = 1.0 if j < k else 0.0
                mrow[0, NOT + t0:NOT + t0 + 4] = 1.0 if j > k else 0.0
        m = dict(shared)
        m.update(xs=xs, xo=xo, mrow=mrow)
        in_maps.append(m)
    return in_maps


def kernel(**inputs):
    in_maps = prep_inputs(inputs)
    nc = build_nc()
    res = run_bass_kernel_spmd(nc, in_maps, core_ids=list(range(8)))
    out = np.empty((NB, L, D), np.float32)
    for core in range(8):
        b, k = core // 4, core % 4
        out[b, k * SEG:(k + 1) * SEG] = res.results[core]["out"]
    return out
```
